# Optimizing a Trainium2 kernel written in Bass

```python
import math
import jax, jax.numpy as jnp
from jax import lax
import numpy as np

D_MODEL = 4096
BATCH = 16
SEQ = 256
DEPTH = 2
DEC_BATCH = 8
DEC_SEQ = 2048
PAST_LEN = 256

GRID_W = 64
CHUNK = 32
CONV_K = 3
ROPE_BASE = 10000.0
N_AB = (DEPTH + 1) // 2
N_CD = DEPTH // 2
MIX_W = D_MODEL
H_A = 8
V_A = (MIX_W // 2) // H_A
K_A = V_A // 2
GLA_RANK = 16
GLA_GATE_NORM = 16.0
H_B = 16
K_B = (MIX_W // 2) // H_B
V_B = K_B
H_C = 8
V_C = (MIX_W // 2) // H_C
K_C = V_C // 2
P_D = 64
H_D = (MIX_W // 2) // P_D
N_D = 128
G_D = 4
WA_QK = H_A * K_A
WA_V = H_A * V_A
WB = H_B * K_B
WC_QK = H_C * K_C
WC_V = H_C * V_C
WD = H_D * P_D
WD_BC = G_D * N_D
AB_SIZES = (WA_QK, WA_QK, WA_V, WA_V, GLA_RANK, GLA_RANK, 3 * WB, WB, H_B, H_B, H_B, H_B)
CD_SIZES = (WC_QK, WC_QK, WC_V, WC_V, WD + 2 * WD_BC, WD, H_D, H_D)
IN_AB = sum(AB_SIZES)
IN_CD = sum(CD_SIZES)
OUT_AB = WA_V + H_B * V_B
OUT_CD = WC_V + WD

kernel_name = "bidir_gla_gdn_retnet_ssd_diffusion_step"


def split_points(sizes):
    return [int(v) for v in np.cumsum(sizes)[:-1]]


def rev(a):
    return jnp.flip(a, axis=1)


def rms_norm(x, g, eps=1e-6):
    xf = x.astype(jnp.float32)
    y = xf * lax.rsqrt(jnp.mean(xf * xf, axis=-1, keepdims=True) + eps)
    return y.astype(x.dtype) * g


def layer_norm(x, g, b, eps=1e-5):
    xf = x.astype(jnp.float32)
    mu = jnp.mean(xf, axis=-1, keepdims=True)
    var = jnp.mean(jnp.square(xf - mu), axis=-1, keepdims=True)
    return ((xf - mu) * lax.rsqrt(var + eps)).astype(x.dtype) * g + b


def l2_normalize(x, eps=1e-6):
    return x * lax.rsqrt(jnp.sum(x * x, axis=-1, keepdims=True) + eps)


def dwconv_centred(x, w, b=None):
    pad = (CONV_K - 1) // 2
    y = lax.conv_general_dilated(x, w[:, None, :].astype(x.dtype), (1,), [(pad, pad)],
                                 dimension_numbers=("NWC", "WIO", "NWC"),
                                 feature_group_count=x.shape[-1])
    return y if b is None else y + b


def axial_rotary(x):
    t, k = x.shape[1], x.shape[-1]
    n_rows = t // GRID_W
    rows = jnp.repeat(jnp.arange(n_rows), GRID_W).astype(jnp.float32)
    cols = jnp.tile(jnp.arange(GRID_W), n_rows).astype(jnp.float32)
    n_freq = k // 4
    inv_freq = ROPE_BASE ** (-jnp.arange(n_freq, dtype=jnp.float32) / n_freq)
    ang = jnp.concatenate([rows[:, None] * inv_freq, cols[:, None] * inv_freq], axis=-1)
    cos, sin = jnp.cos(ang)[None, :, None], jnp.sin(ang)[None, :, None]
    xr = x.reshape(*x.shape[:-1], k // 2, 2)
    x1, x2 = xr[..., 0], xr[..., 1]
    return jnp.stack([x1 * cos - x2 * sin, x1 * sin + x2 * cos], axis=-1).reshape(x.shape)


def to_chunks(x):
    b, t = x.shape[:2]
    return jnp.swapaxes(x.reshape(b, t // CHUNK, CHUNK, *x.shape[2:]), 0, 1)


def from_chunks(y):
    n, b = y.shape[:2]
    return jnp.swapaxes(y, 0, 1).reshape(b, n * CHUNK, *y.shape[3:])


def scalar_decay_scan(q, k, v, log_a, s0):
    mask = jnp.tril(jnp.ones((CHUNK, CHUNK), dtype=bool))[None, :, :, None]

    def step(s, xs):
        qc, kc, vc, lc = xs
        g = jnp.cumsum(lc, axis=1)
        diff = g[:, :, None, :] - g[:, None, :, :]
        decay = jnp.where(mask, jnp.exp(jnp.where(mask, diff, 0.0)), 0.0)
        scores = jnp.einsum("bihk,bjhk->bijh", qc, kc) * decay
        o = (jnp.einsum("bijh,bjhv->bihv", scores, vc)
             + jnp.einsum("bihk,bhkv->bihv", qc * jnp.exp(g)[..., None], s))
        g_last = g[:, -1]
        s = (s * jnp.exp(g_last)[..., None, None]
             + jnp.einsum("bjhk,bjhv->bhkv", kc * jnp.exp(g_last[:, None] - g)[..., None], vc))
        return s, o

    s_fin, o = lax.scan(step, s0.astype(jnp.float32),
                        (to_chunks(q), to_chunks(k), to_chunks(v), to_chunks(log_a)))
    return from_chunks(o), s_fin


def vector_decay_scan(q, k, v, log_a, s0):
    mask = jnp.tril(jnp.ones((CHUNK, CHUNK), dtype=bool))[None, :, :, None, None]

    def step(s, xs):
        qc, kc, vc, lc = xs
        g = jnp.cumsum(lc, axis=1)
        decay = jnp.exp(jnp.where(mask, g[:, :, None] - g[:, None], -jnp.inf))
        scores = jnp.einsum("bijhk,bjhk->bijh", qc[:, :, None] * decay, kc)
        o = (jnp.einsum("bijh,bjhv->bihv", scores, vc)
             + jnp.einsum("bihk,bhkv->bihv", qc * jnp.exp(g), s))
        g_last = g[:, -1]
        s = (s * jnp.exp(g_last)[..., None]
             + jnp.einsum("bjhk,bjhv->bhkv", kc * jnp.exp(g_last[:, None] - g), vc))
        return s, o

    s_fin, o = lax.scan(step, s0.astype(jnp.float32),
                        (to_chunks(q), to_chunks(k), to_chunks(v), to_chunks(log_a)))
    return from_chunks(o), s_fin


def delta_scan(q, k, v, beta, log_a, s0):
    tri = jnp.tril(jnp.ones((CHUNK, CHUNK), dtype=bool))
    stri = jnp.tril(jnp.ones((CHUNK, CHUNK), dtype=bool), -1)
    eye = jnp.eye(CHUNK, dtype=jnp.float32)

    def step(s, xs):
        qc, kc, vc, bc, lc = xs
        g = jnp.cumsum(lc, axis=1)
        gh = jnp.swapaxes(g, 1, 2)
        diff = gh[..., :, None] - gh[..., None, :]
        decay = jnp.where(tri, jnp.exp(jnp.where(tri, diff, 0.0)), 0.0)
        kbeta = kc * bc[..., None]
        lower = jnp.where(stri, jnp.einsum("bihk,bjhk->bhij", kbeta, kc) * decay, 0.0)
        t_inv = lax.linalg.triangular_solve(eye + lower, jnp.broadcast_to(eye, lower.shape),
                                            left_side=True, lower=True, unit_diagonal=True)
        u = jnp.einsum("bhij,bjhv->bihv", t_inv, vc * bc[..., None])
        w = jnp.einsum("bhij,bjhk->bihk", t_inv, kbeta * jnp.exp(g)[..., None])
        v_new = u - jnp.einsum("bihk,bhkv->bihv", w, s)
        attn = jnp.einsum("bihk,bjhk->bhij", qc, kc) * decay
        o = (jnp.einsum("bihk,bhkv->bihv", qc * jnp.exp(g)[..., None], s)
             + jnp.einsum("bhij,bjhv->bihv", attn, v_new))
        g_last = g[:, -1]
        s = (s * jnp.exp(g_last)[..., None, None]
             + jnp.einsum("bjhk,bjhv->bhkv", kc * jnp.exp(g_last[:, None] - g)[..., None], v_new))
        return s, o

    s_fin, o = lax.scan(step, s0.astype(jnp.float32),
                        (to_chunks(q), to_chunks(k), to_chunks(v), to_chunks(beta), to_chunks(log_a)))
    return from_chunks(o), s_fin


def mixer_ab(h, w_in, w_out, gla_w2, gla_b, gla_ng, gdn_conv, gdn_a_log, gdn_dt_bias, gdn_ng,
             st_gla, st_gdn):
    f32 = jnp.float32
    bsz, t, _ = h.shape
    (qa, ka, va, za, ra_f, ra_b, qkv_b, zb, beta_fw, beta_bw, a_fw, a_bw) = jnp.split(
        h @ w_in, split_points(AB_SIZES), axis=-1)
    qa = qa.astype(f32).reshape(bsz, t, H_A, K_A) * K_A ** -0.5
    ka = ka.astype(f32).reshape(bsz, t, H_A, K_A)
    va = va.astype(f32).reshape(bsz, t, H_A, V_A)

    def gla_log_decay(r, d):
        logit = (r @ gla_w2[d] + gla_b[d]).astype(f32)
        return (jax.nn.log_sigmoid(logit) / GLA_GATE_NORM).reshape(bsz, t, H_A, K_A)

    oa_f, sa_f = vector_decay_scan(qa, ka, va, gla_log_decay(ra_f, 0), st_gla[0])
    oa_b, sa_b = vector_decay_scan(rev(qa), rev(ka), rev(va), rev(gla_log_decay(ra_b, 1)), st_gla[1])
    ya = (rms_norm(oa_f + rev(oa_b), gla_ng.astype(f32)).reshape(bsz, t, WA_V)
          * jax.nn.silu(za.astype(f32)))
    qkv_b = jax.nn.silu(dwconv_centred(qkv_b, gdn_conv)).astype(f32)
    qb, kb, vb = jnp.split(qkv_b, 3, axis=-1)
    qb = l2_normalize(qb.reshape(bsz, t, H_B, K_B)) * K_B ** -0.5
    kb = l2_normalize(kb.reshape(bsz, t, H_B, K_B))
    vb = vb.reshape(bsz, t, H_B, V_B)

    def gdn_gates(b_raw, a_raw, d):
        beta = jax.nn.sigmoid(b_raw.astype(f32))
        log_a = -jnp.exp(gdn_a_log[d].astype(f32)) * jax.nn.softplus(
            a_raw.astype(f32) + gdn_dt_bias[d].astype(f32))
        return beta, log_a

    bt_f, la_f = gdn_gates(beta_fw, a_fw, 0)
    bt_b, la_b = gdn_gates(beta_bw, a_bw, 1)
    ob_f, sb_f = delta_scan(qb, kb, vb, bt_f, la_f, st_gdn[0])
    ob_b, sb_b = delta_scan(rev(qb), rev(kb), rev(vb), rev(bt_b), rev(la_b), st_gdn[1])
    yb = (rms_norm(ob_f + rev(ob_b), gdn_ng.astype(f32)).reshape(bsz, t, H_B * V_B)
          * jax.nn.silu(zb.astype(f32)))
    out = jnp.concatenate([ya, yb], axis=-1).astype(h.dtype) @ w_out
    return out, ((sa_f, sa_b), (sb_f, sb_b))


def mixer_cd(h, w_in, w_out, ret_ng, ret_nb, ssd_conv_w, ssd_conv_b, ssd_a_log, ssd_dt_bias, ssd_d,
             ssd_ng, st_ret, st_ssd, on_grid):
    f32 = jnp.float32
    bsz, t, _ = h.shape
    qc, kc, vc, zc, xbc, zd, dt_fw, dt_bw = jnp.split(h @ w_in, split_points(CD_SIZES), axis=-1)
    qc = qc.astype(f32).reshape(bsz, t, H_C, K_C)
    kc = kc.astype(f32).reshape(bsz, t, H_C, K_C) * K_C ** -0.5
    vc = vc.astype(f32).reshape(bsz, t, H_C, V_C)
    if on_grid:
        qc, kc = axial_rotary(qc), axial_rotary(kc)
    log_gamma = jnp.log1p(-jnp.exp2(-5.0 - jnp.arange(H_C, dtype=f32)))
    lg_f = jnp.broadcast_to(log_gamma, (bsz, t, H_C))
    lg_b = jnp.broadcast_to(log_gamma[::-1], (bsz, t, H_C))
    oc_f, sc_f = scalar_decay_scan(qc, kc, vc, lg_f, st_ret[0])
    oc_b, sc_b = scalar_decay_scan(rev(qc), rev(kc), rev(vc), lg_b, st_ret[1])
    yc = (layer_norm(oc_f + rev(oc_b), ret_ng.astype(f32), ret_nb.astype(f32)).reshape(bsz, t, WC_V)
          * jax.nn.silu(zc.astype(f32)))
    xbc = jax.nn.silu(dwconv_centred(xbc, ssd_conv_w, ssd_conv_b)).astype(f32)
    xd, bd, cm = jnp.split(xbc, [WD, WD + WD_BC], axis=-1)
    xd = xd.reshape(bsz, t, H_D, P_D)
    bd = jnp.repeat(bd.reshape(bsz, t, G_D, N_D), H_D // G_D, axis=2)
    cm = jnp.repeat(cm.reshape(bsz, t, G_D, N_D), H_D // G_D, axis=2)

    def ssd_dt(raw, d):
        dt = jax.nn.softplus(raw.astype(f32) + ssd_dt_bias[d].astype(f32))
        return dt, -jnp.exp(ssd_a_log[d].astype(f32)) * dt

    dt_f, la_f = ssd_dt(dt_fw, 0)
    dt_b, la_b = ssd_dt(dt_bw, 1)
    od_f, sd_f = scalar_decay_scan(cm, bd, xd * dt_f[..., None], la_f, st_ssd[0])
    od_b, sd_b = scalar_decay_scan(rev(cm), rev(bd), rev(xd * dt_b[..., None]), rev(la_b), st_ssd[1])
    yd = ((od_f + rev(od_b) + ssd_d.astype(f32)[:, None] * xd).reshape(bsz, t, WD)
          * jax.nn.silu(zd.astype(f32)))
    yd = rms_norm(yd.reshape(bsz, t, G_D, WD // G_D),
                  ssd_ng.astype(f32).reshape(G_D, WD // G_D)).reshape(bsz, t, WD)
    out = jnp.concatenate([yc, yd], axis=-1).astype(h.dtype) @ w_out
    return out, ((sc_f, sc_b), (sd_f, sd_b))


def zero_states(l, bsz):
    if l % 2 == 0:
        shapes = ((H_A, K_A, V_A), (H_B, K_B, V_B))
    else:
        shapes = ((H_C, K_C, V_C), (H_D, N_D, P_D))
    return tuple((jnp.zeros((bsz,) + s, jnp.float32), jnp.zeros((bsz,) + s, jnp.float32)) for s in shapes)


def run_trunk(x, cvec, init_states, on_grid, ada_w, ada_b, norm_g, final_norm_g, ab, cd):
    final_states = []
    for l in range(DEPTH):
        mod = (jax.nn.silu(cvec) @ ada_w[l] + ada_b[l])[:, None, :]
        shift, scale, gate = jnp.split(mod, 3, axis=-1)
        h = rms_norm(x, norm_g[l]) * (1.0 + scale) + shift
        if l % 2 == 0:
            out, st = mixer_ab(h, *[p[l // 2] for p in ab], *init_states[l])
        else:
            out, st = mixer_cd(h, *[p[l // 2] for p in cd], *init_states[l], on_grid)
        x = x + gate * out
        final_states.append(st)
    return rms_norm(x, final_norm_g), final_states


def setup_inputs(seed: int = 0) -> dict:
    key = jax.random.key(seed)
    ks = iter(jax.random.split(key, 48))
    f32 = jnp.float32

    def nrm(shape, s):
        return jax.random.normal(next(ks), shape, f32) * s

    def gain(shape):
        return 1.0 + nrm(shape, 0.02)

    def a_log(shape):
        return jnp.log(jax.random.uniform(next(ks), shape, f32, 1.0, 16.0))

    def dt_bias(shape):
        dt = jnp.exp(jax.random.uniform(next(ks), shape, f32, math.log(1e-3), math.log(1e-1)))
        return dt + jnp.log(-jnp.expm1(-dt))

    return {
        "x_prompt": nrm((BATCH, SEQ, D_MODEL), 1.0),
        "x_sample": nrm((DEC_BATCH, DEC_SEQ, D_MODEL), 1.0),
        "state_gla_fwd": nrm((DEC_BATCH, N_AB, H_A, K_A, V_A), K_A ** -0.5),
        "state_gla_bwd": nrm((DEC_BATCH, N_AB, H_A, K_A, V_A), K_A ** -0.5),
        "state_gdn_fwd": nrm((DEC_BATCH, N_AB, H_B, K_B, V_B), K_B ** -0.5),
        "state_gdn_bwd": nrm((DEC_BATCH, N_AB, H_B, K_B, V_B), K_B ** -0.5),
        "state_ret_fwd": nrm((DEC_BATCH, N_CD, H_C, K_C, V_C), K_C ** -0.5),
        "state_ret_bwd": nrm((DEC_BATCH, N_CD, H_C, K_C, V_C), K_C ** -0.5),
        "state_ssd_fwd": nrm((DEC_BATCH, N_CD, H_D, N_D, P_D), N_D ** -0.5),
        "state_ssd_bwd": nrm((DEC_BATCH, N_CD, H_D, N_D, P_D), N_D ** -0.5),
        "c": nrm((DEC_BATCH, D_MODEL), 1.0),
        "c_ctx": nrm((D_MODEL,), 1.0),
        "ada_w": nrm((DEPTH, D_MODEL, 3 * D_MODEL), 0.5 * D_MODEL ** -0.5),
        "ada_b": nrm((DEPTH, 3 * D_MODEL), 0.02),
        "norm_g": gain((DEPTH, D_MODEL)),
        "final_norm_g": gain((D_MODEL,)),
        "ab_w_in": nrm((N_AB, D_MODEL, IN_AB), D_MODEL ** -0.5),
        "ab_w_out": nrm((N_AB, OUT_AB, D_MODEL), OUT_AB ** -0.5),
        "gla_gate_w2": nrm((N_AB, 2, GLA_RANK, WA_QK), GLA_RANK ** -0.5),
        "gla_gate_b": 1.0 + nrm((N_AB, 2, WA_QK), 0.5),
        "gla_norm_g": gain((N_AB, V_A)),
        "gdn_conv_w": nrm((N_AB, CONV_K, 3 * WB), CONV_K ** -0.5),
        "gdn_a_log": a_log((N_AB, 2, H_B)),
        "gdn_dt_bias": dt_bias((N_AB, 2, H_B)),
        "gdn_norm_g": gain((N_AB, V_B)),
        "cd_w_in": nrm((N_CD, D_MODEL, IN_CD), D_MODEL ** -0.5),
        "cd_w_out": nrm((N_CD, OUT_CD, D_MODEL), OUT_CD ** -0.5),
        "ret_norm_g": gain((N_CD, V_C)),
        "ret_norm_b": nrm((N_CD, V_C), 0.02),
        "ssd_conv_w": nrm((N_CD, CONV_K, WD + 2 * WD_BC), CONV_K ** -0.5),
        "ssd_conv_b": nrm((N_CD, WD + 2 * WD_BC), 0.02),
        "ssd_a_log": a_log((N_CD, 2, H_D)),
        "ssd_dt_bias": dt_bias((N_CD, 2, H_D)),
        "ssd_d": gain((N_CD, H_D)),
        "ssd_norm_g": gain((N_CD, WD)),
    }


def reference(x_prompt, x_sample, state_gla_fwd, state_gla_bwd, state_gdn_fwd, state_gdn_bwd,
              state_ret_fwd, state_ret_bwd, state_ssd_fwd, state_ssd_bwd, c, c_ctx,
              ada_w, ada_b, norm_g, final_norm_g,
              ab_w_in, ab_w_out, gla_gate_w2, gla_gate_b, gla_norm_g,
              gdn_conv_w, gdn_a_log, gdn_dt_bias, gdn_norm_g,
              cd_w_in, cd_w_out, ret_norm_g, ret_norm_b,
              ssd_conv_w, ssd_conv_b, ssd_a_log, ssd_dt_bias, ssd_d, ssd_norm_g):
    ab = (ab_w_in, ab_w_out, gla_gate_w2, gla_gate_b, gla_norm_g,
          gdn_conv_w, gdn_a_log, gdn_dt_bias, gdn_norm_g)
    cd = (cd_w_in, cd_w_out, ret_norm_g, ret_norm_b,
          ssd_conv_w, ssd_conv_b, ssd_a_log, ssd_dt_bias, ssd_d, ssd_norm_g)

    ctx_init = [zero_states(l, x_prompt.shape[0]) for l in range(DEPTH)]
    y_prompt, ctx_states = run_trunk(x_prompt, c_ctx[None, :], ctx_init, False,
                                     ada_w, ada_b, norm_g, final_norm_g, ab, cd)

    lat_init = []
    for l in range(DEPTH):
        i = l // 2
        if l % 2 == 0:
            lat_init.append(((state_gla_fwd[:, i], state_gla_bwd[:, i]),
                             (state_gdn_fwd[:, i], state_gdn_bwd[:, i])))
        else:
            lat_init.append(((state_ret_fwd[:, i], state_ret_bwd[:, i]),
                             (state_ssd_fwd[:, i], state_ssd_bwd[:, i])))
    y_sample, _ = run_trunk(x_sample, c, lat_init, True,
                            ada_w, ada_b, norm_g, final_norm_g, ab, cd)

    ab_st = ctx_states[0::2]
    cd_st = ctx_states[1::2]

    def stack(states, m, d):
        return jnp.stack([s[m][d] for s in states], axis=1).astype(x_prompt.dtype)

    new_gla_fwd = stack(ab_st, 0, 0)
    new_gla_bwd = stack(ab_st, 0, 1)
    new_gdn_fwd = stack(ab_st, 1, 0)
    new_gdn_bwd = stack(ab_st, 1, 1)
    new_ret_fwd = stack(cd_st, 0, 0)
    new_ret_bwd = stack(cd_st, 0, 1)
    new_ssd_fwd = stack(cd_st, 1, 0)
    new_ssd_bwd = stack(cd_st, 1, 1)
    return (y_prompt, y_sample, new_gla_fwd, new_gla_bwd, new_gdn_fwd, new_gdn_bwd,
            new_ret_fwd, new_ret_bwd, new_ssd_fwd, new_ssd_bwd)
```

```python
import math
import os
import numpy as np
CUT = int(os.environ.get('GDN_CUT', 9))
NLV = int(os.environ.get('GDN_NLV', 8))
SUB = int(os.environ.get('GDN_SUB', 9))
TRM = int(os.environ.get('GDN_TRM', 0))
NHD = int(os.environ.get('GDN_NHD', 99))
PAR = int(os.environ.get('GDN_PAR', 2))
from contextlib import ExitStack
import concourse.bass as bass
import concourse.mybir as mybir
from concourse.bass_utils import run_bass_kernel_spmd

F32 = mybir.dt.float32
F32R = mybir.dt.float32r
BF16 = mybir.dt.bfloat16
AF = mybir.ActivationFunctionType
ALU = mybir.AluOpType
AX = mybir.AxisListType

CONV_ENG = os.environ.get('CONV_ENG', 'pool')
FULL_CFG = dict(D=4096, H_A=8, H_B=16, H_C=8, H_D=32, G_D=4, Ts=2048, Tp=256, NP=2, GRID_W=64)


def derive(cfg):
    c = dict(cfg)
    D = c["D"]
    c["KC"] = D // 128
    c["WA_QK"] = c["H_A"] * 128
    c["WA_V"] = c["H_A"] * 256
    c["WB"] = c["H_B"] * 128
    c["WC_QK"] = c["H_C"] * 128
    c["WC_V"] = c["H_C"] * 256
    c["WD"] = c["H_D"] * 64
    c["WD_BC"] = c["G_D"] * 128
    ab = (c["WA_QK"], c["WA_QK"], c["WA_V"], c["WA_V"], 16, 16, 3 * c["WB"], c["WB"],
          c["H_B"], c["H_B"], c["H_B"], c["H_B"])
    cd = (c["WC_QK"], c["WC_QK"], c["WC_V"], c["WC_V"], c["WD"] + 2 * c["WD_BC"], c["WD"], c["H_D"], c["H_D"])
    c["AB_OFF"] = [0] + [int(v) for v in np.cumsum(ab)]
    c["CD_OFF"] = [0] + [int(v) for v in np.cumsum(cd)]
    c["IN_AB"] = c["AB_OFF"][-1]
    c["IN_CD"] = c["CD_OFF"][-1]
    c["NTOK"] = c["Ts"] + c["NP"] * c["Tp"]
    seqs = [(0, 1, c["Ts"], True, 0)]
    tok = c["Ts"]
    row = c["Ts"] + 2
    for i in range(c["NP"]):
        seqs.append((tok, row + 1, c["Tp"], False, i))
        tok += c["Tp"]
        row += c["Tp"] + 2
    c["SEQS"] = seqs
    c["NROW"] = row
    tiles = []
    for si, (t0, r0, T, smp, idx) in enumerate(seqs):
        for j in range(T // 128):
            tiles.append((si, j, t0 + j * 128, r0 + j * 128))
    c["TILES"] = tiles
    return c


class Buf:
    __slots__ = ("name", "last_w", "readers")

    def __init__(self, prog, name):
        self.name = name
        self.last_w = prog.last_barrier
        self.readers = []
        prog.bufs.append(self)


class Op:
    __slots__ = ("eng", "fn", "deps", "needs_inc", "ticket", "dma", "dsem", "dticket", "dn", "idx")


class Prog:
    ENGS = ("pe", "act", "dve", "pool", "sp")
    K = 8
    EPOCH = 30000

    def __init__(self, nc):
        self.nc = nc
        self.ops = {e: [] for e in self.ENGS}
        self.nops = 0
        self.last_barrier = None
        self.bufs = []
        self.bar_tile = None
        self.last_compute = {}
        self.coarse = bool(int(os.environ.get("COARSE", "0")))

    def buf(self, name="b"):
        return Buf(self, name)

    def barrier(self):
        t = self.bar_tile
        bl = list(self.bufs)
        op = self.add("dve", lambda e: e.memset(t[:], 0.0), reads=bl, writes=bl)
        self.last_barrier = op
        self.bufs = []
        return op

    def add(self, eng, fn, reads=(), writes=(), dma=False, tag=None):
        op = Op()

        op.eng = eng
        op.fn = fn
        op.dma = dma
        op.needs_inc = False
        op.ticket = None
        op.idx = self.nops
        self.nops += 1
        deps = {}
        for b in reads:
            w = b.last_w
            if w is not None:
                deps[id(w)] = (w, True)
        for b in writes:
            w = b.last_w
            if w is not None:
                deps[id(w)] = (w, True)
            for r in b.readers:
                if id(r) not in deps:
                    deps[id(r)] = (r, False)
        final = []
        for (d, strong) in deps.values():
            if d is op:
                continue
            if d.eng == eng and not d.dma and not dma:
                if eng == "pe":
                    continue
                if not strong:
                    continue
            if not d.dma and not dma and self.coarse:
                d = self.last_compute[d.eng]
            final.append(d)
            if not d.dma:
                d.needs_inc = True
        op.deps = list({id(x): x for x in final}.values())
        for b in reads:
            b.readers.append(op)
        for b in writes:
            b.last_w = op
            b.readers = []
        self.ops[eng].append(op)
        if not dma:
            self.last_compute[eng] = op
        return op

    def emit(self):
        nc = self.nc
        K, EPOCH = self.K, self.EPOCH
        nsem_c = {}
        ndma = {}
        for eng in self.ENGS:
            cnt = 0
            dcnt = 0
            for op in self.ops[eng]:
                if op.dma:
                    op.dn = dcnt
                    op.dsem = dcnt % K
                    op.dticket = 16 * (dcnt // K + 1)
                    dcnt += 1
                elif op.needs_inc:
                    op.ticket = (cnt // EPOCH, cnt % EPOCH + 1)
                    cnt += 1
            nsem_c[eng] = (cnt + EPOCH - 1) // EPOCH
            ndma[eng] = dcnt
        with ExitStack() as es:
            csem = {}
            dsem = {}
            for eng in self.ENGS:
                csem[eng] = [es.enter_context(nc.semaphore(f"c_{eng}_{i}")) for i in range(nsem_c[eng])]
                dsem[eng] = [es.enter_context(nc.semaphore(f"d_{eng}_{i}")) for i in range(min(K, ndma[eng]))]
            block = es.enter_context(nc.Block())

            self.sim = {en: [] for en in self.ENGS}

            def stream(eng, e):
                waited = {}
                maxep = {}
                cur = []

                def wait(sem, key, val):
                    if waited.get(key, 0) < val:
                        e.wait_ge(sem, val)
                        waited[key] = val
                        cur.append((key, val))

                for op in self.ops[eng]:
                    for d in op.deps:
                        if d.dma:
                            wait(dsem[d.eng][d.dsem], ("d", d.eng, d.dsem), d.dticket)
                        else:
                            ep, val = d.ticket
                            if maxep.get(d.eng, -1) > ep:
                                continue
                            maxep[d.eng] = ep
                            wait(csem[d.eng][ep], ("c", d.eng, ep), val)
                    if op.dma:
                        if op.dn >= K:
                            wait(dsem[eng][op.dsem], ("d", eng, op.dsem), op.dticket - 16)
                        ins = op.fn(e)
                        ins.then_inc(dsem[eng][op.dsem], 16)
                        self.sim[eng].append((list(cur), (("d", eng, op.dsem), 16), op.idx))
                    else:
                        ins = op.fn(e)
                        if op.needs_inc:
                            ins.then_inc(csem[eng][op.ticket[0]], 1)
                            self.sim[eng].append((list(cur), (("c", eng, op.ticket[0]), 1), op.idx))
                        else:
                            self.sim[eng].append((list(cur), None, op.idx))
                    del cur[:]
                n = ndma[eng]
                for s in range(min(K, n)):
                    last = ((n - 1 - s) // K) * K + s
                    wait(dsem[eng][s], ("d", eng, s), 16 * (last // K + 1))

            self.streams_done = False
            if self.ops["pe"]:
                @block.tensor
                def _(e):
                    stream("pe", e)
            if self.ops["act"]:
                @block.scalar
                def _(e):
                    stream("act", e)
            if self.ops["dve"]:
                @block.vector
                def _(e):
                    stream("dve", e)
            if self.ops["pool"]:
                @block.gpsimd
                def _(e):
                    stream("pool", e)
            if self.ops["sp"]:
                @block.sync
                def _(e):
                    stream("sp", e)


class K:
    def __init__(self, cfg, debug=False):
        self.c = derive(cfg)
        self.debug = debug
        self.nc = bass.Bass("TRN2", target_bir_lowering=False)
        self.P = Prog(self.nc)
        self.ins = {}
        self.outs = {}
        self.psi = 0

    def din(self, name, shape):
        self.ins[name] = self.nc.dram_tensor(name, list(shape), F32, kind="ExternalInput").ap()
        return self.ins[name]

    def dout(self, name, shape):
        self.outs[name] = self.nc.dram_tensor(name, list(shape), F32, kind="ExternalOutput").ap()
        return self.outs[name]

    def dscr(self, name, shape, dt=F32):
        return self.nc.dram_tensor(name, list(shape), dt, kind="Internal").ap()

    def sb(self, es, name, shape, dt=F32):
        self.uid = getattr(self, "uid", 0) + 1
        name = f"s{self.uid}_{name}"
        t = es.enter_context(self.nc.sbuf_tensor(name, list(shape), dt))
        return t, self.P.buf(name)

    def ps(self):
        i = self.psi % 8
        self.psi += 1
        return self.pst[i], self.psb[i]

    def dma(self, out, in_, reads=(), writes=(), q="sp"):
        return self.P.add(q, lambda e: e.dma_start(out=out, in_=in_), reads=reads, writes=writes, dma=True)

    def mm(self, out, lhsT, rhs, start, stop, reads, writes):
        return self.P.add("pe", lambda e: e.matmul(out, lhsT, rhs, start=start, stop=stop), reads=reads, writes=writes)

    def tr(self, out, in_, reads, writes):
        idn = self.ident
        return self.P.add("pe", lambda e: e.transpose(out, in_, idn[:]), reads=list(reads) + [self.identb], writes=writes)

    def act(self, out, in_, func, reads, writes, scale=None, bias=None, accum_out=None, eng="act"):
        kw = {}
        if scale is not None:
            kw["scale"] = scale
        if bias is not None:
            kw["bias"] = bias
        if accum_out is not None:
            kw["accum_out"] = accum_out
        return self.P.add("act", lambda e: e.activation(out=out, in_=in_, func=func, **kw), reads=reads, writes=writes)

    def tt(self, out, in0, in1, op, reads, writes, eng="dve"):
        return self.P.add(eng, lambda e: e.tensor_tensor(out=out, in0=in0, in1=in1, op=op), reads=reads, writes=writes)

    def ts(self, out, in0, s1, op0, reads, writes, s2=None, op1=None, eng="dve"):
        if op1 is None:
            return self.P.add(eng, lambda e: e.tensor_scalar(out=out, in0=in0, scalar1=s1, scalar2=None, op0=op0),
                              reads=reads, writes=writes)
        return self.P.add(eng, lambda e: e.tensor_scalar(out=out, in0=in0, scalar1=s1, scalar2=s2, op0=op0, op1=op1),
                          reads=reads, writes=writes)

    def stt(self, out, in0, scalar, in1, op0, op1, reads, writes):
        return self.P.add("dve", lambda e: e.scalar_tensor_tensor(out=out, in0=in0, scalar=scalar, in1=in1, op0=op0, op1=op1),
                          reads=reads, writes=writes)

    def cp(self, out, in_, reads, writes, eng="dve"):
        if eng == "act":
            return self.act(out, in_, AF.Copy, reads, writes)
        return self.P.add(eng, lambda e: e.tensor_copy(out, in_), reads=reads, writes=writes)

    def recip(self, out, in_, reads, writes):
        def f(e):
            with self.nc.allow_low_precision(reason="fp32r-rounded output feeding TensorE fp32r"):
                return e.reciprocal(out, in_)
        return self.P.add("dve", f, reads=reads, writes=writes)

    def red(self, out, in_, reads, writes, op=ALU.add):
        return self.P.add("dve", lambda e: e.tensor_reduce(out=out, in_=in_, axis=AX.X, op=op), reads=reads, writes=writes)

    def rsqrt_chain(self, out, in_, mult, eps, reads, writes):
        self.ts(out, in_, mult, ALU.mult, reads, writes, s2=eps, op1=ALU.add)
        self.act(out, out, AF.Sqrt, writes, writes)
        self.recip(out, out, writes, writes)


def bc(ap, shape):
    return ap.broadcast_to(list(shape))


def declare_io(k):
    c = k.c
    D, KC = c["D"], c["KC"]
    k.din("x", [c["NTOK"], D])
    k.din("cT", [128, KC, 2])
    k.din("ada_w", [2, D, 3 * D])
    k.din("ada_bT", [2, 128, 2 * KC])
    k.din("ada_bg", [2, 2, D])
    k.din("norm_gT", [2, 128, KC])
    k.din("final_norm_g", [D])
    k.din("ab_w_in", [D, c["IN_AB"]])
    k.din("ab_w_out", [D, D])
    k.din("cd_w_in", [D, c["IN_CD"]])
    k.din("cd_w_out", [D, D])
    k.din("ident", [128, 128])
    k.din("sel", [2, 2, 128])
    k.dout("y", [c["NTOK"], D])
    k.hT_scr = k.dscr("hT_scr", [len(c["TILES"]), 128, KC * 128], BF16)
    k.proj = k.dscr("proj", [c["NROW"], max(c["IN_AB"], c["IN_CD"])])
    k.y_scr = k.dscr("y_scr", [c["NTOK"], D])
    k.x1 = k.dscr("x1", [c["NTOK"], D])
    k.x2 = k.dscr("x2", [c["NTOK"], D])
    k.grow_scr = k.dscr("grow_scr", [2, 2, D])
    k.convT = k.dscr("convT", [max(3 * c["WB"], c["WD"] + 2 * c["WD_BC"]), c["NROW"] + 2])
    if k.debug:
        k.dout("dbg_proj", [c["NROW"], max(c["IN_AB"], c["IN_CD"])])
        k.dout("dbg_x1", [c["NTOK"], D])


def phase_mod(k, es):
    c, P = k.c, k.P
    D, KC = c["D"], c["KC"]
    k.gs, k.gsb = k.sb(es, "gs", [128, 2, KC, 2])
    k.sh, k.shb = k.sb(es, "sh", [128, 2, KC, 2])
    with ExitStack() as ph:
        k.grow, k.growb = k.sb(ph, "grow", [2, 2, D])
        cT, cTb = k.sb(ph, "cTt", [128, KC, 2])
        sc, scb = k.sb(ph, "sc", [128, KC, 2])
        bT, bTb = k.sb(ph, "bT", [128, 2, 2 * KC])
        ng, ngb = k.sb(ph, "ngT", [128, 2, KC])
        bg, bgb = k.sb(ph, "bg", [2, 2, D])
        modT, modTb = k.sb(ph, "modT", [128, 2 * KC, 2])
        wp = [k.sb(ph, f"wp{i}", [128, KC, 512]) for i in range(2)]
        k.dma(cT[:], k.ins["cT"][:, :, :], writes=[cTb])
        k.dma(bT[:], k.ins["ada_bT"].rearrange("l p n -> p l n"), writes=[bTb])
        k.dma(ng[:], k.ins["norm_gT"].rearrange("l p n -> p l n"), writes=[ngb])
        k.dma(bg[:], k.ins["ada_bg"].rearrange("l v d -> v l d"), writes=[bgb])
        k.act(sc[:], cT[:], AF.Silu, [cTb], [scb])
        npan = 3 * D // 512
        pi = 0
        for l in range(2):
            pm, pmb = k.ps()
            for pn in range(npan):
                w, wb = wp[pi % 2]
                pi += 1
                src = k.ins["ada_w"][l, :, pn * 512:(pn + 1) * 512].rearrange("(kc p) n -> p kc n", p=128)
                k.dma(w[:], src, writes=[wb])
                if pn < 2 * D // 512:
                    for n4 in range(4):
                        n = pn * 4 + n4
                        for kc in range(KC):
                            k.mm(pm[:, 2 * n:2 * n + 2], w[:, kc, n4 * 128:(n4 + 1) * 128], sc[:, kc, :],
                                 kc == 0, kc == KC - 1, [wb, scb], [pmb])
                if pn == 2 * D // 512 - 1:
                    k.tt(modT[:], pm[:, 0:4 * KC].rearrange("p (n v) -> p n v", v=2),
                         bT[:, l, :].unsqueeze(2).broadcast_to([128, 2 * KC, 2]), ALU.add, [pmb, bTb], [modTb])
                    k.cp(k.sh[:, l], modT[:, 0:KC, :], [modTb], [k.shb])
                    k.stt(k.gs[:, l], modT[:, KC:2 * KC, :], 1.0, ng[:, l, :].unsqueeze(2).broadcast_to([128, KC, 2]),
                          ALU.add, ALU.mult, [modTb, ngb], [k.gsb])
                if pn >= 2 * D // 512:
                    pg, pgb = k.ps()
                    for kc in range(KC):
                        k.mm(pg[0:2, :], sc[:, kc, :], w[:, kc, :], kc == 0, kc == KC - 1, [wb, scb], [pgb])
                    c0 = pn * 512 - 2 * D
                    k.tt(k.grow[:, l, c0:c0 + 512], pg[0:2, :], bg[:, l, c0:c0 + 512], ALU.add, [pgb, bgb], [k.growb])
        k.dma(k.grow_scr.rearrange("l v d -> v l d"), k.grow[:], reads=[k.growb], writes=[k.growsb])
    P.barrier()


def phase_T(k, src, layer, norm):
    c, P = k.c, k.P
    D, KC = c["D"], c["KC"]
    with ExitStack() as ph:
        xa = [k.sb(ph, f"xa{i}", [128, D]) for i in range(2)]
        xn, xnb = k.sb(ph, "xn", [128, D])
        hT = [k.sb(ph, f"hTt{i}", [128, KC, 128], BF16) for i in range(2)]
        ss, ssb = k.sb(ph, "ss", [128, 1])
        for ti, (si, j, t0, r0) in enumerate(c["TILES"]):
            v = 0 if c["SEQS"][si][3] else 1
            x, xb = xa[ti % 2]
            h, hb = hT[ti % 2]
            k.dma(x[:], src[t0:t0 + 128, :], writes=[xb])
            if norm:
                k.act(xn[:], x[:], AF.Square, [xb], [xnb, ssb], accum_out=ss[:, 0:1])
                k.rsqrt_chain(ss[:], ss[:], 1.0 / D, 1e-6, [ssb], [ssb])
                k.act(xn[:], x[:], AF.Copy, [xb, ssb], [xnb], scale=ss[:, 0:1])
                s_, sb_ = xn, xnb
            else:
                s_, sb_ = x, xb
            for q in range(KC // 4):
                pt, ptb = k.ps()
                for r in range(4):
                    kc = q * 4 + r
                    k.tr(pt[:, r * 128:(r + 1) * 128], s_[:, kc * 128:(kc + 1) * 128], [sb_], [ptb])
                if norm:
                    for r in range(4):
                        kc = q * 4 + r
                        if r % 2 == 0:
                            k.act(h[:, kc, :], pt[:, r * 128:(r + 1) * 128], AF.Identity, [ptb, k.gsb, k.shb], [hb],
                                  scale=k.gs[:, layer, kc, v:v + 1], bias=k.sh[:, layer, kc, v:v + 1])
                        else:
                            k.ts(h[:, kc, :], pt[:, r * 128:(r + 1) * 128], k.gs[:, layer, kc, v:v + 1], ALU.mult,
                                 [ptb, k.gsb, k.shb], [hb], s2=k.sh[:, layer, kc, v:v + 1], op1=ALU.add)
                else:
                    if q % 2 == 0:
                        k.cp(h[:, q * 4:q * 4 + 4, :], pt[:].rearrange("p (a b) -> p a b", b=128), [ptb], [hb], eng="act")
                    else:
                        k.cp(h[:, q * 4:q * 4 + 4, :], pt[:].rearrange("p (a b) -> p a b", b=128), [ptb], [hb])
            k.dma(k.hT_scr[ti], h[:].rearrange("p a b -> p (a b)"), reads=[hb], writes=[k.hTsb])
    P.barrier()


def phase_P(k, W, panels, evac):
    c, P = k.c, k.P
    D, KC = c["D"], c["KC"]
    tiles = c["TILES"]
    NG = 10
    with ExitStack() as ph:
        ng = min(NG, len(tiles))
        hTg, _ = k.sb(ph, "hTg", [128, ng, KC * 128], BF16)
        hTb = [P.buf(f"hT{i}") for i in range(ng)]
        wp = [k.sb(ph, f"wpb{i}", [128, KC, 512], BF16) for i in range(2)]
        k.stg = [k.sb(ph, f"stg{i}", [128, 512]) for i in range(4)]
        k.xr = [k.sb(ph, f"xr{i}", [128, 512]) for i in range(2)]
        k.stgi = 0
        pi = 0
        for g0 in range(0, len(tiles), NG):
            grp = list(range(g0, min(g0 + NG, len(tiles))))
            for gi, ti in enumerate(grp):
                k.dma(hTg[:, gi, :], k.hT_scr[ti], reads=[k.hTsb], writes=[hTb[gi]])
            runs = []
            for gi, ti in enumerate(grp):
                si, j = tiles[ti][0], tiles[ti][1]
                if runs and runs[-1][2] == si and runs[-1][1] - runs[-1][0] < 4 and tiles[grp[runs[-1][1] - 1]][1] == j - 1:
                    runs[-1][1] += 1
                else:
                    runs.append([gi, gi + 1, si])
            for pan in panels:
                c0, w = pan[0], pan[1]
                fm = len(pan) > 2
                wt, wb = wp[pi % 2]
                pi += 1
                k.dma(wt[:, :, 0:w], W[:, c0:c0 + w].rearrange("(kc p) n -> p kc n", p=128), writes=[wb], q="pool")
                if not fm:
                    for gi, ti in enumerate(grp):
                        pt, ptb = k.ps()
                        for kc in range(KC):
                            k.mm(pt[:, 0:w], hTg[:, gi, kc * 128:(kc + 1) * 128], wt[:, kc, 0:w], kc == 0, kc == KC - 1,
                                 [hTb[gi], wb], [ptb])
                        evac(ti, tiles[ti], c0, w, pt, ptb)
                else:
                    for (ga, gb_, si) in runs:
                        n = gb_ - ga
                        r0 = tiles[grp[ga]][3]
                        for bk in range(w // 128):
                            pt, ptb = k.ps()
                            for kc in range(KC):
                                k.mm(pt[:, 0:n * 128].rearrange("p (a b) -> p a b", b=128),
                                     wt[:, kc, bk * 128:(bk + 1) * 128], hTg[:, ga:gb_, kc * 128:(kc + 1) * 128],
                                     kc == 0, kc == KC - 1, [hTb[x] for x in range(ga, gb_)] + [wb], [ptb])
                            sg, sgb = k.stg[k.stgi % 4]
                            k.stgi += 1
                            k.cp(sg[:, 0:n * 128], pt[:, 0:n * 128], [ptb], [sgb], eng="act" if k.stgi % 2 else "dve")
                            ch0 = c0 - pan[2] + bk * 128
                            k.dma(k.convT[ch0:ch0 + 128, r0:r0 + n * 128], sg[:, 0:n * 128], reads=[sgb], writes=[k.convTb])
    P.barrier()


def evac_proj(k):
    def f(ti, tinfo, c0, w, pt, ptb):
        si, j, t0, r0 = tinfo
        s, sbuf = k.stg[k.stgi % 4]
        k.stgi += 1
        if k.stgi % 2 == 0:
            k.cp(s[:, 0:w], pt[:, 0:w], [ptb], [sbuf], eng="act")
        else:
            k.cp(s[:, 0:w], pt[:, 0:w], [ptb], [sbuf])
        k.dma(k.proj[r0:r0 + 128, c0:c0 + w], s[:, 0:w], reads=[sbuf], writes=[k.projb])
    return f


def evac_fm(k, tinfo, ch0, w, pt, ptb):
    si, j, t0, r0 = tinfo
    s, sbuf = k.stg[k.stgi % 4]
    k.stgi += 1
    if k.stgi % 2 == 0:
        k.cp(s[:, 0:w], pt[:, 0:w], [ptb], [sbuf], eng="act")
    else:
        k.cp(s[:, 0:w], pt[:, 0:w], [ptb], [sbuf])
    k.dma(k.convT[ch0:ch0 + w, r0:r0 + 128].rearrange("(b p) t -> p b t", p=128),
          s[:, 0:w].rearrange("p (b t) -> p b t", t=128), reads=[sbuf], writes=[k.convTb])


def evac_out(k, layer, xres, xnext):
    def f(ti, tinfo, c0, w, pt, ptb):
        si, j, t0, r0 = tinfo
        v = 0 if k.c["SEQS"][si][3] else 1
        s, sbuf = k.stg[k.stgi % 4]
        xr, xrb = k.xr[k.stgi % 2]
        k.stgi += 1
        k.dma(xr[:, 0:w], xres[t0:t0 + 128, c0:c0 + w], reads=[k.xresb], writes=[xrb])
        k.tt(s[:, 0:w], pt[:, 0:w], k.gateb[:, v, c0:c0 + w], ALU.mult, [ptb, k.gatebb], [sbuf])
        k.tt(s[:, 0:w], s[:, 0:w], xr[:, 0:w], ALU.add, [sbuf, xrb], [sbuf])
        k.dma(xnext[t0:t0 + 128, c0:c0 + w], s[:, 0:w], reads=[sbuf], writes=[k.xnextb])
    return f


def phase_gate(k, es, layer):
    for v in range(2):
        k.dma(k.gateb[:, v, :], k.grow_scr[layer, v].partition_broadcast(128), reads=[k.growsb], writes=[k.gatebb])


def phase_final(k, src):
    c, P = k.c, k.P
    D = c["D"]
    with ExitStack() as ph:
        xa = [k.sb(ph, f"fxa{i}", [128, D]) for i in range(2)]
        xo = [k.sb(ph, f"fxo{i}", [128, D]) for i in range(2)]
        g, gb = k.sb(ph, "fg", [128, D])
        ss, ssb = k.sb(ph, "fss", [128, 1])
        k.dma(g[:], k.ins["final_norm_g"].partition_broadcast(128), writes=[gb])
        for ti, (si, j, t0, r0) in enumerate(c["TILES"]):
            x, xb = xa[ti % 2]
            o, ob = xo[ti % 2]
            k.dma(x[:], src[t0:t0 + 128, :], reads=[k.xsrcb], writes=[xb])
            k.act(o[:], x[:], AF.Square, [xb], [ob, ssb], accum_out=ss[:, 0:1])
            k.rsqrt_chain(ss[:], ss[:], 1.0 / D, 1e-6, [ssb], [ssb])
            k.stt(o[:], x[:], ss[:, 0:1], g[:], ALU.mult, ALU.mult, [xb, ssb, gb], [ob])
            k.dma(k.outs["y"][t0:t0 + 128, :], o[:], reads=[ob])
    P.barrier()


def panels_of(ranges):
    out = []
    for (a, b) in ranges:
        c0 = a
        while c0 < b:
            w = min(512, b - c0)
            out.append((c0, w))
            c0 += w
    return out


def build(cfg, debug=False, stop_after=None, enabled=("gdn", "ssd"), only_layer=None):
    k = K(cfg, debug)
    c, P = k.c, k.P
    D = c["D"]
    declare_io(k)
    declare_mixer_io(k)
    k.enabled = enabled
    es = ExitStack()
    with es:
        P.bar_tile, _ = k.sb(es, "bar", [128, 8])
        k.pst, k.psb = [], []
        for i in range(8):
            t = es.enter_context(k.nc.psum_tensor(f"ps{i}", [128, 512], F32))
            k.pst.append(t)
            k.psb.append(P.buf(f"ps{i}"))
        k.ident, k.identb = k.sb(es, "ident", [128, 128])
        k.sel, k.selb = k.sb(es, "sel", [2, 2, 128])
        k.dma(k.ident[:], k.ins["ident"][:, :], writes=[k.identb])
        k.dma(k.sel[:], k.ins["sel"][:, :, :], writes=[k.selb])
        k.hTsb = P.buf("hTs")
        k.projb = P.buf("proj")
        k.yscrb = P.buf("yscr")
        k.x1b = P.buf("x1")
        k.x2b = P.buf("x2")
        k.xinb = P.buf("xin")
        k.ofsb = P.buf("ofs")
        k.growsb = P.buf("grows")
        k.convTb = P.buf("convT")
        phase_mod(k, es)
        srcs = [k.ins["x"], k.x1, k.x2]
        srcb = [k.xinb, k.x1b, k.x2b]
        if only_layer is not None:
            srcs[only_layer] = k.ins["x"]
            srcb[only_layer] = k.xinb
        for layer in ([only_layer] if only_layer is not None else range(2)):
            W_in = k.ins["ab_w_in"] if layer == 0 else k.ins["cd_w_in"]
            W_out = k.ins["ab_w_out"] if layer == 0 else k.ins["cd_w_out"]
            ncol = c["IN_AB"] if layer == 0 else c["IN_CD"]
            phase_T(k, srcs[layer], layer, True)
            off = c["AB_OFF"] if layer == 0 else c["CD_OFF"]
            fm0, fm1 = (off[6], off[7]) if layer == 0 else (off[4], off[5])
            pans = panels_of([(0, fm0)]) + [(a, w, fm0) for (a, w) in panels_of([(fm0, fm1)])] + panels_of([(fm1, ncol)])
            phase_P(k, W_in, pans, evac_proj(k))
            if stop_after == ("proj", layer):
                break
            mixers(k, layer)
            if stop_after == ("mix", layer):
                with ExitStack() as ph:
                    t, tb = k.sb(ph, "dbgy", [128, D])
                    for r0 in range(0, c["NTOK"], 128):
                        k.dma(t[:], k.y_scr[r0:r0 + 128, :], reads=[k.yscrb], writes=[tb])
                        k.dma(k.outs["dbg_y"][r0:r0 + 128, :], t[:], reads=[tb])
                break
            phase_T(k, k.y_scr, layer, False)
            with ExitStack() as lay:
                k.gateb, k.gatebb = k.sb(lay, "gateb", [128, 2, D])
                phase_gate(k, lay, layer)
                k.xresb, k.xnextb = srcb[layer], srcb[layer + 1]
                phase_P(k, W_out, panels_of([(0, D)]), evac_out(k, layer, srcs[layer], srcs[layer + 1]))
        if stop_after is None:
            k.xsrcb = k.x2b
            phase_final(k, k.x2)
        if debug:
            with ExitStack() as ph:
                t, tb = k.sb(ph, "dbgt", [128, 4096])
                ncol = max(c["IN_AB"], c["IN_CD"])
                for r0 in range(0, c["NROW"], 128):
                    nr = min(128, c["NROW"] - r0)
                    for c0 in range(0, ncol, 4096):
                        w = min(4096, ncol - c0)
                        k.dma(t[0:nr, 0:w], k.proj[r0:r0 + nr, c0:c0 + w], reads=[k.projb], writes=[tb])
                        k.dma(k.outs["dbg_proj"][r0:r0 + nr, c0:c0 + w], t[0:nr, 0:w], reads=[tb])
                for r0 in range(0, c["NTOK"], 128):
                    k.dma(t[:, 0:D], k.x1[r0:r0 + 128, :], reads=[k.x1b], writes=[tb])
                    k.dma(k.outs["dbg_x1"][r0:r0 + 128, :], t[:, 0:D], reads=[tb])
        P.emit()
    return k


def fT(v, KC):
    v = np.asarray(v, np.float32)
    return np.ascontiguousarray(np.swapaxes(v.reshape(v.shape[:-1] + (KC, 128)), -1, -2))


def consts(c):
    out = {}
    out["ident"] = np.eye(128, dtype=np.float32)
    sel = np.zeros((2, 2, 128), np.float32)
    sel[0, 0, :] = 1.0
    sel[1, 1, :] = 1.0
    out["sel"] = sel
    return out


def core_inputs(c, inp, core, shared):
    D, KC, NP = c["D"], c["KC"], c["NP"]
    m = dict(shared)
    xs = [np.asarray(inp["x_sample"][core], np.float32)]
    for i in range(NP):
        xs.append(np.asarray(inp["x_prompt"][core * NP + i], np.float32))
    m["x"] = np.ascontiguousarray(np.concatenate(xs, axis=0))
    cv = np.stack([np.asarray(inp["c"][core], np.float32), np.asarray(inp["c_ctx"], np.float32)], axis=0)
    m["cT"] = np.ascontiguousarray(np.transpose(cv.reshape(2, KC, 128), (2, 1, 0)))
    return m


def shared_inputs(c, inp):
    D, KC = c["D"], c["KC"]
    m = consts(c)
    f = lambda a: np.ascontiguousarray(np.asarray(a, np.float32))
    m["ada_w"] = f(inp["ada_w"])
    ada_b = f(inp["ada_b"])
    m["ada_bT"] = fT(ada_b[:, :2 * D], 2 * KC)
    m["ada_bg"] = np.ascontiguousarray(np.repeat(ada_b[:, None, 2 * D:], 2, axis=1))
    m["norm_gT"] = fT(f(inp["norm_g"]), KC)
    m["final_norm_g"] = f(inp["final_norm_g"])
    m["ab_w_in"] = f(inp["ab_w_in"][0])
    m["ab_w_out"] = f(inp["ab_w_out"][0])
    m["cd_w_in"] = f(inp["cd_w_in"][0])
    m["cd_w_out"] = f(inp["cd_w_out"][0])
    return m


def declare_mixer_io(k):
    c = k.c
    HA, HB, HC, HD, G = c["H_A"], c["H_B"], c["H_C"], c["H_D"], c["G_D"]
    NP = c["NP"]
    for n, s in [("CN", [2, 128, 128]), ("CT", [2, 128, 128]), ("MASK", [2, 128, 128]), ("neg16", [128, 1]),
                 ("ones_row", [1, 128]), ("ones", [128, 128]),
                 ("gla_w2", [2, 16, HA * 128]), ("gla_b", [2, 1, HA * 128]), ("gla_ng", [256]),
                 ("st_gla", [2, HA, 128, 256]), ("st_gdn", [2, HB, 128, 128]),
                 ("st_ret", [2, HC, 128, 256]), ("st_ssd", [2, HD, 128, 64]),
                 ("ret_eg", [2, 128, HC * 128]), ("ret_eng", [2, 128, HC * 128]), ("ret_ekd", [2, 128, HC * 128]),
                 ("ret_egt", [2, 128, HC]), ("ret_ng", [256]), ("ret_nb", [256]),
                 ("rope_cos", [c["Ts"], 64]), ("rope_sin", [c["Ts"], 64]),
                 ("MSK1", [2, 128, 128]), ("NEGM", [2, 128, 128]), ("NEGS", [2, 128, 128]), ("BM", [2, 7, 128, 128]),
                 ("ssd_cw", [128, (c["WD"] + 2 * c["WD_BC"]) // 128, 3]), ("ssd_cb", [128, (c["WD"] + 2 * c["WD_BC"]) // 128]),
                 ("ssd_alog", [2, HD]), ("ssd_dtb", [2, HD]), ("ssd_d", [HD]), ("ssd_ng", [c["WD"]]),
                 ("gdn_cw", [128, 3 * HB, 3]), ("gdn_alog", [2, HB]), ("gdn_dtb", [2, HB]), ("gdn_ng", [128])]:
        k.din(n, s)
    k.dout("o_gla", [NP, 2, HA, 128, 256])
    k.dout("o_gdn", [NP, 2, HB, 128, 128])
    k.dout("o_ret", [NP, 2, HC, 128, 256])
    k.dout("o_ssd", [NP, 2, HD, 128, 64])
    k.of_scr = k.dscr("of_scr", [c["NTOK"], c["D"]])
    if k.debug:
        k.dout("dbg_y", [c["NTOK"], c["D"]])
        for nm in ("PT", "L", "DTs", "LNe"):
            k.dout("dbg_" + nm, [128, 128])


def load_const(k, es, name, shape, src):
    t, b = k.sb(es, name, shape)
    k.dma(t[:], src, writes=[b])
    return t, b


def mixer_gla(k, kind, si, d):
    c, P = k.c, k.P
    gla = kind == "gla"
    H = c["H_A"] if gla else c["H_C"]
    off = c["AB_OFF"] if gla else c["CD_OFF"]
    HK, HV = H * 128, H * 256
    (t0s, r0s, T, smp, idx) = c["SEQS"][si]
    nch = T // 128
    qscale = 128 ** -0.5 if gla else 1.0
    kscale = 1.0 if gla else 128 ** -0.5
    ycol0 = 0
    st_in = k.ins["st_gla" if gla else "st_ret"]
    st_out = k.outs["o_gla" if gla else "o_ret"]
    with ExitStack() as ph:
        S, _ = k.sb(ph, "S", [128, H, 256])
        Sb = [P.buf(f"S{h}") for h in range(H)]
        CN = load_const(k, ph, "CN", [128, 128], k.ins["CN"][d])
        CT = load_const(k, ph, "CT", [128, 128], k.ins["CT"][d])
        MK = load_const(k, ph, "MK", [128, 128], k.ins["MASK"][d])
        n16 = load_const(k, ph, "n16", [128, 1], k.ins["neg16"][:, :])
        onr = load_const(k, ph, "onr", [1, 128], k.ins["ones_row"][:, :])
        ngt = load_const(k, ph, "ngt", [128, 256], k.ins["gla_ng" if gla else "ret_ng"].partition_broadcast(128))
        if gla:
            w2 = load_const(k, ph, "w2", [16, HK], k.ins["gla_w2"][d])
            gb = load_const(k, ph, "gb", [1, HK], k.ins["gla_b"][d])
        else:
            nbt = load_const(k, ph, "nbt", [128, 256], k.ins["ret_nb"].partition_broadcast(128))
            eg = load_const(k, ph, "eg", [128, HK], k.ins["ret_eg"][d])
            eng = load_const(k, ph, "eng", [128, HK], k.ins["ret_eng"][d])
            ekd = load_const(k, ph, "ekd", [128, HK], k.ins["ret_ekd"][d])
            egt = load_const(k, ph, "egt", [128, H], k.ins["ret_egt"][d])
        if smp:
            for h in range(H):
                k.dma(S[:, h, :], st_in[d, h], writes=[Sb[h]])
        else:
            k.P.add("dve", lambda e: e.memset(S[:], 0.0), writes=Sb)
        q, qb = k.sb(ph, "q", [128, HK])
        kk, kb = k.sb(ph, "k", [128, HK])
        v, vb = k.sb(ph, "v", [128, HV])
        z, zb = k.sb(ph, "z", [128, HV])
        ra, rab = k.sb(ph, "ra", [128, 16])
        raT, raTb = k.sb(ph, "raT", [16, 128])
        e1, e1b = k.sb(ph, "e1", [128, HK])
        sp, spb = k.sb(ph, "sp", [128, HK])
        if gla:
            eg = k.sb(ph, "eg", [128, HK])
            eng = k.sb(ph, "eng", [128, HK])
            ekd = k.sb(ph, "ekd", [128, HK])
            egt = k.sb(ph, "egt", [128, H])
        qs, qsb = k.sb(ph, "qs", [128, HK])
        ks, ksb = k.sb(ph, "ks", [128, HK])
        kh, khb = k.sb(ph, "kh", [128, HK])
        qT, qTb = k.sb(ph, "qT", [128, H, 128])
        kT, kTb = k.sb(ph, "kT", [128, H, 128])
        PT = [k.sb(ph, f"PT{i}", [128, 128]) for i in range(H)]
        oa, oab = k.sb(ph, "oa", [128, HV])
        of, ofb = k.sb(ph, "of", [128, HV])
        jk, jkb = k.sb(ph, "jk", [128, HV])
        st, stb = k.sb(ph, "st", [128, H])
        mu, mub = k.sb(ph, "mu", [128, H])
        cs, csb = k.sb(ph, "cs", [128, 64])
        sn, snb = k.sb(ph, "sn", [128, 64])
        ta, tab = k.sb(ph, "ta", [128, H * 64])
        tb_, tbb = k.sb(ph, "tb", [128, H * 64])
        order = list(range(nch)) if d == 0 else list(range(nch - 1, -1, -1))
        for ci in order:
            r0 = r0s + ci * 128
            t0 = t0s + ci * 128
            pos0 = ci * 128
            k.dma(q[:], k.proj[r0:r0 + 128, off[0]:off[0] + HK], reads=[k.projb], writes=[qb])
            k.dma(kk[:], k.proj[r0:r0 + 128, off[1]:off[1] + HK], reads=[k.projb], writes=[kb])
            k.dma(v[:], k.proj[r0:r0 + 128, off[2]:off[2] + HV], reads=[k.projb], writes=[vb])
            if gla:
                k.dma(ra[:], k.proj[r0:r0 + 128, off[4 + d]:off[4 + d] + 16], reads=[k.projb], writes=[rab])
                pt, ptb = k.ps()
                k.tr(pt[0:16, 0:128], ra[:, 0:16], [rab], [ptb])
                k.cp(raT[:], pt[0:16, 0:128], [ptb], [raTb])
                for hf in range(HK // 512 if HK >= 512 else 1):
                    wd = min(512, HK)
                    sl = slice(hf * wd, (hf + 1) * wd)
                    pl, plb = k.ps()
                    k.mm(pl[:, 0:wd], raT[:, :], w2[0][:, sl], True, False, [raTb, w2[1]], [plb])
                    k.mm(pl[:, 0:wd], onr[0][:, :], gb[0][:, sl], False, True, [onr[1], gb[1]], [plb])
                    k.act(e1[:, sl], pl[:, 0:wd], AF.Exp, [plb], [e1b], scale=-1.0)
                    k.act(sp[:, sl], e1[:, sl], AF.Ln, [e1b], [spb], bias=1.0)
                    pg, pgb = k.ps()
                    k.mm(pg[:, 0:wd], CN[0][:, :], sp[:, sl], True, True, [CN[1], spb], [pgb])
                    k.act(eg[0][:, sl], pg[:, 0:wd], AF.Exp, [pgb], [eg[1]])
                    k.act(eng[0][:, sl], pg[:, 0:wd], AF.Exp, [pgb], [eng[1]], scale=-1.0)
                    pk, pkb = k.ps()
                    k.mm(pk[:, 0:wd], CT[0][:, :], sp[:, sl], True, True, [CT[1], spb], [pkb])
                    k.act(ekd[0][:, sl], pk[:, 0:wd], AF.Exp, [pkb], [ekd[1]])
                pe, peb = k.ps()
                for h in range(H):
                    k.mm(pe[:, 2 * h:2 * h + 1], sp[:, h * 128:(h + 1) * 128], n16[0][:, 0:1], True, True, [spb, n16[1]], [peb])
                k.act(egt[0][:, :], pe[:, 0:2 * H].rearrange("p (h two) -> p h two", two=2)[:, :, 0], AF.Exp, [peb], [egt[1]])
            elif smp:
                k.dma(cs[:], k.ins["rope_cos"][pos0:pos0 + 128, :], writes=[csb])
                k.dma(sn[:], k.ins["rope_sin"][pos0:pos0 + 128, :], writes=[snb])
                csB = cs[:, :].unsqueeze(1).broadcast_to([128, H, 64])
                snB = sn[:, :].unsqueeze(1).broadcast_to([128, H, 64])
                for (xt, xbuf) in ((q, qb), (kk, kb)):
                    xv = xt[:].rearrange("p (h f two) -> p h f two", h=H, two=2)
                    x1, x2 = xv[:, :, :, 0], xv[:, :, :, 1]
                    tav = ta[:].rearrange("p (h f) -> p h f", h=H)
                    tbv = tb_[:].rearrange("p (h f) -> p h f", h=H)
                    k.tt(tav, x1, snB, ALU.mult, [xbuf, snb], [tab])
                    k.tt(tbv, x2, snB, ALU.mult, [xbuf, snb], [tbb])
                    k.tt(x1, x1, csB, ALU.mult, [xbuf, csb], [xbuf])
                    k.tt(x2, x2, csB, ALU.mult, [xbuf, csb], [xbuf])
                    k.tt(x1, x1, tbv, ALU.subtract, [xbuf, tbb], [xbuf])
                    k.tt(x2, x2, tav, ALU.add, [xbuf, tab], [xbuf])
            k.stt(qs[:], q[:], qscale, eg[0][:, :], ALU.mult, ALU.mult, [qb, eg[1]], [qsb])
            k.stt(ks[:], kk[:], kscale, eng[0][:, :], ALU.mult, ALU.mult, [kb, eng[1]], [ksb])
            k.stt(kh[:], kk[:], kscale, ekd[0][:, :], ALU.mult, ALU.mult, [kb, ekd[1]], [khb])
            for (src, srcb, dst, dstb) in ((qs, qsb, qT, qTb), (ks, ksb, kT, kTb)):
                for h0 in range(0, H, 4):
                    nh = min(4, H - h0)
                    pt, ptb = k.ps()
                    for r in range(nh):
                        k.tr(pt[:, r * 128:(r + 1) * 128], src[:, (h0 + r) * 128:(h0 + r + 1) * 128], [srcb], [ptb])
                    k.cp(dst[:, h0:h0 + nh, :], pt[:, 0:nh * 128].rearrange("p (a b) -> p a b", b=128), [ptb], [dstb],
                         eng="act")
            for h in range(H):
                pa, pab = k.ps()
                k.mm(pa[:, 0:128], kT[:, h, :], qT[:, h, :], True, True, [kTb, qTb], [pab])
                P_, Pb_ = PT[h]
                k.tt(P_[:], pa[:, 0:128], MK[0][:, :], ALU.mult, [pab, MK[1]], [Pb_])
            if d == 1:
                k.dma(of[:], k.of_scr[t0:t0 + 128, ycol0:ycol0 + HV], reads=[k.ofsb], writes=[ofb])
            for h in range(H):
                P_, Pb_ = PT[h]
                if h % 2 == 0:
                    po, pob = k.ps()
                osl = slice((h % 2) * 256, (h % 2) * 256 + 256)
                k.mm(po[:, osl], P_[:], v[:, h * 256:(h + 1) * 256], True, False, [Pb_, vb], [pob])
                k.mm(po[:, osl], qT[:, h, :], S[:, h, :], False, True, [qTb, Sb[h]], [pob])
                pS, pSb = k.ps()
                k.mm(pS[:, 0:256], kh[:, h * 128:(h + 1) * 128], v[:, h * 256:(h + 1) * 256], True, True, [khb, vb], [pSb])
                k.stt(S[:, h, :], S[:, h, :], egt[0][:, h:h + 1], pS[:, 0:256], ALU.mult, ALU.add,
                      [Sb[h], egt[1], pSb], [Sb[h]])
                if h % 2 == 1 or h == H - 1:
                    hh0 = h - (h % 2)
                    wd = (h - hh0 + 1) * 256
                    if d == 0:
                        k.cp(oa[:, hh0 * 256:hh0 * 256 + wd], po[:, 0:wd], [pob], [oab], eng="act")
                    else:
                        k.tt(oa[:, hh0 * 256:hh0 * 256 + wd], po[:, 0:wd], of[:, hh0 * 256:hh0 * 256 + wd], ALU.add,
                             [pob, ofb], [oab])
            if d == 0:
                k.dma(k.of_scr[t0:t0 + 128, ycol0:ycol0 + HV], oa[:], reads=[oab], writes=[k.ofsb])
            else:
                k.dma(z[:], k.proj[r0:r0 + 128, off[3]:off[3] + HV], reads=[k.projb], writes=[zb])
                oav = oa[:].rearrange("p (h v) -> p h v", h=H)
                jkv = jk[:].rearrange("p (h v) -> p h v", h=H)
                if not gla:
                    k.red(mu[:], oav, [oab], [mub])
                    k.ts(mu[:], mu[:], 1.0 / 256, ALU.mult, [mub], [mub])
                    k.tt(oav, oav, mu[:, :].unsqueeze(2).broadcast_to([128, H, 256]), ALU.subtract, [oab, mub], [oab])
                k.act(jk[:], oa[:], AF.Square, [oab], [jkb])
                k.red(st[:], jkv, [jkb], [stb])
                k.rsqrt_chain(st[:], st[:], 1.0 / 256, 1e-6 if gla else 1e-5, [stb], [stb])
                k.tt(oav, oav, st[:, :].unsqueeze(2).broadcast_to([128, H, 256]), ALU.mult, [oab, stb], [oab])
                k.tt(oav, oav, ngt[0][:, :].unsqueeze(1).broadcast_to([128, H, 256]), ALU.mult, [oab, ngt[1]], [oab])
                if not gla:
                    k.tt(oav, oav, nbt[0][:, :].unsqueeze(1).broadcast_to([128, H, 256]), ALU.add, [oab, nbt[1]], [oab])
                k.act(jk[:], z[:], AF.Silu, [zb], [jkb])
                k.tt(oa[:], oa[:], jk[:], ALU.mult, [oab, jkb], [oab])
                k.dma(k.y_scr[t0:t0 + 128, ycol0:ycol0 + HV], oa[:], reads=[oab], writes=[k.yscrb])
        if not smp:
            for h in range(H):
                k.dma(st_out[idx, d, h], S[:, h, :], reads=[Sb[h]])
    P.barrier()


def mixers(k, layer):
    c = k.c
    for si in range(len(c["SEQS"])):
        for d in (0, 1):
            if layer == 0:
                mixer_gla(k, "gla", si, d)
            else:
                mixer_gla(k, "ret", si, d)
    for si in range(len(c["SEQS"])):
        for d in (0, 1):
            if layer == 0:
                if "gdn" in k.enabled:
                    mixer_gdn(k, si, d)
            else:
                if "ssd" in k.enabled:
                    mixer_ssd(k, si, d)


def mixer_consts(c):
    m = {}
    t = np.arange(128)[:, None]
    i = np.arange(128)[None, :]
    le, ge, lt, gt = (t <= i), (t >= i), (t < i), (t > i)
    m["CN"] = np.stack([le, ge]).astype(np.float32) * (-1.0 / 16)
    m["CT"] = np.stack([gt, lt]).astype(np.float32) * (-1.0 / 16)
    m["MASK"] = np.stack([le, ge]).astype(np.float32)
    m["MSK1"] = np.stack([gt, lt]).astype(np.float32)
    m["NEGM"] = np.stack([np.where(le, 0.0, -30000.0), np.where(ge, 0.0, -30000.0)]).astype(np.float32)
    m["NEGS"] = np.stack([np.where(gt, 0.0, -30000.0), np.where(lt, 0.0, -30000.0)]).astype(np.float32)
    ii = np.arange(128)[:, None]
    jj = np.arange(128)[None, :]
    bm = []
    for n in range(1, 8):
        sz = 2 ** n
        lo = ((ii // sz) == (jj // sz)) & ((ii % sz) >= sz // 2) & ((jj % sz) < sz // 2)
        bm.append(lo)
    bm = np.stack(bm).astype(np.float32)
    m["BM"] = np.stack([bm, np.transpose(bm, (0, 2, 1))]).astype(np.float32)
    m["neg16"] = np.full((128, 1), -1.0 / 16, np.float32)
    m["ones_row"] = np.ones((1, 128), np.float32)
    m["ones"] = np.ones((128, 128), np.float32)
    HC = c["H_C"]
    lg = np.log1p(-np.exp2(-5.0 - np.arange(HC, dtype=np.float64)))
    pos = np.arange(128, dtype=np.float64)
    eg, eng, ekd, egt = [], [], [], []
    for d in range(2):
        l = lg if d == 0 else lg[::-1]
        cnt = (pos + 1) if d == 0 else (128 - pos)
        g = cnt[:, None] * l[None, :]
        gtot = 128 * l[None, :]
        rep = lambda a: np.repeat(a, 128, axis=1)
        eg.append(rep(np.exp(g)))
        eng.append(rep(np.exp(-g)))
        ekd.append(rep(np.exp(gtot - g)))
        egt.append(np.broadcast_to(np.exp(gtot), (128, HC)))
    m["ret_eg"] = np.stack(eg).astype(np.float32)
    m["ret_eng"] = np.stack(eng).astype(np.float32)
    m["ret_ekd"] = np.stack(ekd).astype(np.float32)
    m["ret_egt"] = np.ascontiguousarray(np.stack(egt)).astype(np.float32)
    Ts, GW = c["Ts"], c["GRID_W"]
    tt_ = np.arange(Ts)
    rows = (tt_ // GW).astype(np.float32)
    cols = (tt_ % GW).astype(np.float32)
    inv_freq = (np.float32(10000.0) ** (-np.arange(32, dtype=np.float32) / 32)).astype(np.float32)
    ang = np.concatenate([rows[:, None] * inv_freq, cols[:, None] * inv_freq], axis=-1).astype(np.float32)
    m["rope_cos"] = np.cos(ang).astype(np.float32)
    m["rope_sin"] = np.sin(ang).astype(np.float32)
    return m


def mixer_shared(c, inp):
    f = lambda a: np.ascontiguousarray(np.asarray(a, np.float32))
    m = mixer_consts(c)
    m["gla_w2"] = f(inp["gla_gate_w2"][0])
    m["gla_b"] = f(inp["gla_gate_b"][0])[:, None, :]
    m["gla_ng"] = f(inp["gla_norm_g"][0])
    m["ret_ng"] = f(inp["ret_norm_g"][0])
    m["ret_nb"] = f(inp["ret_norm_b"][0])
    cwl = lambda w: np.ascontiguousarray(np.transpose(f(w).reshape(3, -1, 128), (2, 1, 0)))
    m["ssd_cw"] = cwl(inp["ssd_conv_w"][0])
    m["ssd_cb"] = np.ascontiguousarray(f(inp["ssd_conv_b"][0]).reshape(-1, 128).T)
    m["ssd_alog"] = f(inp["ssd_a_log"][0])
    m["ssd_dtb"] = f(inp["ssd_dt_bias"][0])
    m["ssd_d"] = f(inp["ssd_d"][0])
    m["ssd_ng"] = f(inp["ssd_norm_g"][0])
    m["gdn_cw"] = cwl(inp["gdn_conv_w"][0])
    m["gdn_alog"] = f(inp["gdn_a_log"][0])
    m["gdn_dtb"] = f(inp["gdn_dt_bias"][0])
    m["gdn_ng"] = f(inp["gdn_norm_g"][0])
    return m


def mixer_core(c, inp, core, m):
    f = lambda a: np.ascontiguousarray(np.asarray(a, np.float32))
    for nm in ("gla", "gdn", "ret", "ssd"):
        m["st_" + nm] = f(np.stack([inp[f"state_{nm}_fwd"][core, 0], inp[f"state_{nm}_bwd"][core, 0]]))
    return m


def fm_load_conv(k, ph, nb, r0, ci, nch, cw, cb, tag):
    X, Xb = k.fmX
    acc, accb = k.fmA
    t1, t1b = k.fmT
    s, sb_ = k.fmS
    k.dma(X[:, 0:nb, :], k.convT[0:nb * 128, r0 - 1:r0 + 129].rearrange("(b p) t -> p b t", p=128),
          reads=[k.convTb], writes=[Xb])
    if ci == 0:
        k.P.add("dve", lambda e: e.memset(X[:, 0:nb, 0:1], 0.0), writes=[Xb])
    if ci == nch - 1:
        k.P.add("dve", lambda e: e.memset(X[:, 0:nb, 129:130], 0.0), writes=[Xb])
    w = lambda j: cw[0][:, 0:nb, j:j + 1].broadcast_to([128, nb, 128])
    k.tt(acc[:, 0:nb, :].bitcast(F32R), X[:, 0:nb, 0:128], w(0), ALU.mult, [Xb, cw[1]], [accb], eng=CONV_ENG)
    k.tt(t1[:, 0:nb, :].bitcast(F32R), X[:, 0:nb, 1:129], w(1), ALU.mult, [Xb, cw[1]], [t1b], eng=CONV_ENG)
    k.tt(acc[:, 0:nb, :].bitcast(F32R), acc[:, 0:nb, :], t1[:, 0:nb, :], ALU.add, [accb, t1b], [accb], eng=CONV_ENG)
    k.tt(t1[:, 0:nb, :].bitcast(F32R), X[:, 0:nb, 2:130], w(2), ALU.mult, [Xb, cw[1]], [t1b], eng=CONV_ENG)
    k.tt(acc[:, 0:nb, :].bitcast(F32R), acc[:, 0:nb, :], t1[:, 0:nb, :], ALU.add, [accb, t1b], [accb], eng=CONV_ENG)
    if cb is not None:
        k.tt(acc[:, 0:nb, :].bitcast(F32R), acc[:, 0:nb, :], cb[0][:, 0:nb].unsqueeze(2).broadcast_to([128, nb, 128]), ALU.add,
             [accb, cb[1]], [accb], eng=CONV_ENG)
    k.act(s[:, 0:nb, :], acc[:, 0:nb, :], AF.Silu, [accb], [sb_])
    return s, sb_


def fm_alloc(k, ph, nb):
    k.fmX = k.sb(ph, "fmX", [128, nb, 130])
    k.fmA = k.sb(ph, "fmA", [128, nb, 128])
    k.fmT = k.sb(ph, "fmT", [128, nb, 128])
    k.fmS = k.sb(ph, "fmS", [128, nb, 128])


def softplus_(k, out, in_, reads, writes, tmp, tmpb):
    k.act(tmp, in_, AF.Exp, reads, [tmpb])
    k.act(out, tmp, AF.Ln, [tmpb], writes, bias=1.0)


def mixer_gdn(k, si, d):
    c, P = k.c, k.P
    H = c["H_B"]
    NB = 3 * H
    NG4 = H // 4
    off = c["AB_OFF"]
    (t0s, r0s, T, smp, idx) = c["SEQS"][si]
    nch = T // 128
    ycol0 = c["WA_V"]
    HV = H * 128
    with ExitStack() as ph:
        S, _ = k.sb(ph, "S", [128, H, 128])
        Sg = [P.buf(f"S{i}") for i in range(NG4)]
        MK = load_const(k, ph, "MK", [128, 128], k.ins["MASK"][d])
        M1 = load_const(k, ph, "M1", [128, 128], k.ins["MSK1"][d])
        NG = load_const(k, ph, "NG", [128, 128], k.ins["NEGM"][d])
        NS = load_const(k, ph, "NS", [128, 128], k.ins["NEGS"][d])
        ON = load_const(k, ph, "ON", [128, 128], k.ins["ones"][:, :])
        BM = load_const(k, ph, "BM", [128, 7, 128], k.ins["BM"][d].rearrange("n i j -> i n j"))
        cw = load_const(k, ph, "cw", [128, NB, 3], k.ins["gdn_cw"][:, :, :])
        dtb = load_const(k, ph, "dtb", [128, H], k.ins["gdn_dtb"][d].partition_broadcast(128))
        nA = load_const(k, ph, "nA", [128, H], k.ins["gdn_alog"][d].partition_broadcast(128))
        ngt = load_const(k, ph, "ngt", [128, 128], k.ins["gdn_ng"].partition_broadcast(128))
        k.act(nA[0][:], nA[0][:], AF.Exp, [nA[1]], [nA[1]])
        k.ts(nA[0][:], nA[0][:], -1.0, ALU.mult, [nA[1]], [nA[1]])
        fm_alloc(k, ph, NB)
        A, Ab = k.fmA
        Tt, Ttb = k.fmT
        X, Xb = k.fmX
        Xf = X[:].rearrange("p a b -> p (a b)")
        Tf = Tt[:].rearrange("p a b -> p (a b)")
        slot = lambda flat, i: flat[:, i * HV:(i + 1) * HV].rearrange("p (h v) -> p h v", h=H)
        KH, KB, BV = None, None, None
        PT, VN, NW = slot(Tf, 0), slot(Tf, 1), slot(Tf, 2)
        KHb = [P.buf("KH") for _ in range(NG4)]
        KBb = [P.buf("KB") for _ in range(NG4)]
        BVb = [P.buf("BV") for _ in range(NG4)]
        PTb = [P.buf("PT") for _ in range(NG4)]
        VNb = [P.buf("VN") for _ in range(NG4)]
        NWb = [P.buf("NW") for _ in range(NG4)]
        slotb = KHb + KBb + BVb + PTb + VNb + NWb
        Ug = [P.buf("U") for _ in range(NG4)]
        Tg = [P.buf("T") for _ in range(NG4)]
        Xg = [P.buf("Xs") for _ in range(NG4)]
        Lg = [P.buf("L") for _ in range(NG4)]
        fz, fzb = k.sb(ph, "fz", [128, 2])
        eps, epsb = k.sb(ph, "eps", [128, 1])
        k.P.add("dve", lambda e: e.memset(eps[:], 1e-6), writes=[epsb])
        sm = {n: k.sb(ph, n, [128, H]) for n in ("br", "ar", "tm", "nl", "lnb", "beta", "sp", "la", "bg")}
        egs, egsb = k.sb(ph, "egs", [128, 3 * H])
        big = {n: k.sb(ph, n, [128, H, 128]) for n in ("LA", "LB", "DG", "L", "T", "U", "Bn", "Xs")}
        e4 = [k.sb(ph, f"e4{i}", [128, 4, 128]) for i in range(4)]
        oa, oab = k.sb(ph, "oa", [128, HV])
        of, ofb = k.sb(ph, "of", [128, HV])
        st, stb = k.sb(ph, "st", [128, H])
        R = lambda ap: ap.bitcast(F32R)
        rc = {}
        for nm_, cst in (("MK", MK), ("M1", M1), ("NG", NG), ("NS", NS), ("ON", ON)):
            t2, b2 = k.sb(ph, nm_ + "r", [128, 128])
            k.P.add("dve", lambda e, t=cst[0], o=t2: e.tensor_copy(o[:].bitcast(F32R), t[:]), reads=[cst[1]], writes=[b2])
            rc[nm_] = (t2, b2)
        idr, idrb = k.sb(ph, "idr", [128, 128])
        k.P.add("dve", lambda e: e.tensor_copy(idr[:].bitcast(F32R), k.ident[:]), reads=[k.identb], writes=[idrb])
        ofv0 = of[:].rearrange("p (h v) -> p h v", h=H)
        if smp:
            for h in range(H):
                k.dma(ofv0[:, h, :], k.ins["st_gdn"][d, h], writes=[ofb])
        else:
            k.P.add("dve", lambda e: e.memset(of[:], 0.0), writes=[ofb])
        k.P.add("dve", lambda e: e.tensor_copy(S[:].bitcast(F32R), ofv0), reads=[ofb], writes=Sg)
        t_ = lambda n: (big[n][0] if n in big else sm[n][0])
        b_ = lambda n: (big[n][1] if n in big else sm[n][1])
        bcH = lambda ap2: ap2.unsqueeze(2).broadcast_to([128, H, 128])
        bcM = lambda ap2: ap2.unsqueeze(1).broadcast_to([128, H, 128])
        g4s = lambda g: slice(4 * g, 4 * g + 4)
        flat4 = lambda ap3: ap3.rearrange("p a b -> p (a b)")
        order = list(range(nch)) if d == 0 else list(range(nch - 1, -1, -1))
        for ci in order:
            r0 = r0s + ci * 128
            t0 = t0s + ci * 128
            s, sb_ = fm_load_conv(k, ph, NB, r0, ci, nch, cw, None, "gdn")
            k.act(R(A[:, 0:2 * H, :]), s[:, 0:2 * H, :], AF.Square, [sb_], [Ab])
            for g4 in range(2 * H // 4):
                pn, pnb = k.ps()
                k.mm(pn[:, :], R(rc["ON"][0][:, :]), R(flat4(A[:, 4 * g4:4 * g4 + 4, :])), True, True, [rc["ON"][1], Ab], [pnb])
                k.act(R(flat4(Tt[:, 4 * g4:4 * g4 + 4, :])), pn[:, :], AF.Sqrt, [pnb, epsb], [Ttb], bias=eps[:, 0:1])
            k.recip(R(Tt[:, 0:2 * H, :]), Tt[:, 0:2 * H, :], [Ttb], [Ttb])
            k.stt(R(A[:, 0:H, :]), s[:, 0:H, :], 128 ** -0.5, Tt[:, 0:H, :], ALU.mult, ALU.mult, [sb_, Ttb], [Ab])
            k.tt(R(A[:, H:2 * H, :]), s[:, H:2 * H, :], Tt[:, H:2 * H, :], ALU.mult, [sb_, Ttb], [Ab])
            k.dma(t_("br")[:], k.proj[r0:r0 + 128, off[8 + d]:off[8 + d] + H], reads=[k.projb], writes=[b_("br")])
            k.dma(t_("ar")[:], k.proj[r0:r0 + 128, off[10 + d]:off[10 + d] + H], reads=[k.projb], writes=[b_("ar")])
            k.act(t_("tm")[:], t_("br")[:], AF.Exp, [b_("br")], [b_("tm")], scale=-1.0)
            k.act(t_("nl")[:], t_("tm")[:], AF.Ln, [b_("tm")], [b_("nl")], bias=1.0)
            k.ts(t_("lnb")[:], t_("nl")[:], -1.0, ALU.mult, [b_("nl")], [b_("lnb")])
            k.act(t_("beta")[:], t_("nl")[:], AF.Exp, [b_("nl")], [b_("beta")], scale=-1.0)
            k.tt(t_("ar")[:], t_("ar")[:], dtb[0][:], ALU.add, [b_("ar"), dtb[1]], [b_("ar")])
            softplus_(k, t_("sp")[:], t_("ar")[:], [b_("ar")], [b_("sp")], t_("tm")[:], b_("tm"))
            k.tt(R(t_("la")[:]), t_("sp")[:], nA[0][:], ALU.mult, [b_("sp"), nA[1]], [b_("la")])
            la = t_("la")
            pc, pcb = k.ps()
            k.mm(pc[:, 0:H], R(rc["MK"][0][:, :]), R(la[:, :]), True, True, [rc["MK"][1], b_("la")], [pcb])
            k.mm(pc[:, H:2 * H], R(rc["M1"][0][:, :]), R(la[:, :]), True, True, [rc["M1"][1], b_("la")], [pcb])
            k.mm(pc[:, 2 * H:3 * H], R(rc["ON"][0][:, :]), R(la[:, :]), True, True, [rc["ON"][1], b_("la")], [pcb])
            k.act(egs[:], pc[:, 0:3 * H], AF.Exp, [pcb], [egsb])
            eg, ekd, egt = egs[:, 0:H], egs[:, H:2 * H], egs[:, 2 * H:3 * H]
            k.tt(t_("bg")[:], t_("beta")[:], eg, ALU.mult, [b_("beta"), egsb], [b_("bg")])
            k.tt(R(t_("LA")[:]), bcH(la[:, :]), bcM(M1[0][:, :]), ALU.mult, [b_("la"), M1[1]], [b_("LA")])
            k.tt(R(t_("LB")[:]), bcH(la[:, :]), bcM(MK[0][:, :]), ALU.mult, [b_("la"), MK[1]], [b_("LB")])
            k.tt(R(t_("DG")[:]), bcH(t_("lnb")[:, :]), bcM(k.ident[:, :]), ALU.mult, [b_("lnb"), k.identb], [b_("DG")])
            k.P.add("dve", lambda e: e.memset(fz[:], 0.0), reads=[Ttb, Xb, b_("Xs"), b_("LA"), b_("LB"), b_("DG")] + slotb + Xg, writes=[Ttb, Xb, fzb, b_("Xs"), b_("LA"), b_("LB"), b_("DG")] + slotb + Xg)
            for g in range(NG4):
                hs = g4s(g)
                pg, pgb = k.ps()
                pa, pab = k.ps()
                pd, pdb = k.ps()
                pl, plb = k.ps()
                for r in range(4):
                    h = 4 * g + r
                    cs = slice(r * 128, (r + 1) * 128)
                    qnT, knT = A[:, h, :], A[:, H + h, :]
                    k.mm(pg[:, cs], R(knT), R(knT), True, True, [Ab], [pgb])
                    k.mm(pa[:, cs], R(knT), R(qnT), True, True, [Ab], [pab])
                    k.mm(pd[:, cs], R(t_("LA")[:, h, :]), R(rc["MK"][0][:, :]), True, False, [b_("LA"), rc["MK"][1]], [pdb])
                    k.mm(pd[:, cs], R(idr[:, :]), R(rc["NG"][0][:, :]), False, True, [idrb, rc["NG"][1]], [pdb])
                    k.mm(pl[:, cs], R(t_("LB")[:, h, :]), R(rc["M1"][0][:, :]), True, False, [b_("LB"), rc["M1"][1]], [plb])
                    k.mm(pl[:, cs], R(t_("DG")[:, h, :]), R(rc["ON"][0][:, :]), False, False, [b_("DG"), rc["ON"][1]], [plb])
                    k.mm(pl[:, cs], R(idr[:, :]), R(rc["NS"][0][:, :]), False, True, [idrb, rc["NS"][1]], [plb])
                E0, E0b = e4[2 * (g % 2)]
                E1, E1b = e4[2 * (g % 2) + 1]
                k.act(flat4(E0[:]), pd[:, :], AF.Exp, [pdb], [E0b])
                k.act(flat4(E1[:]), pl[:, :], AF.Exp, [plb], [E1b])
                k.tt(R(PT[:, hs, :]), pa[:, :].rearrange("p (a b) -> p a b", b=128), E0[:], ALU.mult, [pab, E0b], [PTb[g]])
                k.tt(t_("L")[:, hs, :], pg[:, :].rearrange("p (a b) -> p a b", b=128), E1[:], ALU.mult, [pgb, E1b], [Lg[g]])
            k.P.add("dve", lambda e: e.memset(fz[:], 0.0), reads=[Ttb, Xb, b_("Xs"), b_("LA"), b_("LB"), b_("DG")] + slotb + Xg, writes=[Ttb, Xb, fzb, b_("Xs"), b_("LA"), b_("LB"), b_("DG")] + slotb + Xg)
            KH, KB, BV = t_("LA"), t_("LB"), t_("DG")
            for g in range(NG4):
                hs = g4s(g)
                pt, ptb = k.ps()
                for r in range(4):
                    k.tr(pt[:, r * 128:(r + 1) * 128], A[:, H + 4 * g + r, :], [Ab], [ptb])
                ptv = pt[:, :].rearrange("p (a b) -> p a b", b=128)
                k.tt(R(KH[:, hs, :]), ptv, ekd[:, hs].unsqueeze(2).broadcast_to([128, 4, 128]), ALU.mult, [ptb, egsb], [KHb[g]])
                k.tt(R(KB[:, hs, :]), ptv, t_("bg")[:, hs].unsqueeze(2).broadcast_to([128, 4, 128]), ALU.mult,
                     [ptb, b_("bg")], [KBb[g]])
                pt2, pt2b = k.ps()
                for r in range(4):
                    k.tr(pt2[:, r * 128:(r + 1) * 128], s[:, 2 * H + 4 * g + r, :], [sb_], [pt2b])
                k.tt(R(BV[:, hs, :]), pt2[:, :].rearrange("p (a b) -> p a b", b=128),
                     t_("beta")[:, hs].unsqueeze(2).broadcast_to([128, 4, 128]), ALU.mult, [pt2b, b_("beta")], [BVb[g]])
            k.tt(R(t_("Bn")[:]), t_("L")[:], bcM(BM[0][:, 0, :]), ALU.mult, Lg + [BM[1]], [b_("Bn")])
            k.tt(R(t_("T")[:]), bcM(k.ident[:, :]), t_("Bn")[:], ALU.subtract, [k.identb, b_("Bn")], Tg)
            for g in range(NG4):
                pu, pub = k.ps()
                for r in range(4):
                    k.tr(pu[:, r * 128:(r + 1) * 128], t_("T")[:, 4 * g + r, :], [Tg[g]], [pub])
                k.cp(R(flat4(t_("U")[:, g4s(g), :])), pu[:, :], [pub], [Ug[g]], eng="act")
            for n in range(2, 8):
                k.tt(R(t_("Bn")[:]), t_("L")[:], bcM(BM[0][:, n - 1, :]), ALU.mult, Lg + [BM[1]], [b_("Bn")])
                for g in range(NG4):
                    px, pxb = k.ps()
                    for r in range(4):
                        h = 4 * g + r
                        k.mm(px[:, r * 128:(r + 1) * 128], R(t_("Bn")[:, h, :]), R(t_("U")[:, h, :]), True, True,
                             [b_("Bn"), Ug[g]], [pxb])
                    k.cp(R(flat4(t_("Xs")[:, g4s(g), :])), px[:, :], [pxb], [Xg[g]], eng="act")
                for g in range(NG4):
                    py, pyb = k.ps()
                    for r in range(4):
                        h = 4 * g + r
                        k.mm(py[:, r * 128:(r + 1) * 128], R(t_("T")[:, h, :]), R(t_("Xs")[:, h, :]), True, True,
                             [Tg[g], Xg[g]], [pyb])
                    k.tt(R(t_("U")[:, g4s(g), :]), t_("U")[:, g4s(g), :], py[:, :].rearrange("p (a b) -> p a b", b=128),
                         ALU.subtract, [Ug[g], pyb], [Ug[g]])
                if n < 7:
                    for g in range(NG4):
                        pz, pzb = k.ps()
                        for r in range(4):
                            k.tr(pz[:, r * 128:(r + 1) * 128], t_("U")[:, 4 * g + r, :], [Ug[g]], [pzb])
                        k.cp(R(flat4(t_("T")[:, g4s(g), :])), pz[:, :], [pzb], [Tg[g]], eng="act")
            if d == 1:
                k.dma(of[:], k.of_scr[t0:t0 + 128, ycol0:ycol0 + HV], reads=[k.ofsb], writes=[ofb])
            for g in range(NG4):
                hs = g4s(g)
                pw, pwb = k.ps()
                for r in range(4):
                    h = 4 * g + r
                    k.mm(pw[:, r * 128:(r + 1) * 128], R(KB[:, h, :]), R(t_("U")[:, h, :]), True, True, [KBb[g], Ug[g]], [pwb])
                k.act(R(flat4(NW[:, hs, :])), pw[:, :], AF.Copy, [pwb], [NWb[g]], scale=-1.0)
                pv, pvb = k.ps()
                for r in range(4):
                    h = 4 * g + r
                    cs = slice(r * 128, (r + 1) * 128)
                    k.mm(pv[:, cs], R(t_("U")[:, h, :]), R(BV[:, h, :]), True, False, [Ug[g], BVb[g]], [pvb])
                    k.mm(pv[:, cs], R(NW[:, h, :]), R(S[:, h, :]), False, True, [NWb[g], Sg[g]], [pvb])
                k.cp(R(flat4(VN[:, hs, :])), pv[:, :], [pvb], [VNb[g]], eng="act")
                po, pob = k.ps()
                pr, prb = k.ps()
                pS, pSb = k.ps()
                for r in range(4):
                    h = 4 * g + r
                    cs = slice(r * 128, (r + 1) * 128)
                    k.mm(po[:, cs], R(PT[:, h, :]), R(VN[:, h, :]), True, True, [PTb[g], VNb[g]], [pob])
                    k.mm(pr[:, cs], R(A[:, h, :]), R(S[:, h, :]), True, True, [Ab, Sg[g]], [prb])
                    k.mm(pS[:, cs], R(KH[:, h, :]), R(VN[:, h, :]), True, True, [KHb[g], VNb[g]], [pSb])
                E0, E0b = e4[g % 4]
                k.tt(E0[:], pr[:, :].rearrange("p (a b) -> p a b", b=128), eg[:, hs].unsqueeze(2).broadcast_to([128, 4, 128]),
                     ALU.mult, [prb, egsb], [E0b])
                osl = oa[:, 4 * g * 128:(4 * g + 4) * 128]
                k.tt(osl, po[:, :], flat4(E0[:]), ALU.add, [pob, E0b], [oab])
                if d == 1:
                    k.tt(osl, osl, of[:, 4 * g * 128:(4 * g + 4) * 128], ALU.add, [oab, ofb], [oab])
                k.tt(R(S[:, hs, :]), S[:, hs, :], egt[:, hs].unsqueeze(2).broadcast_to([128, 4, 128]), ALU.mult, [Sg[g], egsb], [Sg[g]])
                k.tt(R(S[:, hs, :]), S[:, hs, :], pS[:, :].rearrange("p (a b) -> p a b", b=128), ALU.add, [Sg[g], pSb], [Sg[g]])
            k.P.add("dve", lambda e: e.memset(fz[:], 0.0), reads=[Ttb, Xb, b_("Xs"), b_("LA"), b_("LB"), b_("DG")] + slotb + Xg, writes=[Ttb, Xb, fzb, b_("Xs"), b_("LA"), b_("LB"), b_("DG")] + slotb + Xg)
            if d == 0:
                k.dma(k.of_scr[t0:t0 + 128, ycol0:ycol0 + HV], oa[:], reads=[oab], writes=[k.ofsb])
            else:
                z, zb = Xf[:, 0:HV], Xb
                jk, jkb = t_("Bn")[:].rearrange("p a b -> p (a b)"), b_("Bn")
                k.dma(z, k.proj[r0:r0 + 128, off[7]:off[7] + HV], reads=[k.projb], writes=[zb])
                oav = oa[:].rearrange("p (h v) -> p h v", h=H)
                k.act(R(jk), oa[:], AF.Square, [oab], [jkb])
                k.red(st[:], t_("Bn")[:], [jkb], [stb])
                k.rsqrt_chain(st[:], st[:], 1.0 / 128, 1e-6, [stb], [stb])
                k.tt(oav, oav, bcH(st[:, :]), ALU.mult, [oab, stb], [oab])
                k.tt(oav, oav, bcM(ngt[0][:, :]), ALU.mult, [oab, ngt[1]], [oab])
                k.act(R(jk), z, AF.Silu, [zb], [jkb])
                k.tt(oa[:], oa[:], jk, ALU.mult, [oab, jkb], [oab])
                k.dma(k.y_scr[t0:t0 + 128, ycol0:ycol0 + HV], oa[:], reads=[oab], writes=[k.yscrb])
        if not smp:
            for h in range(H):
                k.dma(k.outs["o_gdn"][idx, d, h], S[:, h, :], reads=[Sg[h // 4]])
    P.barrier()


def mixer_ssd(k, si, d):
    c, P = k.c, k.P
    H, G = c["H_D"], c["G_D"]
    HG = H // G
    WD = c["WD"]
    NBX = WD // 128
    NB = NBX + 2 * G
    off = c["CD_OFF"]
    (t0s, r0s, T, smp, idx) = c["SEQS"][si]
    nch = T // 128
    ycol0 = c["WC_V"]
    with ExitStack() as ph:
        S, _ = k.sb(ph, "S", [128, H, 64])
        Sb = [P.buf(f"S{g}") for g in range(H // 8)]
        MK = load_const(k, ph, "MK", [128, 128], k.ins["MASK"][d])
        M1 = load_const(k, ph, "M1", [128, 128], k.ins["MSK1"][d])
        NG = load_const(k, ph, "NG", [128, 128], k.ins["NEGM"][d])
        ON = load_const(k, ph, "ON", [128, 128], k.ins["ones"][:, :])
        cw = load_const(k, ph, "cw", [128, NB, 3], k.ins["ssd_cw"][:, :, :])
        cb = load_const(k, ph, "cb", [128, NB], k.ins["ssd_cb"][:, :])
        dtb = load_const(k, ph, "dtb", [128, H], k.ins["ssd_dtb"][d].partition_broadcast(128))
        nA = load_const(k, ph, "nA", [128, H], k.ins["ssd_alog"][d].partition_broadcast(128))
        dsk = load_const(k, ph, "dsk", [128, H], k.ins["ssd_d"].partition_broadcast(128))
        ngt = load_const(k, ph, "ngt", [128, WD], k.ins["ssd_ng"].partition_broadcast(128))
        k.act(nA[0][:], nA[0][:], AF.Exp, [nA[1]], [nA[1]])
        k.ts(nA[0][:], nA[0][:], -1.0, ALU.mult, [nA[1]], [nA[1]])
        if smp:
            for h in range(H):
                k.dma(S[:, h, :], k.ins["st_ssd"][d, h], writes=[Sb[h // 8]])
        else:
            k.P.add("dve", lambda e: e.memset(S[:], 0.0), writes=Sb)
        fm_alloc(k, ph, NB)
        xt, xtb = k.sb(ph, "xt", [128, H, 64])
        Bt, Btb = k.sb(ph, "Bt", [128, G, 128])
        dtr, dtrb = k.sb(ph, "dtr", [128, H])
        dt, dtb_ = k.sb(ph, "dt", [128, H])
        tm, tmb = k.sb(ph, "tm", [128, H])
        la, lab = k.sb(ph, "la", [128, H])
        egs, egsb = k.sb(ph, "egs", [128, 3 * H])
        xdt, xdtb = k.sb(ph, "xdt", [128, H, 64])
        xdk, xdkb = k.sb(ph, "xdk", [128, H, 64])
        LA, LAb = k.sb(ph, "LA", [128, H, 128])
        ATs = [k.sb(ph, f"ATs{i}", [128, 128]) for i in range(2)]
        DTs = [k.sb(ph, f"DTs{i}", [128, 4, 128]) for i in range(2)]
        Pall = [k.sb(ph, f"Pall{i}", [128, 4, 128]) for i in range(H // 4)]
        tmp, tmpb = k.sb(ph, "tmp", [128, 8, 64])
        oa, oab = k.sb(ph, "oa", [128, WD])
        of, ofb = k.sb(ph, "of", [128, WD])
        z, zb = k.sb(ph, "z", [128, WD])
        st, stb = k.sb(ph, "st", [128, G])
        order = list(range(nch)) if d == 0 else list(range(nch - 1, -1, -1))
        for ci in order:
            r0 = r0s + ci * 128
            t0 = t0s + ci * 128
            s, sb_ = fm_load_conv(k, ph, NB, r0, ci, nch, cw, cb, "ssd")
            for b0 in range(0, NBX, 4):
                nbk = min(4, NBX - b0)
                pt, ptb = k.ps()
                for r in range(nbk):
                    k.tr(pt[:, r * 128:(r + 1) * 128], s[:, b0 + r, :], [sb_], [ptb])
                k.cp(xt[:].rearrange("p h f -> p (h f)")[:, b0 * 128:(b0 + nbk) * 128], pt[:, 0:nbk * 128], [ptb], [xtb],
                     eng="act")
            pt, ptb = k.ps()
            for g in range(G):
                k.tr(pt[:, g * 128:(g + 1) * 128], s[:, NBX + g, :], [sb_], [ptb])
            k.cp(Bt[:].rearrange("p g n -> p (g n)"), pt[:, 0:G * 128], [ptb], [Btb], eng="act")
            k.dma(dtr[:], k.proj[r0:r0 + 128, off[6 + d]:off[6 + d] + H], reads=[k.projb], writes=[dtrb])
            k.tt(dtr[:], dtr[:], dtb[0][:], ALU.add, [dtrb, dtb[1]], [dtrb])
            softplus_(k, dt[:], dtr[:], [dtrb], [dtb_], tm[:], tmb)
            k.tt(la[:], dt[:], nA[0][:], ALU.mult, [dtb_, nA[1]], [lab])
            pc, pcb = k.ps()
            k.mm(pc[:, 0:H], MK[0][:, :], la[:, :], True, True, [MK[1], lab], [pcb])
            k.mm(pc[:, H:2 * H], M1[0][:, :], la[:, :], True, True, [M1[1], lab], [pcb])
            k.mm(pc[:, 2 * H:3 * H], ON[0][:, :], la[:, :], True, True, [ON[1], lab], [pcb])
            k.act(egs[:], pc[:, 0:3 * H], AF.Exp, [pcb], [egsb])
            eg, ekd, egt = egs[:, 0:H], egs[:, H:2 * H], egs[:, 2 * H:3 * H]
            k.tt(xdt[:], xt[:], dt[:, :].unsqueeze(2).broadcast_to([128, H, 64]), ALU.mult, [xtb, dtb_], [xdtb])
            k.tt(xdk[:], xdt[:], ekd.unsqueeze(2).broadcast_to([128, H, 64]), ALU.mult, [xdtb, egsb], [xdkb])
            k.tt(LA[:], la[:, :].unsqueeze(2).broadcast_to([128, H, 128]),
                 M1[0][:, :].unsqueeze(1).broadcast_to([128, H, 128]), ALU.mult, [lab, M1[1]], [LAb])
            for g in range(G):
                A_, Ab_ = ATs[g % 2]
                pa, pab = k.ps()
                k.mm(pa[:, 0:128], s[:, NBX + g, :], s[:, NBX + G + g, :], True, True, [sb_], [pab])
                k.cp(A_[:], pa[:, 0:128], [pab], [Ab_], eng="act")
                for q4 in range(HG // 4):
                    qi = (g * HG) // 4 + q4
                    D_, Db_ = DTs[qi % 2]
                    P_, Pb_ = Pall[qi]
                    pd, pdb = k.ps()
                    for r in range(4):
                        h = g * HG + q4 * 4 + r
                        k.mm(pd[:, r * 128:(r + 1) * 128], LA[:, h, :], MK[0][:, :], True, False, [LAb, MK[1]], [pdb])
                        k.mm(pd[:, r * 128:(r + 1) * 128], k.ident[:, :], NG[0][:, :], False, True, [k.identb, NG[1]], [pdb])
                    k.act(D_[:].rearrange("p a b -> p (a b)"), pd[:, :], AF.Exp, [pdb], [Db_])
                    k.tt(P_[:], D_[:], A_[:, :].unsqueeze(1).broadcast_to([128, 4, 128]), ALU.mult, [Db_, Ab_], [Pb_])
            for hb0 in range(0, H, 8):
                g = hb0 // HG
                po, pob = k.ps()
                pr, prb = k.ps()
                pS, pSb = k.ps()
                for hh in range(8):
                    h = hb0 + hh
                    P_, Pb_ = Pall[h // 4]
                    r = h % 4
                    k.mm(po[:, hh * 64:(hh + 1) * 64], P_[:, r, :], xdt[:, h, :], True, True, [Pb_, xdtb], [pob])
                    k.mm(pr[:, hh * 64:(hh + 1) * 64], s[:, NBX + G + g, :], S[:, h, :], True, True,
                         [sb_, Sb[h // 8]], [prb])
                    k.mm(pS[:, hh * 64:(hh + 1) * 64], Bt[:, g, :], xdk[:, h, :], True, True, [Btb, xdkb], [pSb])
                hs = slice(hb0, hb0 + 8)
                k.tt(tmp[:], pr[:, :].rearrange("p (h f) -> p h f", f=64),
                     eg[:, hs].unsqueeze(2).broadcast_to([128, 8, 64]), ALU.mult, [prb, egsb], [tmpb])
                oav = oa[:, hb0 * 64:(hb0 + 8) * 64]
                k.tt(oav, po[:, :], tmp[:].rearrange("p h f -> p (h f)"), ALU.add, [pob, tmpb], [oab])
                k.tt(S[:, hs, :], S[:, hs, :], egt[:, hs].unsqueeze(2).broadcast_to([128, 8, 64]), ALU.mult,
                     [Sb[hb0 // 8], egsb], [Sb[hb0 // 8]])
                k.tt(S[:, hs, :], S[:, hs, :], pS[:, :].rearrange("p (h f) -> p h f", f=64), ALU.add,
                     [Sb[hb0 // 8], pSb], [Sb[hb0 // 8]])
            if d == 0:
                k.dma(k.of_scr[t0:t0 + 128, ycol0:ycol0 + WD], oa[:], reads=[oab], writes=[k.ofsb])
            else:
                k.dma(of[:], k.of_scr[t0:t0 + 128, ycol0:ycol0 + WD], reads=[k.ofsb], writes=[ofb])
                k.dma(z[:], k.proj[r0:r0 + 128, off[5]:off[5] + WD], reads=[k.projb], writes=[zb])
                k.tt(oa[:], oa[:], of[:], ALU.add, [oab, ofb], [oab])
                k.tt(of[:].rearrange("p (h f) -> p h f", f=64), xt[:], dsk[0][:, :].unsqueeze(2).broadcast_to([128, H, 64]),
                     ALU.mult, [xtb, dsk[1]], [ofb])
                k.tt(oa[:], oa[:], of[:], ALU.add, [oab, ofb], [oab])
                k.act(of[:], z[:], AF.Silu, [zb], [ofb])
                k.tt(oa[:], oa[:], of[:], ALU.mult, [oab, ofb], [oab])
                k.act(of[:], oa[:], AF.Square, [oab], [ofb])
                k.red(st[:], of[:].rearrange("p (g f) -> p g f", g=G), [ofb], [stb])
                k.rsqrt_chain(st[:], st[:], 1.0 / (WD // G), 1e-6, [stb], [stb])
                k.tt(oa[:].rearrange("p (g f) -> p g f", g=G), oa[:].rearrange("p (g f) -> p g f", g=G),
                     st[:, :].unsqueeze(2).broadcast_to([128, G, WD // G]), ALU.mult, [oab, stb], [oab])
                k.tt(oa[:], oa[:], ngt[0][:], ALU.mult, [oab, ngt[1]], [oab])
                k.dma(k.y_scr[t0:t0 + 128, ycol0:ycol0 + WD], oa[:], reads=[oab], writes=[k.yscrb])
        if not smp:
            for h in range(H):
                k.dma(k.outs["o_ssd"][idx, d, h], S[:, h, :], reads=[Sb[h // 8]])
    P.barrier()


_CACHE = {}


def kernel(**inp):
    cfg = FULL_CFG
    c = derive(cfg)
    NP = c["NP"]
    enabled = ENABLED
    if "k" not in _CACHE:
        _CACHE["k"] = build(cfg, enabled=enabled)
    k = _CACHE["k"]
    sh = shared_inputs(c, inp)
    sh.update(mixer_shared(c, inp))
    in_maps = []
    for core in range(8):
        m = core_inputs(c, inp, core, sh)
        m = mixer_core(c, inp, core, m)
        in_maps.append({n: v for n, v in m.items() if n in k.ins})
    res = run_bass_kernel_spmd(k.nc, in_maps, core_ids=list(range(8)))
    R = res.results
    Ts, Tp, D = c["Ts"], c["Tp"], c["D"]
    y_sample = np.stack([np.asarray(R[i]["y"][:Ts]) for i in range(8)]).astype(np.float32)
    y_prompt = np.stack([np.asarray(R[i]["y"][Ts + j * Tp:Ts + (j + 1) * Tp]) for i in range(8) for j in range(NP)]).astype(np.float32)

    def st(name, d):
        a = np.concatenate([np.asarray(R[i][name][:, d]) for i in range(8)], axis=0)
        return np.ascontiguousarray(a[:, None]).astype(np.float32)

    return (y_prompt, y_sample, st("o_gla", 0), st("o_gla", 1), st("o_gdn", 0), st("o_gdn", 1),
            st("o_ret", 0), st("o_ret", 1), st("o_ssd", 0), st("o_ssd", 1))


ENABLED = ("gdn", "ssd")


def simulate(P):
    sems = {}
    pc = {e: 0 for e in P.ENGS}
    progress = True
    while progress:
        progress = False
        for e in P.ENGS:
            while pc[e] < len(P.sim[e]):
                waits, inc, idx = P.sim[e][pc[e]]
                if all(sems.get(k_, 0) >= v for (k_, v) in waits):
                    if inc is not None:
                        sems[inc[0]] = sems.get(inc[0], 0) + inc[1]
                    pc[e] += 1
                    progress = True
                else:
                    break
    stuck = {e: (pc[e], len(P.sim[e])) for e in P.ENGS if pc[e] < len(P.sim[e])}
    return stuck
```

```python
import math
import os
import numpy as np
CUT = int(os.environ.get('GDN_CUT', 9))
NLV = int(os.environ.get('GDN_NLV', 8))
SUB = int(os.environ.get('GDN_SUB', 9))
TRM = int(os.environ.get('GDN_TRM', 0))
NHD = int(os.environ.get('GDN_NHD', 99))
PAR = int(os.environ.get('GDN_PAR', 2))
from contextlib import ExitStack
import concourse.bass as bass
import concourse.mybir as mybir
from concourse.bass_utils import run_bass_kernel_spmd

F32 = mybir.dt.float32
F32R = mybir.dt.float32r
BF16 = mybir.dt.bfloat16
AF = mybir.ActivationFunctionType
ALU = mybir.AluOpType
AX = mybir.AxisListType

FULL_CFG = dict(D=4096, H_A=8, H_B=16, H_C=8, H_D=32, G_D=4, Ts=2048, Tp=256, NP=2, GRID_W=64)


def derive(cfg):
    c = dict(cfg)
    D = c["D"]
    c["KC"] = D // 128
    c["WA_QK"] = c["H_A"] * 128
    c["WA_V"] = c["H_A"] * 256
    c["WB"] = c["H_B"] * 128
    c["WC_QK"] = c["H_C"] * 128
    c["WC_V"] = c["H_C"] * 256
    c["WD"] = c["H_D"] * 64
    c["WD_BC"] = c["G_D"] * 128
    ab = (c["WA_QK"], c["WA_QK"], c["WA_V"], c["WA_V"], 16, 16, 3 * c["WB"], c["WB"],
          c["H_B"], c["H_B"], c["H_B"], c["H_B"])
    cd = (c["WC_QK"], c["WC_QK"], c["WC_V"], c["WC_V"], c["WD"] + 2 * c["WD_BC"], c["WD"], c["H_D"], c["H_D"])
    c["AB_OFF"] = [0] + [int(v) for v in np.cumsum(ab)]
    c["CD_OFF"] = [0] + [int(v) for v in np.cumsum(cd)]
    c["IN_AB"] = c["AB_OFF"][-1]
    c["IN_CD"] = c["CD_OFF"][-1]
    c["NTOK"] = c["Ts"] + c["NP"] * c["Tp"]
    seqs = [(0, 1, c["Ts"], True, 0)]
    tok = c["Ts"]
    row = c["Ts"] + 2
    for i in range(c["NP"]):
        seqs.append((tok, row + 1, c["Tp"], False, i))
        tok += c["Tp"]
        row += c["Tp"] + 2
    c["SEQS"] = seqs
    c["NROW"] = row
    tiles = []
    for si, (t0, r0, T, smp, idx) in enumerate(seqs):
        for j in range(T // 128):
            tiles.append((si, j, t0 + j * 128, r0 + j * 128))
    c["TILES"] = tiles
    return c


class Buf:
    __slots__ = ("name", "last_w", "readers")

    def __init__(self, prog, name):
        self.name = name
        self.last_w = prog.last_barrier
        self.readers = []
        prog.bufs.append(self)


class Op:
    __slots__ = ("eng", "fn", "deps", "needs_inc", "ticket", "dma", "dsem", "dticket", "dn", "idx")


class Prog:
    ENGS = ("pe", "act", "dve", "pool", "sp")
    K = 8
    EPOCH = 30000

    def __init__(self, nc):
        self.nc = nc
        self.ops = {e: [] for e in self.ENGS}
        self.nops = 0
        self.last_barrier = None
        self.bufs = []
        self.bar_tile = None
        self.last_compute = {}
        self.coarse = bool(int(os.environ.get("COARSE", "0")))

    def buf(self, name="b"):
        return Buf(self, name)

    def barrier(self):
        t = self.bar_tile
        bl = list(self.bufs)
        op = self.add("dve", lambda e: e.memset(t[:], 0.0), reads=bl, writes=bl)
        self.last_barrier = op
        self.bufs = []
        return op

    def add(self, eng, fn, reads=(), writes=(), dma=False, tag=None):
        op = Op()

        op.eng = eng
        op.fn = fn
        op.dma = dma
        op.needs_inc = False
        op.ticket = None
        op.idx = self.nops
        self.nops += 1
        deps = {}
        for b in reads:
            w = b.last_w
            if w is not None:
                deps[id(w)] = (w, True)
        for b in writes:
            w = b.last_w
            if w is not None:
                deps[id(w)] = (w, True)
            for r in b.readers:
                if id(r) not in deps:
                    deps[id(r)] = (r, False)
        final = []
        for (d, strong) in deps.values():
            if d is op:
                continue
            if d.eng == eng and not d.dma and not dma:
                if eng == "pe":
                    continue
                if not strong:
                    continue
            if not d.dma and not dma and self.coarse:
                d = self.last_compute[d.eng]
            final.append(d)
            if not d.dma:
                d.needs_inc = True
        op.deps = list({id(x): x for x in final}.values())
        for b in reads:
            b.readers.append(op)
        for b in writes:
            b.last_w = op
            b.readers = []
        self.ops[eng].append(op)
        if not dma:
            self.last_compute[eng] = op
        return op

    def emit(self):
        nc = self.nc
        K, EPOCH = self.K, self.EPOCH
        nsem_c = {}
        ndma = {}
        for eng in self.ENGS:
            cnt = 0
            dcnt = 0
            for op in self.ops[eng]:
                if op.dma:
                    op.dn = dcnt
                    op.dsem = dcnt % K
                    op.dticket = 16 * (dcnt // K + 1)
                    dcnt += 1
                elif op.needs_inc:
                    op.ticket = (cnt // EPOCH, cnt % EPOCH + 1)
                    cnt += 1
            nsem_c[eng] = (cnt + EPOCH - 1) // EPOCH
            ndma[eng] = dcnt
        with ExitStack() as es:
            csem = {}
            dsem = {}
            for eng in self.ENGS:
                csem[eng] = [es.enter_context(nc.semaphore(f"c_{eng}_{i}")) for i in range(nsem_c[eng])]
                dsem[eng] = [es.enter_context(nc.semaphore(f"d_{eng}_{i}")) for i in range(min(K, ndma[eng]))]
            block = es.enter_context(nc.Block())

            self.sim = {en: [] for en in self.ENGS}

            def stream(eng, e):
                waited = {}
                maxep = {}
                cur = []

                def wait(sem, key, val):
                    if waited.get(key, 0) < val:
                        e.wait_ge(sem, val)
                        waited[key] = val
                        cur.append((key, val))

                for op in self.ops[eng]:
                    for d in op.deps:
                        if d.dma:
                            wait(dsem[d.eng][d.dsem], ("d", d.eng, d.dsem), d.dticket)
                        else:
                            ep, val = d.ticket
                            if maxep.get(d.eng, -1) > ep:
                                continue
                            maxep[d.eng] = ep
                            wait(csem[d.eng][ep], ("c", d.eng, ep), val)
                    if op.dma:
                        if op.dn >= K:
                            wait(dsem[eng][op.dsem], ("d", eng, op.dsem), op.dticket - 16)
                        ins = op.fn(e)
                        ins.then_inc(dsem[eng][op.dsem], 16)
                        self.sim[eng].append((list(cur), (("d", eng, op.dsem), 16), op.idx))
                    else:
                        ins = op.fn(e)
                        if op.needs_inc:
                            ins.then_inc(csem[eng][op.ticket[0]], 1)
                            self.sim[eng].append((list(cur), (("c", eng, op.ticket[0]), 1), op.idx))
                        else:
                            self.sim[eng].append((list(cur), None, op.idx))
                    del cur[:]
                n = ndma[eng]
                for s in range(min(K, n)):
                    last = ((n - 1 - s) // K) * K + s
                    wait(dsem[eng][s], ("d", eng, s), 16 * (last // K + 1))

            self.streams_done = False
            if self.ops["pe"]:
                @block.tensor
                def _(e):
                    stream("pe", e)
            if self.ops["act"]:
                @block.scalar
                def _(e):
                    stream("act", e)
            if self.ops["dve"]:
                @block.vector
                def _(e):
                    stream("dve", e)
            if self.ops["pool"]:
                @block.gpsimd
                def _(e):
                    stream("pool", e)
            if self.ops["sp"]:
                @block.sync
                def _(e):
                    stream("sp", e)


class K:
    def __init__(self, cfg, debug=False):
        self.c = derive(cfg)
        self.debug = debug
        self.nc = bass.Bass("TRN2", target_bir_lowering=False)
        self.P = Prog(self.nc)
        self.ins = {}
        self.outs = {}
        self.psi = 0

    def din(self, name, shape):
        self.ins[name] = self.nc.dram_tensor(name, list(shape), F32, kind="ExternalInput").ap()
        return self.ins[name]

    def dout(self, name, shape):
        self.outs[name] = self.nc.dram_tensor(name, list(shape), F32, kind="ExternalOutput").ap()
        return self.outs[name]

    def dscr(self, name, shape, dt=F32):
        return self.nc.dram_tensor(name, list(shape), dt, kind="Internal").ap()

    def sb(self, es, name, shape, dt=F32):
        self.uid = getattr(self, "uid", 0) + 1
        name = f"s{self.uid}_{name}"
        t = es.enter_context(self.nc.sbuf_tensor(name, list(shape), dt))
        return t, self.P.buf(name)

    def ps(self):
        i = self.psi % 8
        self.psi += 1
        return self.pst[i], self.psb[i]

    def dma(self, out, in_, reads=(), writes=(), q="sp"):
        return self.P.add(q, lambda e: e.dma_start(out=out, in_=in_), reads=reads, writes=writes, dma=True)

    def mm(self, out, lhsT, rhs, start, stop, reads, writes):
        return self.P.add("pe", lambda e: e.matmul(out, lhsT, rhs, start=start, stop=stop), reads=reads, writes=writes)

    def tr(self, out, in_, reads, writes):
        idn = self.ident
        return self.P.add("pe", lambda e: e.transpose(out, in_, idn[:]), reads=list(reads) + [self.identb], writes=writes)

    def act(self, out, in_, func, reads, writes, scale=None, bias=None, accum_out=None, eng="act"):
        kw = {}
        if scale is not None:
            kw["scale"] = scale
        if bias is not None:
            kw["bias"] = bias
        if accum_out is not None:
            kw["accum_out"] = accum_out
        return self.P.add("act", lambda e: e.activation(out=out, in_=in_, func=func, **kw), reads=reads, writes=writes)

    def tt(self, out, in0, in1, op, reads, writes, eng="dve"):
        return self.P.add(eng, lambda e: e.tensor_tensor(out=out, in0=in0, in1=in1, op=op), reads=reads, writes=writes)

    def ts(self, out, in0, s1, op0, reads, writes, s2=None, op1=None, eng="dve"):
        if op1 is None:
            return self.P.add(eng, lambda e: e.tensor_scalar(out=out, in0=in0, scalar1=s1, scalar2=None, op0=op0),
                              reads=reads, writes=writes)
        return self.P.add(eng, lambda e: e.tensor_scalar(out=out, in0=in0, scalar1=s1, scalar2=s2, op0=op0, op1=op1),
                          reads=reads, writes=writes)

    def stt(self, out, in0, scalar, in1, op0, op1, reads, writes):
        return self.P.add("dve", lambda e: e.scalar_tensor_tensor(out=out, in0=in0, scalar=scalar, in1=in1, op0=op0, op1=op1),
                          reads=reads, writes=writes)

    def cp(self, out, in_, reads, writes, eng="dve"):
        if eng == "act":
            return self.act(out, in_, AF.Copy, reads, writes)
        return self.P.add(eng, lambda e: e.tensor_copy(out, in_), reads=reads, writes=writes)

    def recip(self, out, in_, reads, writes):
        def f(e):
            with self.nc.allow_low_precision(reason="fp32r-rounded output feeding TensorE fp32r"):
                return e.reciprocal(out, in_)
        return self.P.add("dve", f, reads=reads, writes=writes)

    def red(self, out, in_, reads, writes, op=ALU.add):
        return self.P.add("dve", lambda e: e.tensor_reduce(out=out, in_=in_, axis=AX.X, op=op), reads=reads, writes=writes)

    def rsqrt_chain(self, out, in_, mult, eps, reads, writes):
        self.ts(out, in_, mult, ALU.mult, reads, writes, s2=eps, op1=ALU.add)
        self.act(out, out, AF.Sqrt, writes, writes)
        self.recip(out, out, writes, writes)


def bc(ap, shape):
    return ap.broadcast_to(list(shape))


def declare_io(k):
    c = k.c
    D, KC = c["D"], c["KC"]
    k.din("x", [c["NTOK"], D])
    k.din("cT", [128, KC, 2])
    k.din("ada_w", [2, D, 3 * D])
    k.din("ada_bT", [2, 128, 2 * KC])
    k.din("ada_bg", [2, 2, D])
    k.din("norm_gT", [2, 128, KC])
    k.din("final_norm_g", [D])
    k.din("ab_w_in", [D, c["IN_AB"]])
    k.din("ab_w_out", [D, D])
    k.din("cd_w_in", [D, c["IN_CD"]])
    k.din("cd_w_out", [D, D])
    k.din("ident", [128, 128])
    k.din("sel", [2, 2, 128])
    k.dout("y", [c["NTOK"], D])
    k.hT_scr = k.dscr("hT_scr", [len(c["TILES"]), 128, KC * 128], BF16)
    k.proj = k.dscr("proj", [c["NROW"], max(c["IN_AB"], c["IN_CD"])])
    k.y_scr = k.dscr("y_scr", [c["NTOK"], D])
    k.x1 = k.dscr("x1", [c["NTOK"], D])
    k.x2 = k.dscr("x2", [c["NTOK"], D])
    k.grow_scr = k.dscr("grow_scr", [2, 2, D])
    k.convT = k.dscr("convT", [max(3 * c["WB"], c["WD"] + 2 * c["WD_BC"]), c["NROW"] + 2])
    if k.debug:
        k.dout("dbg_proj", [c["NROW"], max(c["IN_AB"], c["IN_CD"])])
        k.dout("dbg_x1", [c["NTOK"], D])


def phase_mod(k, es):
    c, P = k.c, k.P
    D, KC = c["D"], c["KC"]
    PW = 256
    k.gs, k.gsb = k.sb(es, "gs", [128, 2, KC, 2])
    k.sh, k.shb = k.sb(es, "sh", [128, 2, KC, 2])
    R = lambda ap: ap.bitcast(F32R)
    with ExitStack() as ph:
        cT, cTb = k.sb(ph, "cTt", [128, KC, 2])
        sc, scb = k.sb(ph, "sc", [128, KC, 2])
        scp, scpb = k.sb(ph, "scp", [128, KC, 128])
        zr, zrb = k.sb(ph, "zr", [128, 128])
        bT, bTb = k.sb(ph, "bT", [128, 2, 2 * KC])
        ng, ngb = k.sb(ph, "ngT", [128, 2, KC])
        bgs = [k.sb(ph, f"bgs{i}", [2, PW]) for i in range(2)]
        grs = [k.sb(ph, f"grs{i}", [2, PW]) for i in range(2)]
        modT, modTb = k.sb(ph, "modT", [128, 2 * KC, 2])
        wp = [k.sb(ph, f"wp{i}", [128, KC, PW]) for i in range(2)]
        wr = [k.sb(ph, f"wr{i}", [128, KC, PW]) for i in range(2)]
        k.dma(cT[:], k.ins["cT"][:, :, :], writes=[cTb])
        k.dma(bT[:], k.ins["ada_bT"].rearrange("l p n -> p l n"), writes=[bTb])
        k.dma(ng[:], k.ins["norm_gT"].rearrange("l p n -> p l n"), writes=[ngb])
        k.act(R(sc[:]), cT[:], AF.Silu, [cTb], [scb])
        k.P.add("dve", lambda e: e.memset(zr[:], 0.0), writes=[zrb])
        k.cp(R(scp[:]), zr[:, :].unsqueeze(1).broadcast_to([128, KC, 128]), [zrb], [scpb])
        k.act(R(scp[:, :, 0:2]), cT[:], AF.Silu, [cTb, scpb], [scpb])
        npan = 3 * D // PW
        nss = 2 * D // PW
        pi = 0
        for l in range(2):
            pm, pmb = k.ps()
            for pn in range(npan):
                w, wb = wp[pi % 2]
                wq, wqb = wr[pi % 2]
                pi += 1
                src = k.ins["ada_w"][l, :, pn * PW:(pn + 1) * PW].rearrange("(kc p) n -> p kc n", p=128)
                k.dma(w[:], src, writes=[wb])
                k.cp(R(wq[:]), w[:], [wb], [wqb], eng="act" if pi % 2 else "dve")
                if pn < nss:
                    for n2 in range(PW // 128):
                        n = pn * (PW // 128) + n2
                        for kc in range(KC):
                            k.mm(pm[:, 2 * n:2 * n + 2], R(wq[:, kc, n2 * 128:(n2 + 1) * 128]), R(sc[:, kc, :]),
                                 kc == 0, kc == KC - 1, [wqb, scb], [pmb])
                if pn == nss - 1:
                    k.tt(modT[:], pm[:, 0:4 * KC].rearrange("p (n v) -> p n v", v=2),
                         bT[:, l, :].unsqueeze(2).broadcast_to([128, 2 * KC, 2]), ALU.add, [pmb, bTb], [modTb])
                    k.cp(k.sh[:, l], modT[:, 0:KC, :], [modTb], [k.shb])
                    k.stt(k.gs[:, l], modT[:, KC:2 * KC, :], 1.0, ng[:, l, :].unsqueeze(2).broadcast_to([128, KC, 2]),
                          ALU.add, ALU.mult, [modTb, ngb], [k.gsb])
                if pn >= nss:
                    pg, pgb = k.ps()
                    for kc in range(KC):
                        k.mm(pg[:, 0:PW], R(scp[:, kc, :]), R(wq[:, kc, :]), kc == 0, kc == KC - 1, [wqb, scpb], [pgb])
                    c0 = pn * PW - 2 * D
                    bgt, bgtb = bgs[pn % 2]
                    grt, grtb = grs[pn % 2]
                    k.dma(bgt[:], k.ins["ada_bg"][l, :, c0:c0 + PW], writes=[bgtb])
                    k.tt(grt[:], pg[0:2, 0:PW], bgt[:], ALU.add, [pgb, bgtb], [grtb])
                    k.dma(k.grow_scr[l, :, c0:c0 + PW], grt[:], reads=[grtb], writes=[k.growsb])
    P.barrier()


def phase_T(k, src, layer, norm):
    c, P = k.c, k.P
    D, KC = c["D"], c["KC"]
    with ExitStack() as ph:
        xa = [k.sb(ph, f"xa{i}", [128, D]) for i in range(2)]
        xn, xnb = k.sb(ph, "xn", [128, D])
        hT = [k.sb(ph, f"hTt{i}", [128, KC, 128], BF16) for i in range(2)]
        ss, ssb = k.sb(ph, "ss", [128, 1])
        for ti, (si, j, t0, r0) in enumerate(c["TILES"]):
            v = 0 if c["SEQS"][si][3] else 1
            x, xb = xa[ti % 2]
            h, hb = hT[ti % 2]
            k.dma(x[:], src[t0:t0 + 128, :], writes=[xb])
            if norm:
                k.act(xn[:], x[:], AF.Square, [xb], [xnb, ssb], accum_out=ss[:, 0:1])
                k.rsqrt_chain(ss[:], ss[:], 1.0 / D, 1e-6, [ssb], [ssb])
                k.act(xn[:], x[:], AF.Copy, [xb, ssb], [xnb], scale=ss[:, 0:1])
                s_, sb_ = xn, xnb
            else:
                s_, sb_ = x, xb
            for q in range(KC // 4):
                pt, ptb = k.ps()
                for r in range(4):
                    kc = q * 4 + r
                    k.tr(pt[:, r * 128:(r + 1) * 128], s_[:, kc * 128:(kc + 1) * 128], [sb_], [ptb])
                if norm:
                    for r in range(4):
                        kc = q * 4 + r
                        if r % 2 == 0:
                            k.act(h[:, kc, :], pt[:, r * 128:(r + 1) * 128], AF.Identity, [ptb, k.gsb, k.shb], [hb],
                                  scale=k.gs[:, layer, kc, v:v + 1], bias=k.sh[:, layer, kc, v:v + 1])
                        else:
                            k.ts(h[:, kc, :], pt[:, r * 128:(r + 1) * 128], k.gs[:, layer, kc, v:v + 1], ALU.mult,
                                 [ptb, k.gsb, k.shb], [hb], s2=k.sh[:, layer, kc, v:v + 1], op1=ALU.add)
                else:
                    if q % 2 == 0:
                        k.cp(h[:, q * 4:q * 4 + 4, :], pt[:].rearrange("p (a b) -> p a b", b=128), [ptb], [hb], eng="act")
                    else:
                        k.cp(h[:, q * 4:q * 4 + 4, :], pt[:].rearrange("p (a b) -> p a b", b=128), [ptb], [hb])
            k.dma(k.hT_scr[ti], h[:].rearrange("p a b -> p (a b)"), reads=[hb], writes=[k.hTsb])
    P.barrier()


def phase_P(k, W, panels, evac):
    c, P = k.c, k.P
    D, KC = c["D"], c["KC"]
    tiles = c["TILES"]
    NG = 10
    with ExitStack() as ph:
        ng = min(NG, len(tiles))
        hTg, _ = k.sb(ph, "hTg", [128, ng, KC * 128], BF16)
        hTb = [P.buf(f"hT{i}") for i in range(ng)]
        wp = [k.sb(ph, f"wpb{i}", [128, KC, 512], BF16) for i in range(2)]
        k.stg = [k.sb(ph, f"stg{i}", [128, 512]) for i in range(4)]
        k.xr = [k.sb(ph, f"xr{i}", [128, 512]) for i in range(2)]
        k.stgi = 0
        pi = 0
        for g0 in range(0, len(tiles), NG):
            grp = list(range(g0, min(g0 + NG, len(tiles))))
            for gi, ti in enumerate(grp):
                k.dma(hTg[:, gi, :], k.hT_scr[ti], reads=[k.hTsb], writes=[hTb[gi]])
            runs = []
            for gi, ti in enumerate(grp):
                si, j = tiles[ti][0], tiles[ti][1]
                if runs and runs[-1][2] == si and runs[-1][1] - runs[-1][0] < 4 and tiles[grp[runs[-1][1] - 1]][1] == j - 1:
                    runs[-1][1] += 1
                else:
                    runs.append([gi, gi + 1, si])
            for pan in panels:
                c0, w = pan[0], pan[1]
                fm = len(pan) > 2
                wt, wb = wp[pi % 2]
                pi += 1
                k.dma(wt[:, :, 0:w], W[:, c0:c0 + w].rearrange("(kc p) n -> p kc n", p=128), writes=[wb], q="pool")
                if not fm:
                    for gi, ti in enumerate(grp):
                        pt, ptb = k.ps()
                        for kc in range(KC):
                            k.mm(pt[:, 0:w], hTg[:, gi, kc * 128:(kc + 1) * 128], wt[:, kc, 0:w], kc == 0, kc == KC - 1,
                                 [hTb[gi], wb], [ptb])
                        evac(ti, tiles[ti], c0, w, pt, ptb)
                else:
                    for (ga, gb_, si) in runs:
                        n = gb_ - ga
                        r0 = tiles[grp[ga]][3]
                        for bk in range(w // 128):
                            pt, ptb = k.ps()
                            for kc in range(KC):
                                k.mm(pt[:, 0:n * 128].rearrange("p (a b) -> p a b", b=128),
                                     wt[:, kc, bk * 128:(bk + 1) * 128], hTg[:, ga:gb_, kc * 128:(kc + 1) * 128],
                                     kc == 0, kc == KC - 1, [hTb[x] for x in range(ga, gb_)] + [wb], [ptb])
                            sg, sgb = k.stg[k.stgi % 4]
                            k.stgi += 1
                            k.cp(sg[:, 0:n * 128], pt[:, 0:n * 128], [ptb], [sgb], eng="act" if k.stgi % 2 else "dve")
                            ch0 = c0 - pan[2] + bk * 128
                            k.dma(k.convT[ch0:ch0 + 128, r0:r0 + n * 128], sg[:, 0:n * 128], reads=[sgb], writes=[k.convTb])
    P.barrier()


def evac_proj(k):
    def f(ti, tinfo, c0, w, pt, ptb):
        si, j, t0, r0 = tinfo
        s, sbuf = k.stg[k.stgi % 4]
        k.stgi += 1
        if k.stgi % 2 == 0:
            k.cp(s[:, 0:w], pt[:, 0:w], [ptb], [sbuf], eng="act")
        else:
            k.cp(s[:, 0:w], pt[:, 0:w], [ptb], [sbuf])
        k.dma(k.proj[r0:r0 + 128, c0:c0 + w], s[:, 0:w], reads=[sbuf], writes=[k.projb])
    return f


def evac_fm(k, tinfo, ch0, w, pt, ptb):
    si, j, t0, r0 = tinfo
    s, sbuf = k.stg[k.stgi % 4]
    k.stgi += 1
    if k.stgi % 2 == 0:
        k.cp(s[:, 0:w], pt[:, 0:w], [ptb], [sbuf], eng="act")
    else:
        k.cp(s[:, 0:w], pt[:, 0:w], [ptb], [sbuf])
    k.dma(k.convT[ch0:ch0 + w, r0:r0 + 128].rearrange("(b p) t -> p b t", p=128),
          s[:, 0:w].rearrange("p (b t) -> p b t", t=128), reads=[sbuf], writes=[k.convTb])


def evac_out(k, layer, xres, xnext):
    def f(ti, tinfo, c0, w, pt, ptb):
        si, j, t0, r0 = tinfo
        v = 0 if k.c["SEQS"][si][3] else 1
        s, sbuf = k.stg[k.stgi % 4]
        xr, xrb = k.xr[k.stgi % 2]
        k.stgi += 1
        k.dma(xr[:, 0:w], xres[t0:t0 + 128, c0:c0 + w], reads=[k.xresb], writes=[xrb])
        k.tt(s[:, 0:w], pt[:, 0:w], k.gateb[:, v, c0:c0 + w], ALU.mult, [ptb, k.gatebb], [sbuf])
        k.tt(s[:, 0:w], s[:, 0:w], xr[:, 0:w], ALU.add, [sbuf, xrb], [sbuf])
        k.dma(xnext[t0:t0 + 128, c0:c0 + w], s[:, 0:w], reads=[sbuf], writes=[k.xnextb])
    return f


def phase_gate(k, es, layer):
    for v in range(2):
        k.dma(k.gateb[:, v, :], k.grow_scr[layer, v].partition_broadcast(128), reads=[k.growsb], writes=[k.gatebb])


def phase_final(k, src):
    c, P = k.c, k.P
    D = c["D"]
    with ExitStack() as ph:
        xa = [k.sb(ph, f"fxa{i}", [128, D]) for i in range(2)]
        xo = [k.sb(ph, f"fxo{i}", [128, D]) for i in range(2)]
        g, gb = k.sb(ph, "fg", [128, D])
        ss, ssb = k.sb(ph, "fss", [128, 1])
        k.dma(g[:], k.ins["final_norm_g"].partition_broadcast(128), writes=[gb])
        for ti, (si, j, t0, r0) in enumerate(c["TILES"]):
            x, xb = xa[ti % 2]
            o, ob = xo[ti % 2]
            k.dma(x[:], src[t0:t0 + 128, :], reads=[k.xsrcb], writes=[xb])
            k.act(o[:], x[:], AF.Square, [xb], [ob, ssb], accum_out=ss[:, 0:1])
            k.rsqrt_chain(ss[:], ss[:], 1.0 / D, 1e-6, [ssb], [ssb])
            k.stt(o[:], x[:], ss[:, 0:1], g[:], ALU.mult, ALU.mult, [xb, ssb, gb], [ob])
            k.dma(k.outs["y"][t0:t0 + 128, :], o[:], reads=[ob])
    P.barrier()


def panels_of(ranges):
    out = []
    for (a, b) in ranges:
        c0 = a
        while c0 < b:
            w = min(512, b - c0)
            out.append((c0, w))
            c0 += w
    return out


def build(cfg, debug=False, stop_after=None, enabled=("gdn", "ssd"), only_layer=None):
    k = K(cfg, debug)
    c, P = k.c, k.P
    D = c["D"]
    declare_io(k)
    declare_mixer_io(k)
    k.enabled = enabled
    es = ExitStack()
    with es:
        P.bar_tile, _ = k.sb(es, "bar", [128, 8])
        k.pst, k.psb = [], []
        for i in range(8):
            t = es.enter_context(k.nc.psum_tensor(f"ps{i}", [128, 512], F32))
            k.pst.append(t)
            k.psb.append(P.buf(f"ps{i}"))
        k.ident, k.identb = k.sb(es, "ident", [128, 128])
        k.sel, k.selb = k.sb(es, "sel", [2, 2, 128])
        k.dma(k.ident[:], k.ins["ident"][:, :], writes=[k.identb])
        k.dma(k.sel[:], k.ins["sel"][:, :, :], writes=[k.selb])
        k.hTsb = P.buf("hTs")
        k.projb = P.buf("proj")
        k.yscrb = P.buf("yscr")
        k.x1b = P.buf("x1")
        k.x2b = P.buf("x2")
        k.xinb = P.buf("xin")
        k.ofsb = P.buf("ofs")
        k.growsb = P.buf("grows")
        k.convTb = P.buf("convT")
        phase_mod(k, es)
        srcs = [k.ins["x"], k.x1, k.x2]
        srcb = [k.xinb, k.x1b, k.x2b]
        if only_layer is not None:
            srcs[only_layer] = k.ins["x"]
            srcb[only_layer] = k.xinb
        for layer in ([only_layer] if only_layer is not None else range(2)):
            W_in = k.ins["ab_w_in"] if layer == 0 else k.ins["cd_w_in"]
            W_out = k.ins["ab_w_out"] if layer == 0 else k.ins["cd_w_out"]
            ncol = c["IN_AB"] if layer == 0 else c["IN_CD"]
            phase_T(k, srcs[layer], layer, True)
            off = c["AB_OFF"] if layer == 0 else c["CD_OFF"]
            fm0, fm1 = (off[6], off[7]) if layer == 0 else (off[4], off[5])
            pans = panels_of([(0, fm0)]) + [(a, w, fm0) for (a, w) in panels_of([(fm0, fm1)])] + panels_of([(fm1, ncol)])
            phase_P(k, W_in, pans, evac_proj(k))
            if stop_after == ("proj", layer):
                break
            mixers(k, layer)
            if stop_after == ("mix", layer):
                with ExitStack() as ph:
                    t, tb = k.sb(ph, "dbgy", [128, D])
                    for r0 in range(0, c["NTOK"], 128):
                        k.dma(t[:], k.y_scr[r0:r0 + 128, :], reads=[k.yscrb], writes=[tb])
                        k.dma(k.outs["dbg_y"][r0:r0 + 128, :], t[:], reads=[tb])
                break
            phase_T(k, k.y_scr, layer, False)
            with ExitStack() as lay:
                k.gateb, k.gatebb = k.sb(lay, "gateb", [128, 2, D])
                phase_gate(k, lay, layer)
                k.xresb, k.xnextb = srcb[layer], srcb[layer + 1]
                phase_P(k, W_out, panels_of([(0, D)]), evac_out(k, layer, srcs[layer], srcs[layer + 1]))
        if stop_after is None:
            k.xsrcb = k.x2b
            phase_final(k, k.x2)
        if debug:
            with ExitStack() as ph:
                t, tb = k.sb(ph, "dbgt", [128, 4096])
                ncol = max(c["IN_AB"], c["IN_CD"])
                for r0 in range(0, c["NROW"], 128):
                    nr = min(128, c["NROW"] - r0)
                    for c0 in range(0, ncol, 4096):
                        w = min(4096, ncol - c0)
                        k.dma(t[0:nr, 0:w], k.proj[r0:r0 + nr, c0:c0 + w], reads=[k.projb], writes=[tb])
                        k.dma(k.outs["dbg_proj"][r0:r0 + nr, c0:c0 + w], t[0:nr, 0:w], reads=[tb])
                for r0 in range(0, c["NTOK"], 128):
                    k.dma(t[:, 0:D], k.x1[r0:r0 + 128, :], reads=[k.x1b], writes=[tb])
                    k.dma(k.outs["dbg_x1"][r0:r0 + 128, :], t[:, 0:D], reads=[tb])
        P.emit()
    return k


def fT(v, KC):
    v = np.asarray(v, np.float32)
    return np.ascontiguousarray(np.swapaxes(v.reshape(v.shape[:-1] + (KC, 128)), -1, -2))


def consts(c):
    out = {}
    out["ident"] = np.eye(128, dtype=np.float32)
    sel = np.zeros((2, 2, 128), np.float32)
    sel[0, 0, :] = 1.0
    sel[1, 1, :] = 1.0
    out["sel"] = sel
    return out


def core_inputs(c, inp, core, shared):
    D, KC, NP = c["D"], c["KC"], c["NP"]
    m = dict(shared)
    xs = [np.asarray(inp["x_sample"][core], np.float32)]
    for i in range(NP):
        xs.append(np.asarray(inp["x_prompt"][core * NP + i], np.float32))
    m["x"] = np.ascontiguousarray(np.concatenate(xs, axis=0))
    cv = np.stack([np.asarray(inp["c"][core], np.float32), np.asarray(inp["c_ctx"], np.float32)], axis=0)
    m["cT"] = np.ascontiguousarray(np.transpose(cv.reshape(2, KC, 128), (2, 1, 0)))
    return m


def shared_inputs(c, inp):
    D, KC = c["D"], c["KC"]
    m = consts(c)
    f = lambda a: np.ascontiguousarray(np.asarray(a, np.float32))
    m["ada_w"] = f(inp["ada_w"])
    ada_b = f(inp["ada_b"])
    m["ada_bT"] = fT(ada_b[:, :2 * D], 2 * KC)
    m["ada_bg"] = np.ascontiguousarray(np.repeat(ada_b[:, None, 2 * D:], 2, axis=1))
    m["norm_gT"] = fT(f(inp["norm_g"]), KC)
    m["final_norm_g"] = f(inp["final_norm_g"])
    m["ab_w_in"] = f(inp["ab_w_in"][0])
    m["ab_w_out"] = f(inp["ab_w_out"][0])
    m["cd_w_in"] = f(inp["cd_w_in"][0])
    m["cd_w_out"] = f(inp["cd_w_out"][0])
    return m


def declare_mixer_io(k):
    c = k.c
    HA, HB, HC, HD, G = c["H_A"], c["H_B"], c["H_C"], c["H_D"], c["G_D"]
    NP = c["NP"]
    for n, s in [("CN", [2, 128, 128]), ("CT", [2, 128, 128]), ("MASK", [2, 128, 128]), ("neg16", [128, 1]),
                 ("ones_row", [1, 128]), ("ones", [128, 128]),
                 ("gla_w2", [2, 16, HA * 128]), ("gla_b", [2, 1, HA * 128]), ("gla_ng", [256]),
                 ("st_gla", [2, HA, 128, 256]), ("st_gdn", [2, HB, 128, 128]),
                 ("st_ret", [2, HC, 128, 256]), ("st_ssd", [2, HD, 128, 64]),
                 ("ret_eg", [2, 128, HC * 128]), ("ret_eng", [2, 128, HC * 128]), ("ret_ekd", [2, 128, HC * 128]),
                 ("ret_egt", [2, 128, HC]), ("ret_ng", [256]), ("ret_nb", [256]),
                 ("rope_cos", [c["Ts"], 64]), ("rope_sin", [c["Ts"], 64]),
                 ("MSK1", [2, 128, 128]), ("NEGM", [2, 128, 128]), ("NEGS", [2, 128, 128]), ("BM", [2, 7, 128, 128]),
                 ("ssd_cw", [128, (c["WD"] + 2 * c["WD_BC"]) // 128, 3]), ("ssd_cb", [128, (c["WD"] + 2 * c["WD_BC"]) // 128]),
                 ("ssd_alog", [2, HD]), ("ssd_dtb", [2, HD]), ("ssd_d", [HD]), ("ssd_ng", [c["WD"]]),
                 ("gdn_cw", [128, 3 * HB, 3]), ("gdn_alog", [2, HB]), ("gdn_dtb", [2, HB]), ("gdn_ng", [128])]:
        k.din(n, s)
    k.dout("o_gla", [NP, 2, HA, 128, 256])
    k.dout("o_gdn", [NP, 2, HB, 128, 128])
    k.dout("o_ret", [NP, 2, HC, 128, 256])
    k.dout("o_ssd", [NP, 2, HD, 128, 64])
    k.of_scr = k.dscr("of_scr", [c["NTOK"], c["D"]])
    if k.debug:
        k.dout("dbg_y", [c["NTOK"], c["D"]])
        for nm in ("PT", "L", "DTs", "LNe"):
            k.dout("dbg_" + nm, [128, 128])


def load_const(k, es, name, shape, src):
    t, b = k.sb(es, name, shape)
    k.dma(t[:], src, writes=[b])
    return t, b


def mixer_gla(k, kind, si, d):
    c, P = k.c, k.P
    gla = kind == "gla"
    H = c["H_A"] if gla else c["H_C"]
    off = c["AB_OFF"] if gla else c["CD_OFF"]
    HK, HV = H * 128, H * 256
    (t0s, r0s, T, smp, idx) = c["SEQS"][si]
    nch = T // 128
    qscale = 128 ** -0.5 if gla else 1.0
    kscale = 1.0 if gla else 128 ** -0.5
    ycol0 = 0
    st_in = k.ins["st_gla" if gla else "st_ret"]
    st_out = k.outs["o_gla" if gla else "o_ret"]
    with ExitStack() as ph:
        S, _ = k.sb(ph, "S", [128, H, 256])
        Sb = [P.buf(f"S{h}") for h in range(H)]
        CN = load_const(k, ph, "CN", [128, 128], k.ins["CN"][d])
        CT = load_const(k, ph, "CT", [128, 128], k.ins["CT"][d])
        MK = load_const(k, ph, "MK", [128, 128], k.ins["MASK"][d])
        n16 = load_const(k, ph, "n16", [128, 1], k.ins["neg16"][:, :])
        onr = load_const(k, ph, "onr", [1, 128], k.ins["ones_row"][:, :])
        ngt = load_const(k, ph, "ngt", [128, 256], k.ins["gla_ng" if gla else "ret_ng"].partition_broadcast(128))
        if gla:
            w2 = load_const(k, ph, "w2", [16, HK], k.ins["gla_w2"][d])
            gb = load_const(k, ph, "gb", [1, HK], k.ins["gla_b"][d])
        else:
            nbt = load_const(k, ph, "nbt", [128, 256], k.ins["ret_nb"].partition_broadcast(128))
            eg = load_const(k, ph, "eg", [128, HK], k.ins["ret_eg"][d])
            eng = load_const(k, ph, "eng", [128, HK], k.ins["ret_eng"][d])
            ekd = load_const(k, ph, "ekd", [128, HK], k.ins["ret_ekd"][d])
            egt = load_const(k, ph, "egt", [128, H], k.ins["ret_egt"][d])
        if smp:
            for h in range(H):
                k.dma(S[:, h, :], st_in[d, h], writes=[Sb[h]])
        else:
            k.P.add("dve", lambda e: e.memset(S[:], 0.0), writes=Sb)
        q, qb = k.sb(ph, "q", [128, HK])
        kk, kb = k.sb(ph, "k", [128, HK])
        v, vb = k.sb(ph, "v", [128, HV])
        z, zb = k.sb(ph, "z", [128, HV])
        ra, rab = k.sb(ph, "ra", [128, 16])
        raT, raTb = k.sb(ph, "raT", [16, 128])
        e1, e1b = k.sb(ph, "e1", [128, HK])
        sp, spb = k.sb(ph, "sp", [128, HK])
        if gla:
            eg = k.sb(ph, "eg", [128, HK])
            eng = k.sb(ph, "eng", [128, HK])
            ekd = k.sb(ph, "ekd", [128, HK])
            egt = k.sb(ph, "egt", [128, H])
        qs, qsb = k.sb(ph, "qs", [128, HK])
        ks, ksb = k.sb(ph, "ks", [128, HK])
        kh, khb = k.sb(ph, "kh", [128, HK])
        qT, qTb = k.sb(ph, "qT", [128, H, 128])
        kT, kTb = k.sb(ph, "kT", [128, H, 128])
        PT = [k.sb(ph, f"PT{i}", [128, 128]) for i in range(H)]
        oa, oab = k.sb(ph, "oa", [128, HV])
        of, ofb = k.sb(ph, "of", [128, HV])
        jk, jkb = k.sb(ph, "jk", [128, HV])
        st, stb = k.sb(ph, "st", [128, H])
        mu, mub = k.sb(ph, "mu", [128, H])
        cs, csb = k.sb(ph, "cs", [128, 64])
        sn, snb = k.sb(ph, "sn", [128, 64])
        ta, tab = k.sb(ph, "ta", [128, H * 64])
        tb_, tbb = k.sb(ph, "tb", [128, H * 64])
        order = list(range(nch)) if d == 0 else list(range(nch - 1, -1, -1))
        for ci in order:
            r0 = r0s + ci * 128
            t0 = t0s + ci * 128
            pos0 = ci * 128
            k.dma(q[:], k.proj[r0:r0 + 128, off[0]:off[0] + HK], reads=[k.projb], writes=[qb])
            k.dma(kk[:], k.proj[r0:r0 + 128, off[1]:off[1] + HK], reads=[k.projb], writes=[kb])
            k.dma(v[:], k.proj[r0:r0 + 128, off[2]:off[2] + HV], reads=[k.projb], writes=[vb])
            if gla:
                k.dma(ra[:], k.proj[r0:r0 + 128, off[4 + d]:off[4 + d] + 16], reads=[k.projb], writes=[rab])
                pt, ptb = k.ps()
                k.tr(pt[0:16, 0:128], ra[:, 0:16], [rab], [ptb])
                k.cp(raT[:], pt[0:16, 0:128], [ptb], [raTb])
                for hf in range(HK // 512 if HK >= 512 else 1):
                    wd = min(512, HK)
                    sl = slice(hf * wd, (hf + 1) * wd)
                    pl, plb = k.ps()
                    k.mm(pl[:, 0:wd], raT[:, :], w2[0][:, sl], True, False, [raTb, w2[1]], [plb])
                    k.mm(pl[:, 0:wd], onr[0][:, :], gb[0][:, sl], False, True, [onr[1], gb[1]], [plb])
                    k.act(e1[:, sl], pl[:, 0:wd], AF.Exp, [plb], [e1b], scale=-1.0)
                    k.act(sp[:, sl], e1[:, sl], AF.Ln, [e1b], [spb], bias=1.0)
                    pg, pgb = k.ps()
                    k.mm(pg[:, 0:wd], CN[0][:, :], sp[:, sl], True, True, [CN[1], spb], [pgb])
                    k.act(eg[0][:, sl], pg[:, 0:wd], AF.Exp, [pgb], [eg[1]])
                    k.act(eng[0][:, sl], pg[:, 0:wd], AF.Exp, [pgb], [eng[1]], scale=-1.0)
                    pk, pkb = k.ps()
                    k.mm(pk[:, 0:wd], CT[0][:, :], sp[:, sl], True, True, [CT[1], spb], [pkb])
                    k.act(ekd[0][:, sl], pk[:, 0:wd], AF.Exp, [pkb], [ekd[1]])
                pe, peb = k.ps()
                for h in range(H):
                    k.mm(pe[:, 2 * h:2 * h + 1], sp[:, h * 128:(h + 1) * 128], n16[0][:, 0:1], True, True, [spb, n16[1]], [peb])
                k.act(egt[0][:, :], pe[:, 0:2 * H].rearrange("p (h two) -> p h two", two=2)[:, :, 0], AF.Exp, [peb], [egt[1]])
            elif smp:
                k.dma(cs[:], k.ins["rope_cos"][pos0:pos0 + 128, :], writes=[csb])
                k.dma(sn[:], k.ins["rope_sin"][pos0:pos0 + 128, :], writes=[snb])
                csB = cs[:, :].unsqueeze(1).broadcast_to([128, H, 64])
                snB = sn[:, :].unsqueeze(1).broadcast_to([128, H, 64])
                for (xt, xbuf) in ((q, qb), (kk, kb)):
                    xv = xt[:].rearrange("p (h f two) -> p h f two", h=H, two=2)
                    x1, x2 = xv[:, :, :, 0], xv[:, :, :, 1]
                    tav = ta[:].rearrange("p (h f) -> p h f", h=H)
                    tbv = tb_[:].rearrange("p (h f) -> p h f", h=H)
                    k.tt(tav, x1, snB, ALU.mult, [xbuf, snb], [tab])
                    k.tt(tbv, x2, snB, ALU.mult, [xbuf, snb], [tbb])
                    k.tt(x1, x1, csB, ALU.mult, [xbuf, csb], [xbuf])
                    k.tt(x2, x2, csB, ALU.mult, [xbuf, csb], [xbuf])
                    k.tt(x1, x1, tbv, ALU.subtract, [xbuf, tbb], [xbuf])
                    k.tt(x2, x2, tav, ALU.add, [xbuf, tab], [xbuf])
            k.stt(qs[:], q[:], qscale, eg[0][:, :], ALU.mult, ALU.mult, [qb, eg[1]], [qsb])
            k.stt(ks[:], kk[:], kscale, eng[0][:, :], ALU.mult, ALU.mult, [kb, eng[1]], [ksb])
            k.stt(kh[:], kk[:], kscale, ekd[0][:, :], ALU.mult, ALU.mult, [kb, ekd[1]], [khb])
            for (src, srcb, dst, dstb) in ((qs, qsb, qT, qTb), (ks, ksb, kT, kTb)):
                for h0 in range(0, H, 4):
                    nh = min(4, H - h0)
                    pt, ptb = k.ps()
                    for r in range(nh):
                        k.tr(pt[:, r * 128:(r + 1) * 128], src[:, (h0 + r) * 128:(h0 + r + 1) * 128], [srcb], [ptb])
                    k.cp(dst[:, h0:h0 + nh, :], pt[:, 0:nh * 128].rearrange("p (a b) -> p a b", b=128), [ptb], [dstb],
                         eng="act")
            for h in range(H):
                pa, pab = k.ps()
                k.mm(pa[:, 0:128], kT[:, h, :], qT[:, h, :], True, True, [kTb, qTb], [pab])
                P_, Pb_ = PT[h]
                k.tt(P_[:], pa[:, 0:128], MK[0][:, :], ALU.mult, [pab, MK[1]], [Pb_])
            if d == 1:
                k.dma(of[:], k.of_scr[t0:t0 + 128, ycol0:ycol0 + HV], reads=[k.ofsb], writes=[ofb])
            for h in range(H):
                P_, Pb_ = PT[h]
                if h % 2 == 0:
                    po, pob = k.ps()
                osl = slice((h % 2) * 256, (h % 2) * 256 + 256)
                k.mm(po[:, osl], P_[:], v[:, h * 256:(h + 1) * 256], True, False, [Pb_, vb], [pob])
                k.mm(po[:, osl], qT[:, h, :], S[:, h, :], False, True, [qTb, Sb[h]], [pob])
                pS, pSb = k.ps()
                k.mm(pS[:, 0:256], kh[:, h * 128:(h + 1) * 128], v[:, h * 256:(h + 1) * 256], True, True, [khb, vb], [pSb])
                k.stt(S[:, h, :], S[:, h, :], egt[0][:, h:h + 1], pS[:, 0:256], ALU.mult, ALU.add,
                      [Sb[h], egt[1], pSb], [Sb[h]])
                if h % 2 == 1 or h == H - 1:
                    hh0 = h - (h % 2)
                    wd = (h - hh0 + 1) * 256
                    if d == 0:
                        k.cp(oa[:, hh0 * 256:hh0 * 256 + wd], po[:, 0:wd], [pob], [oab], eng="act")
                    else:
                        k.tt(oa[:, hh0 * 256:hh0 * 256 + wd], po[:, 0:wd], of[:, hh0 * 256:hh0 * 256 + wd], ALU.add,
                             [pob, ofb], [oab])
            if d == 0:
                k.dma(k.of_scr[t0:t0 + 128, ycol0:ycol0 + HV], oa[:], reads=[oab], writes=[k.ofsb])
            else:
                k.dma(z[:], k.proj[r0:r0 + 128, off[3]:off[3] + HV], reads=[k.projb], writes=[zb])
                oav = oa[:].rearrange("p (h v) -> p h v", h=H)
                jkv = jk[:].rearrange("p (h v) -> p h v", h=H)
                if not gla:
                    k.red(mu[:], oav, [oab], [mub])
                    k.ts(mu[:], mu[:], 1.0 / 256, ALU.mult, [mub], [mub])
                    k.tt(oav, oav, mu[:, :].unsqueeze(2).broadcast_to([128, H, 256]), ALU.subtract, [oab, mub], [oab])
                k.act(jk[:], oa[:], AF.Square, [oab], [jkb])
                k.red(st[:], jkv, [jkb], [stb])
                k.rsqrt_chain(st[:], st[:], 1.0 / 256, 1e-6 if gla else 1e-5, [stb], [stb])
                k.tt(oav, oav, st[:, :].unsqueeze(2).broadcast_to([128, H, 256]), ALU.mult, [oab, stb], [oab])
                k.tt(oav, oav, ngt[0][:, :].unsqueeze(1).broadcast_to([128, H, 256]), ALU.mult, [oab, ngt[1]], [oab])
                if not gla:
                    k.tt(oav, oav, nbt[0][:, :].unsqueeze(1).broadcast_to([128, H, 256]), ALU.add, [oab, nbt[1]], [oab])
                k.act(jk[:], z[:], AF.Silu, [zb], [jkb])
                k.tt(oa[:], oa[:], jk[:], ALU.mult, [oab, jkb], [oab])
                k.dma(k.y_scr[t0:t0 + 128, ycol0:ycol0 + HV], oa[:], reads=[oab], writes=[k.yscrb])
        if not smp:
            for h in range(H):
                k.dma(st_out[idx, d, h], S[:, h, :], reads=[Sb[h]])
    P.barrier()


def mixers(k, layer):
    c = k.c
    for si in range(len(c["SEQS"])):
        for d in (0, 1):
            if layer == 0:
                mixer_gla(k, "gla", si, d)
            else:
                mixer_gla(k, "ret", si, d)
    for si in range(len(c["SEQS"])):
        for d in (0, 1):
            if layer == 0:
                if "gdn" in k.enabled:
                    mixer_gdn(k, si, d)
            else:
                if "ssd" in k.enabled:
                    mixer_ssd(k, si, d)


def mixer_consts(c):
    m = {}
    t = np.arange(128)[:, None]
    i = np.arange(128)[None, :]
    le, ge, lt, gt = (t <= i), (t >= i), (t < i), (t > i)
    m["CN"] = np.stack([le, ge]).astype(np.float32) * (-1.0 / 16)
    m["CT"] = np.stack([gt, lt]).astype(np.float32) * (-1.0 / 16)
    m["MASK"] = np.stack([le, ge]).astype(np.float32)
    m["MSK1"] = np.stack([gt, lt]).astype(np.float32)
    m["NEGM"] = np.stack([np.where(le, 0.0, -30000.0), np.where(ge, 0.0, -30000.0)]).astype(np.float32)
    m["NEGS"] = np.stack([np.where(gt, 0.0, -30000.0), np.where(lt, 0.0, -30000.0)]).astype(np.float32)
    ii = np.arange(128)[:, None]
    jj = np.arange(128)[None, :]
    bm = []
    for n in range(1, 8):
        sz = 2 ** n
        lo = ((ii // sz) == (jj // sz)) & ((ii % sz) >= sz // 2) & ((jj % sz) < sz // 2)
        bm.append(lo)
    bm = np.stack(bm).astype(np.float32)
    m["BM"] = np.stack([bm, np.transpose(bm, (0, 2, 1))]).astype(np.float32)
    m["neg16"] = np.full((128, 1), -1.0 / 16, np.float32)
    m["ones_row"] = np.ones((1, 128), np.float32)
    m["ones"] = np.ones((128, 128), np.float32)
    HC = c["H_C"]
    lg = np.log1p(-np.exp2(-5.0 - np.arange(HC, dtype=np.float64)))
    pos = np.arange(128, dtype=np.float64)
    eg, eng, ekd, egt = [], [], [], []
    for d in range(2):
        l = lg if d == 0 else lg[::-1]
        cnt = (pos + 1) if d == 0 else (128 - pos)
        g = cnt[:, None] * l[None, :]
        gtot = 128 * l[None, :]
        rep = lambda a: np.repeat(a, 128, axis=1)
        eg.append(rep(np.exp(g)))
        eng.append(rep(np.exp(-g)))
        ekd.append(rep(np.exp(gtot - g)))
        egt.append(np.broadcast_to(np.exp(gtot), (128, HC)))
    m["ret_eg"] = np.stack(eg).astype(np.float32)
    m["ret_eng"] = np.stack(eng).astype(np.float32)
    m["ret_ekd"] = np.stack(ekd).astype(np.float32)
    m["ret_egt"] = np.ascontiguousarray(np.stack(egt)).astype(np.float32)
    Ts, GW = c["Ts"], c["GRID_W"]
    tt_ = np.arange(Ts)
    rows = (tt_ // GW).astype(np.float32)
    cols = (tt_ % GW).astype(np.float32)
    inv_freq = (np.float32(10000.0) ** (-np.arange(32, dtype=np.float32) / 32)).astype(np.float32)
    ang = np.concatenate([rows[:, None] * inv_freq, cols[:, None] * inv_freq], axis=-1).astype(np.float32)
    m["rope_cos"] = np.cos(ang).astype(np.float32)
    m["rope_sin"] = np.sin(ang).astype(np.float32)
    return m


def mixer_shared(c, inp):
    f = lambda a: np.ascontiguousarray(np.asarray(a, np.float32))
    m = mixer_consts(c)
    m["gla_w2"] = f(inp["gla_gate_w2"][0])
    m["gla_b"] = f(inp["gla_gate_b"][0])[:, None, :]
    m["gla_ng"] = f(inp["gla_norm_g"][0])
    m["ret_ng"] = f(inp["ret_norm_g"][0])
    m["ret_nb"] = f(inp["ret_norm_b"][0])
    cwl = lambda w: np.ascontiguousarray(np.transpose(f(w).reshape(3, -1, 128), (2, 1, 0)))
    m["ssd_cw"] = cwl(inp["ssd_conv_w"][0])
    m["ssd_cb"] = np.ascontiguousarray(f(inp["ssd_conv_b"][0]).reshape(-1, 128).T)
    m["ssd_alog"] = f(inp["ssd_a_log"][0])
    m["ssd_dtb"] = f(inp["ssd_dt_bias"][0])
    m["ssd_d"] = f(inp["ssd_d"][0])
    m["ssd_ng"] = f(inp["ssd_norm_g"][0])
    m["gdn_cw"] = cwl(inp["gdn_conv_w"][0])
    m["gdn_alog"] = f(inp["gdn_a_log"][0])
    m["gdn_dtb"] = f(inp["gdn_dt_bias"][0])
    m["gdn_ng"] = f(inp["gdn_norm_g"][0])
    return m


def mixer_core(c, inp, core, m):
    f = lambda a: np.ascontiguousarray(np.asarray(a, np.float32))
    for nm in ("gla", "gdn", "ret", "ssd"):
        m["st_" + nm] = f(np.stack([inp[f"state_{nm}_fwd"][core, 0], inp[f"state_{nm}_bwd"][core, 0]]))
    return m


def fm_load_conv(k, ph, nb, r0, ci, nch, cw, cb, tag):
    X, Xb = k.fmX
    acc, accb = k.fmA
    t1, t1b = k.fmT
    s, sb_ = k.fmS
    k.dma(X[:, 0:nb, :], k.convT[0:nb * 128, r0 - 1:r0 + 129].rearrange("(b p) t -> p b t", p=128),
          reads=[k.convTb], writes=[Xb])
    if ci == 0:
        k.P.add("dve", lambda e: e.memset(X[:, 0:nb, 0:1], 0.0), writes=[Xb])
    if ci == nch - 1:
        k.P.add("dve", lambda e: e.memset(X[:, 0:nb, 129:130], 0.0), writes=[Xb])
    w = lambda j: cw[0][:, 0:nb, j:j + 1].broadcast_to([128, nb, 128])
    k.tt(acc[:, 0:nb, :].bitcast(F32R), X[:, 0:nb, 0:128], w(0), ALU.mult, [Xb, cw[1]], [accb])
    k.tt(t1[:, 0:nb, :].bitcast(F32R), X[:, 0:nb, 1:129], w(1), ALU.mult, [Xb, cw[1]], [t1b])
    k.tt(acc[:, 0:nb, :].bitcast(F32R), acc[:, 0:nb, :], t1[:, 0:nb, :], ALU.add, [accb, t1b], [accb])
    k.tt(t1[:, 0:nb, :].bitcast(F32R), X[:, 0:nb, 2:130], w(2), ALU.mult, [Xb, cw[1]], [t1b])
    k.tt(acc[:, 0:nb, :].bitcast(F32R), acc[:, 0:nb, :], t1[:, 0:nb, :], ALU.add, [accb, t1b], [accb])
    if cb is not None:
        k.tt(acc[:, 0:nb, :].bitcast(F32R), acc[:, 0:nb, :], cb[0][:, 0:nb].unsqueeze(2).broadcast_to([128, nb, 128]), ALU.add,
             [accb, cb[1]], [accb])
    k.act(s[:, 0:nb, :], acc[:, 0:nb, :], AF.Silu, [accb], [sb_])
    return s, sb_


def fm_alloc(k, ph, nb):
    k.fmX = k.sb(ph, "fmX", [128, nb, 130])
    k.fmA = k.sb(ph, "fmA", [128, nb, 128])
    k.fmT = k.sb(ph, "fmT", [128, nb, 128])
    k.fmS = k.sb(ph, "fmS", [128, nb, 128])


def softplus_(k, out, in_, reads, writes, tmp, tmpb):
    k.act(tmp, in_, AF.Exp, reads, [tmpb])
    k.act(out, tmp, AF.Ln, [tmpb], writes, bias=1.0)


def mixer_gdn(k, si, d):
    c, P = k.c, k.P
    H = c["H_B"]
    NB = 3 * H
    NG4 = H // 4
    off = c["AB_OFF"]
    (t0s, r0s, T, smp, idx) = c["SEQS"][si]
    nch = T // 128
    ycol0 = c["WA_V"]
    HV = H * 128
    with ExitStack() as ph:
        S, _ = k.sb(ph, "S", [128, H, 128])
        Sg = [P.buf(f"S{i}") for i in range(NG4)]
        MK = load_const(k, ph, "MK", [128, 128], k.ins["MASK"][d])
        M1 = load_const(k, ph, "M1", [128, 128], k.ins["MSK1"][d])
        NG = load_const(k, ph, "NG", [128, 128], k.ins["NEGM"][d])
        NS = load_const(k, ph, "NS", [128, 128], k.ins["NEGS"][d])
        ON = load_const(k, ph, "ON", [128, 128], k.ins["ones"][:, :])
        BM = load_const(k, ph, "BM", [128, 7, 128], k.ins["BM"][d].rearrange("n i j -> i n j"))
        cw = load_const(k, ph, "cw", [128, NB, 3], k.ins["gdn_cw"][:, :, :])
        dtb = load_const(k, ph, "dtb", [128, H], k.ins["gdn_dtb"][d].partition_broadcast(128))
        nA = load_const(k, ph, "nA", [128, H], k.ins["gdn_alog"][d].partition_broadcast(128))
        ngt = load_const(k, ph, "ngt", [128, 128], k.ins["gdn_ng"].partition_broadcast(128))
        k.act(nA[0][:], nA[0][:], AF.Exp, [nA[1]], [nA[1]])
        k.ts(nA[0][:], nA[0][:], -1.0, ALU.mult, [nA[1]], [nA[1]])
        fm_alloc(k, ph, NB)
        A, Ab = k.fmA
        Tt, Ttb = k.fmT
        X, Xb = k.fmX
        Xf = X[:].rearrange("p a b -> p (a b)")
        Tf = Tt[:].rearrange("p a b -> p (a b)")
        slot = lambda flat, i: flat[:, i * HV:(i + 1) * HV].rearrange("p (h v) -> p h v", h=H)
        KH, KB, BV = None, None, None
        PT, VN, NW = slot(Tf, 0), slot(Tf, 1), slot(Tf, 2)
        KHb = [P.buf("KH") for _ in range(NG4)]
        KBb = [P.buf("KB") for _ in range(NG4)]
        BVb = [P.buf("BV") for _ in range(NG4)]
        PTb = [P.buf("PT") for _ in range(NG4)]
        VNb = [P.buf("VN") for _ in range(NG4)]
        NWb = [P.buf("NW") for _ in range(NG4)]
        slotb = KHb + KBb + BVb + PTb + VNb + NWb
        Ug = [P.buf("U") for _ in range(NG4)]
        Tg = [P.buf("T") for _ in range(NG4)]
        Xg = [P.buf("Xs") for _ in range(NG4)]
        Lg = [P.buf("L") for _ in range(NG4)]
        fz, fzb = k.sb(ph, "fz", [128, 2])
        eps, epsb = k.sb(ph, "eps", [128, 1])
        k.P.add("dve", lambda e: e.memset(eps[:], 1e-6), writes=[epsb])
        sm = {n: k.sb(ph, n, [128, H]) for n in ("br", "ar", "tm", "nl", "lnb", "beta", "sp", "la", "bg")}
        egs, egsb = k.sb(ph, "egs", [128, 3 * H])
        big = {n: k.sb(ph, n, [128, H, 128]) for n in ("LA", "LB", "DG", "L", "T", "U", "Bn", "Xs")}
        e4 = [k.sb(ph, f"e4{i}", [128, 4, 128]) for i in range(4)]
        oa, oab = k.sb(ph, "oa", [128, HV])
        of, ofb = k.sb(ph, "of", [128, HV])
        st, stb = k.sb(ph, "st", [128, H])
        R = lambda ap: ap.bitcast(F32R)
        rc = {}
        for nm_, cst in (("MK", MK), ("M1", M1), ("NG", NG), ("NS", NS), ("ON", ON)):
            t2, b2 = k.sb(ph, nm_ + "r", [128, 128])
            k.P.add("dve", lambda e, t=cst[0], o=t2: e.tensor_copy(o[:].bitcast(F32R), t[:]), reads=[cst[1]], writes=[b2])
            rc[nm_] = (t2, b2)
        idr, idrb = k.sb(ph, "idr", [128, 128])
        k.P.add("dve", lambda e: e.tensor_copy(idr[:].bitcast(F32R), k.ident[:]), reads=[k.identb], writes=[idrb])
        ofv0 = of[:].rearrange("p (h v) -> p h v", h=H)
        if smp:
            for h in range(H):
                k.dma(ofv0[:, h, :], k.ins["st_gdn"][d, h], writes=[ofb])
        else:
            k.P.add("dve", lambda e: e.memset(of[:], 0.0), writes=[ofb])
        k.P.add("dve", lambda e: e.tensor_copy(S[:].bitcast(F32R), ofv0), reads=[ofb], writes=Sg)
        t_ = lambda n: (big[n][0] if n in big else sm[n][0])
        b_ = lambda n: (big[n][1] if n in big else sm[n][1])
        bcH = lambda ap2: ap2.unsqueeze(2).broadcast_to([128, H, 128])
        bcM = lambda ap2: ap2.unsqueeze(1).broadcast_to([128, H, 128])
        g4s = lambda g: slice(4 * g, 4 * g + 4)
        flat4 = lambda ap3: ap3.rearrange("p a b -> p (a b)")
        order = list(range(nch)) if d == 0 else list(range(nch - 1, -1, -1))
        for ci in order:
            r0 = r0s + ci * 128
            t0 = t0s + ci * 128
            s, sb_ = fm_load_conv(k, ph, NB, r0, ci, nch, cw, None, "gdn")
            k.act(R(A[:, 0:2 * H, :]), s[:, 0:2 * H, :], AF.Square, [sb_], [Ab])
            for g4 in range(2 * H // 4):
                pn, pnb = k.ps()
                k.mm(pn[:, :], R(rc["ON"][0][:, :]), R(flat4(A[:, 4 * g4:4 * g4 + 4, :])), True, True, [rc["ON"][1], Ab], [pnb])
                k.act(R(flat4(Tt[:, 4 * g4:4 * g4 + 4, :])), pn[:, :], AF.Sqrt, [pnb, epsb], [Ttb], bias=eps[:, 0:1])
            k.recip(R(Tt[:, 0:2 * H, :]), Tt[:, 0:2 * H, :], [Ttb], [Ttb])
            k.stt(R(A[:, 0:H, :]), s[:, 0:H, :], 128 ** -0.5, Tt[:, 0:H, :], ALU.mult, ALU.mult, [sb_, Ttb], [Ab])
            k.tt(R(A[:, H:2 * H, :]), s[:, H:2 * H, :], Tt[:, H:2 * H, :], ALU.mult, [sb_, Ttb], [Ab])
            k.dma(t_("br")[:], k.proj[r0:r0 + 128, off[8 + d]:off[8 + d] + H], reads=[k.projb], writes=[b_("br")])
            k.dma(t_("ar")[:], k.proj[r0:r0 + 128, off[10 + d]:off[10 + d] + H], reads=[k.projb], writes=[b_("ar")])
            k.act(t_("tm")[:], t_("br")[:], AF.Exp, [b_("br")], [b_("tm")], scale=-1.0)
            k.act(t_("nl")[:], t_("tm")[:], AF.Ln, [b_("tm")], [b_("nl")], bias=1.0)
            k.ts(t_("lnb")[:], t_("nl")[:], -1.0, ALU.mult, [b_("nl")], [b_("lnb")])
            k.act(t_("beta")[:], t_("nl")[:], AF.Exp, [b_("nl")], [b_("beta")], scale=-1.0)
            k.tt(t_("ar")[:], t_("ar")[:], dtb[0][:], ALU.add, [b_("ar"), dtb[1]], [b_("ar")])
            softplus_(k, t_("sp")[:], t_("ar")[:], [b_("ar")], [b_("sp")], t_("tm")[:], b_("tm"))
            k.tt(R(t_("la")[:]), t_("sp")[:], nA[0][:], ALU.mult, [b_("sp"), nA[1]], [b_("la")])
            la = t_("la")
            pc, pcb = k.ps()
            k.mm(pc[:, 0:H], R(rc["MK"][0][:, :]), R(la[:, :]), True, True, [rc["MK"][1], b_("la")], [pcb])
            k.mm(pc[:, H:2 * H], R(rc["M1"][0][:, :]), R(la[:, :]), True, True, [rc["M1"][1], b_("la")], [pcb])
            k.mm(pc[:, 2 * H:3 * H], R(rc["ON"][0][:, :]), R(la[:, :]), True, True, [rc["ON"][1], b_("la")], [pcb])
            k.act(egs[:], pc[:, 0:3 * H], AF.Exp, [pcb], [egsb])
            eg, ekd, egt = egs[:, 0:H], egs[:, H:2 * H], egs[:, 2 * H:3 * H]
            k.tt(t_("bg")[:], t_("beta")[:], eg, ALU.mult, [b_("beta"), egsb], [b_("bg")])
            k.tt(R(t_("LA")[:]), bcH(la[:, :]), bcM(M1[0][:, :]), ALU.mult, [b_("la"), M1[1]], [b_("LA")])
            k.tt(R(t_("LB")[:]), bcH(la[:, :]), bcM(MK[0][:, :]), ALU.mult, [b_("la"), MK[1]], [b_("LB")])
            k.tt(R(t_("DG")[:]), bcH(t_("lnb")[:, :]), bcM(k.ident[:, :]), ALU.mult, [b_("lnb"), k.identb], [b_("DG")])
            k.P.add("dve", lambda e: e.memset(fz[:], 0.0), reads=[Ttb, Xb, b_("Xs"), b_("LA"), b_("LB"), b_("DG")] + slotb + Xg, writes=[Ttb, Xb, fzb, b_("Xs"), b_("LA"), b_("LB"), b_("DG")] + slotb + Xg)
            for g in range(NG4):
                hs = g4s(g)
                pg, pgb = k.ps()
                pa, pab = k.ps()
                pd, pdb = k.ps()
                pl, plb = k.ps()
                for r in range(4):
                    h = 4 * g + r
                    cs = slice(r * 128, (r + 1) * 128)
                    qnT, knT = A[:, h, :], A[:, H + h, :]
                    k.mm(pg[:, cs], R(knT), R(knT), True, True, [Ab], [pgb])
                    k.mm(pa[:, cs], R(knT), R(qnT), True, True, [Ab], [pab])
                    k.mm(pd[:, cs], R(t_("LA")[:, h, :]), R(rc["MK"][0][:, :]), True, False, [b_("LA"), rc["MK"][1]], [pdb])
                    k.mm(pd[:, cs], R(idr[:, :]), R(rc["NG"][0][:, :]), False, True, [idrb, rc["NG"][1]], [pdb])
                    k.mm(pl[:, cs], R(t_("LB")[:, h, :]), R(rc["M1"][0][:, :]), True, False, [b_("LB"), rc["M1"][1]], [plb])
                    k.mm(pl[:, cs], R(t_("DG")[:, h, :]), R(rc["ON"][0][:, :]), False, False, [b_("DG"), rc["ON"][1]], [plb])
                    k.mm(pl[:, cs], R(idr[:, :]), R(rc["NS"][0][:, :]), False, True, [idrb, rc["NS"][1]], [plb])
                E0, E0b = e4[2 * (g % 2)]
                E1, E1b = e4[2 * (g % 2) + 1]
                k.act(flat4(E0[:]), pd[:, :], AF.Exp, [pdb], [E0b])
                k.act(flat4(E1[:]), pl[:, :], AF.Exp, [plb], [E1b])
                k.tt(R(PT[:, hs, :]), pa[:, :].rearrange("p (a b) -> p a b", b=128), E0[:], ALU.mult, [pab, E0b], [PTb[g]])
                k.tt(t_("L")[:, hs, :], pg[:, :].rearrange("p (a b) -> p a b", b=128), E1[:], ALU.mult, [pgb, E1b], [Lg[g]])
            k.P.add("dve", lambda e: e.memset(fz[:], 0.0), reads=[Ttb, Xb, b_("Xs"), b_("LA"), b_("LB"), b_("DG")] + slotb + Xg, writes=[Ttb, Xb, fzb, b_("Xs"), b_("LA"), b_("LB"), b_("DG")] + slotb + Xg)
            KH, KB, BV = t_("LA"), t_("LB"), t_("DG")
            for g in range(NG4):
                hs = g4s(g)
                pt, ptb = k.ps()
                for r in range(4):
                    k.tr(pt[:, r * 128:(r + 1) * 128], A[:, H + 4 * g + r, :], [Ab], [ptb])
                ptv = pt[:, :].rearrange("p (a b) -> p a b", b=128)
                k.tt(R(KH[:, hs, :]), ptv, ekd[:, hs].unsqueeze(2).broadcast_to([128, 4, 128]), ALU.mult, [ptb, egsb], [KHb[g]])
                k.tt(R(KB[:, hs, :]), ptv, t_("bg")[:, hs].unsqueeze(2).broadcast_to([128, 4, 128]), ALU.mult,
                     [ptb, b_("bg")], [KBb[g]])
                pt2, pt2b = k.ps()
                for r in range(4):
                    k.tr(pt2[:, r * 128:(r + 1) * 128], s[:, 2 * H + 4 * g + r, :], [sb_], [pt2b])
                k.tt(R(BV[:, hs, :]), pt2[:, :].rearrange("p (a b) -> p a b", b=128),
                     t_("beta")[:, hs].unsqueeze(2).broadcast_to([128, 4, 128]), ALU.mult, [pt2b, b_("beta")], [BVb[g]])
            k.tt(R(t_("Bn")[:]), t_("L")[:], bcM(BM[0][:, 0, :]), ALU.mult, Lg + [BM[1]], [b_("Bn")])
            k.tt(R(t_("T")[:]), bcM(k.ident[:, :]), t_("Bn")[:], ALU.subtract, [k.identb, b_("Bn")], Tg)
            for g in range(NG4):
                pu, pub = k.ps()
                for r in range(4):
                    k.tr(pu[:, r * 128:(r + 1) * 128], t_("T")[:, 4 * g + r, :], [Tg[g]], [pub])
                k.cp(R(flat4(t_("U")[:, g4s(g), :])), pu[:, :], [pub], [Ug[g]], eng="act")
            for n in range(2, 8):
                k.tt(R(t_("Bn")[:]), t_("L")[:], bcM(BM[0][:, n - 1, :]), ALU.mult, Lg + [BM[1]], [b_("Bn")])
                for g in range(NG4):
                    px, pxb = k.ps()
                    for r in range(4):
                        h = 4 * g + r
                        k.mm(px[:, r * 128:(r + 1) * 128], R(t_("Bn")[:, h, :]), R(t_("U")[:, h, :]), True, True,
                             [b_("Bn"), Ug[g]], [pxb])
                    k.cp(R(flat4(t_("Xs")[:, g4s(g), :])), px[:, :], [pxb], [Xg[g]], eng="act")
                for g in range(NG4):
                    py, pyb = k.ps()
                    for r in range(4):
                        h = 4 * g + r
                        k.mm(py[:, r * 128:(r + 1) * 128], R(t_("T")[:, h, :]), R(t_("Xs")[:, h, :]), True, True,
                             [Tg[g], Xg[g]], [pyb])
                    k.tt(R(t_("U")[:, g4s(g), :]), t_("U")[:, g4s(g), :], py[:, :].rearrange("p (a b) -> p a b", b=128),
                         ALU.subtract, [Ug[g], pyb], [Ug[g]])
                if n < 7:
                    for g in range(NG4):
                        pz, pzb = k.ps()
                        for r in range(4):
                            k.tr(pz[:, r * 128:(r + 1) * 128], t_("U")[:, 4 * g + r, :], [Ug[g]], [pzb])
                        k.cp(R(flat4(t_("T")[:, g4s(g), :])), pz[:, :], [pzb], [Tg[g]], eng="act")
            if d == 1:
                k.dma(of[:], k.of_scr[t0:t0 + 128, ycol0:ycol0 + HV], reads=[k.ofsb], writes=[ofb])
            for g in range(NG4):
                hs = g4s(g)
                pw, pwb = k.ps()
                for r in range(4):
                    h = 4 * g + r
                    k.mm(pw[:, r * 128:(r + 1) * 128], R(KB[:, h, :]), R(t_("U")[:, h, :]), True, True, [KBb[g], Ug[g]], [pwb])
                k.act(R(flat4(NW[:, hs, :])), pw[:, :], AF.Copy, [pwb], [NWb[g]], scale=-1.0)
                pv, pvb = k.ps()
                for r in range(4):
                    h = 4 * g + r
                    cs = slice(r * 128, (r + 1) * 128)
                    k.mm(pv[:, cs], R(t_("U")[:, h, :]), R(BV[:, h, :]), True, False, [Ug[g], BVb[g]], [pvb])
                    k.mm(pv[:, cs], R(NW[:, h, :]), R(S[:, h, :]), False, True, [NWb[g], Sg[g]], [pvb])
                k.cp(R(flat4(VN[:, hs, :])), pv[:, :], [pvb], [VNb[g]], eng="act")
                po, pob = k.ps()
                pr, prb = k.ps()
                pS, pSb = k.ps()
                for r in range(4):
                    h = 4 * g + r
                    cs = slice(r * 128, (r + 1) * 128)
                    k.mm(po[:, cs], R(PT[:, h, :]), R(VN[:, h, :]), True, True, [PTb[g], VNb[g]], [pob])
                    k.mm(pr[:, cs], R(A[:, h, :]), R(S[:, h, :]), True, True, [Ab, Sg[g]], [prb])
                    k.mm(pS[:, cs], R(KH[:, h, :]), R(VN[:, h, :]), True, True, [KHb[g], VNb[g]], [pSb])
                E0, E0b = e4[g % 4]
                k.tt(E0[:], pr[:, :].rearrange("p (a b) -> p a b", b=128), eg[:, hs].unsqueeze(2).broadcast_to([128, 4, 128]),
                     ALU.mult, [prb, egsb], [E0b])
                osl = oa[:, 4 * g * 128:(4 * g + 4) * 128]
                k.tt(osl, po[:, :], flat4(E0[:]), ALU.add, [pob, E0b], [oab])
                if d == 1:
                    k.tt(osl, osl, of[:, 4 * g * 128:(4 * g + 4) * 128], ALU.add, [oab, ofb], [oab])
                k.tt(R(S[:, hs, :]), S[:, hs, :], egt[:, hs].unsqueeze(2).broadcast_to([128, 4, 128]), ALU.mult, [Sg[g], egsb], [Sg[g]])
                k.tt(R(S[:, hs, :]), S[:, hs, :], pS[:, :].rearrange("p (a b) -> p a b", b=128), ALU.add, [Sg[g], pSb], [Sg[g]])
            k.P.add("dve", lambda e: e.memset(fz[:], 0.0), reads=[Ttb, Xb, b_("Xs"), b_("LA"), b_("LB"), b_("DG")] + slotb + Xg, writes=[Ttb, Xb, fzb, b_("Xs"), b_("LA"), b_("LB"), b_("DG")] + slotb + Xg)
            if d == 0:
                k.dma(k.of_scr[t0:t0 + 128, ycol0:ycol0 + HV], oa[:], reads=[oab], writes=[k.ofsb])
            else:
                z, zb = Xf[:, 0:HV], Xb
                jk, jkb = t_("Bn")[:].rearrange("p a b -> p (a b)"), b_("Bn")
                k.dma(z, k.proj[r0:r0 + 128, off[7]:off[7] + HV], reads=[k.projb], writes=[zb])
                oav = oa[:].rearrange("p (h v) -> p h v", h=H)
                k.act(R(jk), oa[:], AF.Square, [oab], [jkb])
                k.red(st[:], t_("Bn")[:], [jkb], [stb])
                k.rsqrt_chain(st[:], st[:], 1.0 / 128, 1e-6, [stb], [stb])
                k.tt(oav, oav, bcH(st[:, :]), ALU.mult, [oab, stb], [oab])
                k.tt(oav, oav, bcM(ngt[0][:, :]), ALU.mult, [oab, ngt[1]], [oab])
                k.act(R(jk), z, AF.Silu, [zb], [jkb])
                k.tt(oa[:], oa[:], jk, ALU.mult, [oab, jkb], [oab])
                k.dma(k.y_scr[t0:t0 + 128, ycol0:ycol0 + HV], oa[:], reads=[oab], writes=[k.yscrb])
        if not smp:
            for h in range(H):
                k.dma(k.outs["o_gdn"][idx, d, h], S[:, h, :], reads=[Sg[h // 4]])
    P.barrier()


def mixer_ssd(k, si, d):
    c, P = k.c, k.P
    H, G = c["H_D"], c["G_D"]
    HG = H // G
    WD = c["WD"]
    NBX = WD // 128
    NB = NBX + 2 * G
    off = c["CD_OFF"]
    (t0s, r0s, T, smp, idx) = c["SEQS"][si]
    nch = T // 128
    ycol0 = c["WC_V"]
    with ExitStack() as ph:
        S, _ = k.sb(ph, "S", [128, H, 64])
        Sb = [P.buf(f"S{g}") for g in range(H // 8)]
        MK = load_const(k, ph, "MK", [128, 128], k.ins["MASK"][d])
        M1 = load_const(k, ph, "M1", [128, 128], k.ins["MSK1"][d])
        NG = load_const(k, ph, "NG", [128, 128], k.ins["NEGM"][d])
        ON = load_const(k, ph, "ON", [128, 128], k.ins["ones"][:, :])
        cw = load_const(k, ph, "cw", [128, NB, 3], k.ins["ssd_cw"][:, :, :])
        cb = load_const(k, ph, "cb", [128, NB], k.ins["ssd_cb"][:, :])
        dtb = load_const(k, ph, "dtb", [128, H], k.ins["ssd_dtb"][d].partition_broadcast(128))
        nA = load_const(k, ph, "nA", [128, H], k.ins["ssd_alog"][d].partition_broadcast(128))
        dsk = load_const(k, ph, "dsk", [128, H], k.ins["ssd_d"].partition_broadcast(128))
        ngt = load_const(k, ph, "ngt", [128, WD], k.ins["ssd_ng"].partition_broadcast(128))
        k.act(nA[0][:], nA[0][:], AF.Exp, [nA[1]], [nA[1]])
        k.ts(nA[0][:], nA[0][:], -1.0, ALU.mult, [nA[1]], [nA[1]])
        if smp:
            for h in range(H):
                k.dma(S[:, h, :], k.ins["st_ssd"][d, h], writes=[Sb[h // 8]])
        else:
            k.P.add("dve", lambda e: e.memset(S[:], 0.0), writes=Sb)
        fm_alloc(k, ph, NB)
        xt, xtb = k.sb(ph, "xt", [128, H, 64])
        Bt, Btb = k.sb(ph, "Bt", [128, G, 128])
        dtr, dtrb = k.sb(ph, "dtr", [128, H])
        dt, dtb_ = k.sb(ph, "dt", [128, H])
        tm, tmb = k.sb(ph, "tm", [128, H])
        la, lab = k.sb(ph, "la", [128, H])
        egs, egsb = k.sb(ph, "egs", [128, 3 * H])
        xdt, xdtb = k.sb(ph, "xdt", [128, H, 64])
        xdk, xdkb = k.sb(ph, "xdk", [128, H, 64])
        LA, LAb = k.sb(ph, "LA", [128, H, 128])
        ATs = [k.sb(ph, f"ATs{i}", [128, 128]) for i in range(2)]
        DTs = [k.sb(ph, f"DTs{i}", [128, 4, 128]) for i in range(2)]
        Pall = [k.sb(ph, f"Pall{i}", [128, 4, 128]) for i in range(H // 4)]
        tmp, tmpb = k.sb(ph, "tmp", [128, 8, 64])
        oa, oab = k.sb(ph, "oa", [128, WD])
        of, ofb = k.sb(ph, "of", [128, WD])
        z, zb = k.sb(ph, "z", [128, WD])
        st, stb = k.sb(ph, "st", [128, G])
        order = list(range(nch)) if d == 0 else list(range(nch - 1, -1, -1))
        for ci in order:
            r0 = r0s + ci * 128
            t0 = t0s + ci * 128
            s, sb_ = fm_load_conv(k, ph, NB, r0, ci, nch, cw, cb, "ssd")
            for b0 in range(0, NBX, 4):
                nbk = min(4, NBX - b0)
                pt, ptb = k.ps()
                for r in range(nbk):
                    k.tr(pt[:, r * 128:(r + 1) * 128], s[:, b0 + r, :], [sb_], [ptb])
                k.cp(xt[:].rearrange("p h f -> p (h f)")[:, b0 * 128:(b0 + nbk) * 128], pt[:, 0:nbk * 128], [ptb], [xtb],
                     eng="act")
            pt, ptb = k.ps()
            for g in range(G):
                k.tr(pt[:, g * 128:(g + 1) * 128], s[:, NBX + g, :], [sb_], [ptb])
            k.cp(Bt[:].rearrange("p g n -> p (g n)"), pt[:, 0:G * 128], [ptb], [Btb], eng="act")
            k.dma(dtr[:], k.proj[r0:r0 + 128, off[6 + d]:off[6 + d] + H], reads=[k.projb], writes=[dtrb])
            k.tt(dtr[:], dtr[:], dtb[0][:], ALU.add, [dtrb, dtb[1]], [dtrb])
            softplus_(k, dt[:], dtr[:], [dtrb], [dtb_], tm[:], tmb)
            k.tt(la[:], dt[:], nA[0][:], ALU.mult, [dtb_, nA[1]], [lab])
            pc, pcb = k.ps()
            k.mm(pc[:, 0:H], MK[0][:, :], la[:, :], True, True, [MK[1], lab], [pcb])
            k.mm(pc[:, H:2 * H], M1[0][:, :], la[:, :], True, True, [M1[1], lab], [pcb])
            k.mm(pc[:, 2 * H:3 * H], ON[0][:, :], la[:, :], True, True, [ON[1], lab], [pcb])
            k.act(egs[:], pc[:, 0:3 * H], AF.Exp, [pcb], [egsb])
            eg, ekd, egt = egs[:, 0:H], egs[:, H:2 * H], egs[:, 2 * H:3 * H]
            k.tt(xdt[:], xt[:], dt[:, :].unsqueeze(2).broadcast_to([128, H, 64]), ALU.mult, [xtb, dtb_], [xdtb])
            k.tt(xdk[:], xdt[:], ekd.unsqueeze(2).broadcast_to([128, H, 64]), ALU.mult, [xdtb, egsb], [xdkb])
            k.tt(LA[:], la[:, :].unsqueeze(2).broadcast_to([128, H, 128]),
                 M1[0][:, :].unsqueeze(1).broadcast_to([128, H, 128]), ALU.mult, [lab, M1[1]], [LAb])
            for g in range(G):
                A_, Ab_ = ATs[g % 2]
                pa, pab = k.ps()
                k.mm(pa[:, 0:128], s[:, NBX + g, :], s[:, NBX + G + g, :], True, True, [sb_], [pab])
                k.cp(A_[:], pa[:, 0:128], [pab], [Ab_], eng="act")
                for q4 in range(HG // 4):
                    qi = (g * HG) // 4 + q4
                    D_, Db_ = DTs[qi % 2]
                    P_, Pb_ = Pall[qi]
                    pd, pdb = k.ps()
                    for r in range(4):
                        h = g * HG + q4 * 4 + r
                        k.mm(pd[:, r * 128:(r + 1) * 128], LA[:, h, :], MK[0][:, :], True, False, [LAb, MK[1]], [pdb])
                        k.mm(pd[:, r * 128:(r + 1) * 128], k.ident[:, :], NG[0][:, :], False, True, [k.identb, NG[1]], [pdb])
                    k.act(D_[:].rearrange("p a b -> p (a b)"), pd[:, :], AF.Exp, [pdb], [Db_])
                    k.tt(P_[:], D_[:], A_[:, :].unsqueeze(1).broadcast_to([128, 4, 128]), ALU.mult, [Db_, Ab_], [Pb_])
            for hb0 in range(0, H, 8):
                g = hb0 // HG
                po, pob = k.ps()
                pr, prb = k.ps()
                pS, pSb = k.ps()
                for hh in range(8):
                    h = hb0 + hh
                    P_, Pb_ = Pall[h // 4]
                    r = h % 4
                    k.mm(po[:, hh * 64:(hh + 1) * 64], P_[:, r, :], xdt[:, h, :], True, True, [Pb_, xdtb], [pob])
                    k.mm(pr[:, hh * 64:(hh + 1) * 64], s[:, NBX + G + g, :], S[:, h, :], True, True,
                         [sb_, Sb[h // 8]], [prb])
                    k.mm(pS[:, hh * 64:(hh + 1) * 64], Bt[:, g, :], xdk[:, h, :], True, True, [Btb, xdkb], [pSb])
                hs = slice(hb0, hb0 + 8)
                k.tt(tmp[:], pr[:, :].rearrange("p (h f) -> p h f", f=64),
                     eg[:, hs].unsqueeze(2).broadcast_to([128, 8, 64]), ALU.mult, [prb, egsb], [tmpb])
                oav = oa[:, hb0 * 64:(hb0 + 8) * 64]
                k.tt(oav, po[:, :], tmp[:].rearrange("p h f -> p (h f)"), ALU.add, [pob, tmpb], [oab])
                k.tt(S[:, hs, :], S[:, hs, :], egt[:, hs].unsqueeze(2).broadcast_to([128, 8, 64]), ALU.mult,
                     [Sb[hb0 // 8], egsb], [Sb[hb0 // 8]])
                k.tt(S[:, hs, :], S[:, hs, :], pS[:, :].rearrange("p (h f) -> p h f", f=64), ALU.add,
                     [Sb[hb0 // 8], pSb], [Sb[hb0 // 8]])
            if d == 0:
                k.dma(k.of_scr[t0:t0 + 128, ycol0:ycol0 + WD], oa[:], reads=[oab], writes=[k.ofsb])
            else:
                k.dma(of[:], k.of_scr[t0:t0 + 128, ycol0:ycol0 + WD], reads=[k.ofsb], writes=[ofb])
                k.dma(z[:], k.proj[r0:r0 + 128, off[5]:off[5] + WD], reads=[k.projb], writes=[zb])
                k.tt(oa[:], oa[:], of[:], ALU.add, [oab, ofb], [oab])
                k.tt(of[:].rearrange("p (h f) -> p h f", f=64), xt[:], dsk[0][:, :].unsqueeze(2).broadcast_to([128, H, 64]),
                     ALU.mult, [xtb, dsk[1]], [ofb])
                k.tt(oa[:], oa[:], of[:], ALU.add, [oab, ofb], [oab])
                k.act(of[:], z[:], AF.Silu, [zb], [ofb])
                k.tt(oa[:], oa[:], of[:], ALU.mult, [oab, ofb], [oab])
                k.act(of[:], oa[:], AF.Square, [oab], [ofb])
                k.red(st[:], of[:].rearrange("p (g f) -> p g f", g=G), [ofb], [stb])
                k.rsqrt_chain(st[:], st[:], 1.0 / (WD // G), 1e-6, [stb], [stb])
                k.tt(oa[:].rearrange("p (g f) -> p g f", g=G), oa[:].rearrange("p (g f) -> p g f", g=G),
                     st[:, :].unsqueeze(2).broadcast_to([128, G, WD // G]), ALU.mult, [oab, stb], [oab])
                k.tt(oa[:], oa[:], ngt[0][:], ALU.mult, [oab, ngt[1]], [oab])
                k.dma(k.y_scr[t0:t0 + 128, ycol0:ycol0 + WD], oa[:], reads=[oab], writes=[k.yscrb])
        if not smp:
            for h in range(H):
                k.dma(k.outs["o_ssd"][idx, d, h], S[:, h, :], reads=[Sb[h // 8]])
    P.barrier()


_CACHE = {}


def kernel(**inp):
    cfg = FULL_CFG
    c = derive(cfg)
    NP = c["NP"]
    enabled = ENABLED
    if "k" not in _CACHE:
        _CACHE["k"] = build(cfg, enabled=enabled)
    k = _CACHE["k"]
    sh = shared_inputs(c, inp)
    sh.update(mixer_shared(c, inp))
    in_maps = []
    for core in range(8):
        m = core_inputs(c, inp, core, sh)
        m = mixer_core(c, inp, core, m)
        in_maps.append({n: v for n, v in m.items() if n in k.ins})
    res = run_bass_kernel_spmd(k.nc, in_maps, core_ids=list(range(8)))
    R = res.results
    Ts, Tp, D = c["Ts"], c["Tp"], c["D"]
    y_sample = np.stack([np.asarray(R[i]["y"][:Ts]) for i in range(8)]).astype(np.float32)
    y_prompt = np.stack([np.asarray(R[i]["y"][Ts + j * Tp:Ts + (j + 1) * Tp]) for i in range(8) for j in range(NP)]).astype(np.float32)

    def st(name, d):
        a = np.concatenate([np.asarray(R[i][name][:, d]) for i in range(8)], axis=0)
        return np.ascontiguousarray(a[:, None]).astype(np.float32)

    return (y_prompt, y_sample, st("o_gla", 0), st("o_gla", 1), st("o_gdn", 0), st("o_gdn", 1),
            st("o_ret", 0), st("o_ret", 1), st("o_ssd", 0), st("o_ssd", 1))


ENABLED = ("gdn", "ssd")


def simulate(P):
    sems = {}
    pc = {e: 0 for e in P.ENGS}
    progress = True
    while progress:
        progress = False
        for e in P.ENGS:
            while pc[e] < len(P.sim[e]):
                waits, inc, idx = P.sim[e][pc[e]]
                if all(sems.get(k_, 0) >= v for (k_, v) in waits):
                    if inc is not None:
                        sems[inc[0]] = sems.get(inc[0], 0) + inc[1]
                    pc[e] += 1
                    progress = True
                else:
                    break
    stuck = {e: (pc[e], len(P.sim[e])) for e in P.ENGS if pc[e] < len(P.sim[e])}
    return stuck
```
